# Optimizing a Trainium2 kernel written in Bass

```python
import math
import jax, jax.numpy as jnp
from jax import lax
import numpy as np

D_MODEL = 1024
BATCH = 2
SEQ = 8192
DEPTH = 2
DEC_BATCH = 4
DEC_SEQ = 4096
PAST_LEN = 128

HEAD_DIM = 64
A_HEADS = 8
A_KV_HEADS = 2
WINDOW = 128
B_HEADS = 8
B_Q_RANK = 512
B_KV_RANK = 256
B_NOPE = 64
B_ROPE = 32
B_V_DIM = 64
B_QK_DIM = B_NOPE + B_ROPE
C_HEADS = 4
C_V_DIM = 2 * HEAD_DIM
D_HEADS = 8
D_KV_HEADS = 2
D_FF = 2816
GRID_W = 64
ROPE_THETA = 10000.0
NORM_EPS = 1e-6
Q_BLOCK = 128
NEG_INF = -1e30
N_EVEN_LAYERS = (DEPTH + 1) // 2
N_ODD_LAYERS = DEPTH // 2
EV_IN_SIZES = (A_HEADS * HEAD_DIM, A_KV_HEADS * HEAD_DIM, A_KV_HEADS * HEAD_DIM, B_Q_RANK, B_KV_RANK, B_ROPE)
OD_IN_SIZES = (2 * C_HEADS * HEAD_DIM, 2 * C_HEADS * HEAD_DIM, C_HEADS * C_V_DIM, D_HEADS * HEAD_DIM, D_KV_HEADS * HEAD_DIM, D_KV_HEADS * HEAD_DIM)
EV_IN = sum(EV_IN_SIZES)
OD_IN = sum(OD_IN_SIZES)
EV_MIX = A_HEADS * HEAD_DIM + B_HEADS * B_V_DIM
OD_MIX = C_HEADS * C_V_DIM + D_HEADS * HEAD_DIM

kernel_name = 'hybrid_bidir_encoder_swa_mla_diff_axial'


def rms_norm(x, g):
    xf = x.astype(jnp.float32)
    y = xf * lax.rsqrt(jnp.mean(xf * xf, axis=-1, keepdims=True) + NORM_EPS)
    return (y * g.astype(jnp.float32)).astype(x.dtype)


def rope_angles(pos, dim):
    inv = ROPE_THETA ** (-(jnp.arange(0, dim, 2, dtype=jnp.float32) / dim))
    ang = pos.astype(jnp.float32)[:, None] * inv[None, :]
    return jnp.cos(ang), jnp.sin(ang)


def apply_rope(x, cos, sin):
    x1, x2 = jnp.split(x, 2, axis=-1)
    c = cos[:, None, :].astype(x.dtype)
    s = sin[:, None, :].astype(x.dtype)
    return jnp.concatenate([x1 * c - x2 * s, x1 * s + x2 * c], axis=-1)


def apply_axial_rope(x, rope_row, rope_col):
    xr, xc = jnp.split(x, 2, axis=-1)
    return jnp.concatenate([apply_rope(xr, *rope_row), apply_rope(xc, *rope_col)], axis=-1)


def _split_cols(z, sizes):
    offs = np.cumsum(sizes)[:-1].tolist()
    return jnp.split(z, offs, axis=-1)


def _query_blocks(q):
    b, s = q.shape[0], q.shape[1]
    return jnp.moveaxis(q.reshape((b, s // Q_BLOCK, Q_BLOCK) + q.shape[2:]), 1, 0)


def _merge_blocks(o):
    o = jnp.moveaxis(o, 0, 1)
    return o.reshape((o.shape[0], o.shape[1] * o.shape[2]) + o.shape[3:])


def swiglu_ffn(x, g, w_in, w_out):
    gate, up = jnp.split(rms_norm(x, g) @ w_in, 2, axis=-1)
    return (jax.nn.silu(gate) * up) @ w_out


def blocked_attention(q, k, v, scale):
    def one_block(qb):
        sc = jnp.einsum('bqkgd,bjkd->bkgqj', qb, k).astype(jnp.float32) * scale
        pr = jax.nn.softmax(sc, axis=-1).astype(v.dtype)
        return jnp.einsum('bkgqj,bjke->bqkge', pr, v)
    o = _merge_blocks(lax.map(one_block, _query_blocks(q)))
    return o.reshape(o.shape[0], o.shape[1], -1)


def sliding_window_attention_with_sink(q, k, v, sink):
    b, s, hq, d = q.shape
    hk = k.shape[2]
    g = hq // hk
    nb = s // WINDOW
    qb = q.reshape(b, nb, WINDOW, hk, g, d)
    pad = ((0, 0), (WINDOW, WINDOW), (0, 0), (0, 0))

    def band(t):
        tb = jnp.pad(t, pad).reshape(b, nb + 2, WINDOW, hk, d)
        return jnp.concatenate([tb[:, :-2], tb[:, 1:-1], tb[:, 2:]], axis=2)

    kb, vb = band(k), band(v)
    qi = jnp.arange(WINDOW)[:, None]
    kj = jnp.arange(3 * WINDOW)[None, :]
    rel = kj - WINDOW - qi
    kpos = jnp.arange(nb)[:, None] * WINDOW - WINDOW + jnp.arange(3 * WINDOW)[None, :]
    valid = (jnp.abs(rel) <= WINDOW)[None] & ((kpos >= 0) & (kpos < s))[:, None, :]
    sc = jnp.einsum('bnqkgd,bnjkd->bnkgqj', qb, kb).astype(jnp.float32) * (d ** -0.5)
    sc = jnp.where(valid[None, :, None, None], sc, NEG_INF)
    sink_l = jnp.broadcast_to(sink.astype(jnp.float32).reshape(1, 1, hk, g, 1, 1), sc.shape[:-1] + (1,))
    pr = jax.nn.softmax(jnp.concatenate([sc, sink_l], axis=-1), axis=-1)[..., :-1]
    o = jnp.einsum('bnkgqj,bnjkd->bnqkgd', pr.astype(v.dtype), vb)
    return o.reshape(b, s, hq * d)


def differential_attention(q, k, v, lam):
    scale = HEAD_DIM ** -0.5

    def one_block(qb):
        sc = jnp.einsum('bqhcd,bjhcd->bhcqj', qb, k).astype(jnp.float32) * scale
        pm = jax.nn.softmax(sc, axis=-1)
        diff = pm[:, :, 0] - lam * pm[:, :, 1]
        return jnp.einsum('bhqj,bjhe->bqhe', diff.astype(v.dtype), v)
    return _merge_blocks(lax.map(one_block, _query_blocks(q)))


def even_mixer(h, p, i, rope_full, rope_mla):
    b, s, _ = h.shape
    a_q, a_k, a_v, b_cq, b_ckv, b_kr = _split_cols(h @ p['ev_w_in'][i], EV_IN_SIZES)
    qa = apply_rope(rms_norm(a_q.reshape(b, s, A_HEADS, HEAD_DIM), p['a_q_norm'][i]), *rope_full)
    ka = apply_rope(rms_norm(a_k.reshape(b, s, A_KV_HEADS, HEAD_DIM), p['a_k_norm'][i]), *rope_full)
    va = a_v.reshape(b, s, A_KV_HEADS, HEAD_DIM)
    o_a = sliding_window_attention_with_sink(qa, ka, va, p['a_sink'][i])
    qb = (rms_norm(b_cq, p['b_cq_norm'][i]) @ p['b_w_uq'][i]).reshape(b, s, B_HEADS, B_QK_DIM)
    kv = (rms_norm(b_ckv, p['b_ckv_norm'][i]) @ p['b_w_ukv'][i]).reshape(b, s, B_HEADS, B_NOPE + B_V_DIM)
    k_nope, vb = jnp.split(kv, [B_NOPE], axis=-1)
    k_r = jnp.broadcast_to(b_kr[:, :, None, :], (b, s, B_HEADS, B_ROPE))
    kb = jnp.concatenate([k_nope, k_r], axis=-1)
    qb = rms_norm(qb, p['b_q_norm'][i])
    kb = rms_norm(kb, p['b_k_norm'][i])
    qb = jnp.concatenate([qb[..., :B_NOPE], apply_rope(qb[..., B_NOPE:], *rope_mla)], axis=-1)
    kb = jnp.concatenate([kb[..., :B_NOPE], apply_rope(kb[..., B_NOPE:], *rope_mla)], axis=-1)
    o_b = blocked_attention(qb[:, :, :, None, :], kb, vb, B_QK_DIM ** -0.5)
    return jnp.concatenate([o_a, o_b], axis=-1) @ p['ev_w_out'][i]


def odd_mixer(h, p, i, layer, rope_full, rope_row, rope_col):
    b, s, _ = h.shape
    c_q, c_k, c_v, d_q, d_k, d_v = _split_cols(h @ p['od_w_in'][i], OD_IN_SIZES)
    lam_init = 0.8 - 0.6 * math.exp(-0.3 * layer)
    qc = apply_rope(rms_norm(c_q.reshape(b, s, 2 * C_HEADS, HEAD_DIM), p['c_q_norm'][i]), *rope_full)
    kc = apply_rope(rms_norm(c_k.reshape(b, s, 2 * C_HEADS, HEAD_DIM), p['c_k_norm'][i]), *rope_full)
    qc = qc.reshape(b, s, C_HEADS, 2, HEAD_DIM)
    kc = kc.reshape(b, s, C_HEADS, 2, HEAD_DIM)
    vc = c_v.reshape(b, s, C_HEADS, C_V_DIM)
    lp = p['c_lambda'][i].astype(jnp.float32)
    lam = jnp.exp(jnp.sum(lp[0] * lp[1])) - jnp.exp(jnp.sum(lp[2] * lp[3])) + lam_init
    o_c = differential_attention(qc, kc, vc, lam)
    o_c = (rms_norm(o_c, p['c_out_norm'][i]) * (1.0 - lam_init)).reshape(b, s, C_HEADS * C_V_DIM)
    qd = apply_axial_rope(rms_norm(d_q.reshape(b, s, D_HEADS, HEAD_DIM), p['d_q_norm'][i]), rope_row, rope_col)
    kd = apply_axial_rope(rms_norm(d_k.reshape(b, s, D_KV_HEADS, HEAD_DIM), p['d_k_norm'][i]), rope_row, rope_col)
    vd = d_v.reshape(b, s, D_KV_HEADS, HEAD_DIM)
    o_d = blocked_attention(qd.reshape(b, s, D_KV_HEADS, D_HEADS // D_KV_HEADS, HEAD_DIM), kd, vd, HEAD_DIM ** -0.5)
    return jnp.concatenate([o_c, o_d], axis=-1) @ p['od_w_out'][i]


def encoder_trunk(x, p):
    s = x.shape[1]
    rows = s // GRID_W
    pos = jnp.arange(s)
    row = jnp.repeat(jnp.arange(rows), GRID_W)
    col = jnp.tile(jnp.arange(GRID_W), rows)
    rope_full = rope_angles(pos, HEAD_DIM)
    rope_mla = rope_angles(pos, B_ROPE)
    rope_row = rope_angles(row, HEAD_DIM // 2)
    rope_col = rope_angles(col, HEAD_DIM // 2)
    for l in range(DEPTH):
        x = x + 0.5 * swiglu_ffn(x, p['ffn1_norm'][l], p['ffn1_w_in'][l], p['ffn1_w_out'][l])
        if l % 2 == 0:
            i = l // 2
            x = x + even_mixer(rms_norm(x, p['ev_norm'][i]), p, i, rope_full, rope_mla)
        else:
            i = l // 2
            x = x + odd_mixer(rms_norm(x, p['od_norm'][i]), p, i, l, rope_full, rope_row, rope_col)
        x = x + 0.5 * swiglu_ffn(x, p['ffn2_norm'][l], p['ffn2_w_in'][l], p['ffn2_w_out'][l])
    return x


def setup_inputs(seed: int = 0) -> dict:
    key = jax.random.key(seed)
    ks = iter(jax.random.split(key, 32))

    def nrm(shape, scale):
        return jax.random.normal(next(ks), shape, jnp.float32) * scale

    def gain(shape):
        return 1.0 + 0.05 * jax.random.normal(next(ks), shape, jnp.float32)

    L, E, O = DEPTH, N_EVEN_LAYERS, N_ODD_LAYERS
    return {
        'x_prompt': nrm((BATCH, SEQ, D_MODEL), 1.0),
        'x_sample': nrm((DEC_BATCH, DEC_SEQ, D_MODEL), 1.0),
        'ffn1_norm': gain((L, D_MODEL)),
        'ffn1_w_in': nrm((L, D_MODEL, 2 * D_FF), D_MODEL ** -0.5),
        'ffn1_w_out': nrm((L, D_FF, D_MODEL), D_FF ** -0.5),
        'ffn2_norm': gain((L, D_MODEL)),
        'ffn2_w_in': nrm((L, D_MODEL, 2 * D_FF), D_MODEL ** -0.5),
        'ffn2_w_out': nrm((L, D_FF, D_MODEL), D_FF ** -0.5),
        'ev_norm': gain((E, D_MODEL)),
        'ev_w_in': nrm((E, D_MODEL, EV_IN), D_MODEL ** -0.5),
        'a_q_norm': gain((E, HEAD_DIM)),
        'a_k_norm': gain((E, HEAD_DIM)),
        'a_sink': nrm((E, A_HEADS), 0.5),
        'b_cq_norm': gain((E, B_Q_RANK)),
        'b_w_uq': nrm((E, B_Q_RANK, B_HEADS * B_QK_DIM), B_Q_RANK ** -0.5),
        'b_ckv_norm': gain((E, B_KV_RANK)),
        'b_w_ukv': nrm((E, B_KV_RANK, B_HEADS * (B_NOPE + B_V_DIM)), B_KV_RANK ** -0.5),
        'b_q_norm': gain((E, B_QK_DIM)),
        'b_k_norm': gain((E, B_QK_DIM)),
        'ev_w_out': nrm((E, EV_MIX, D_MODEL), EV_MIX ** -0.5),
        'od_norm': gain((O, D_MODEL)),
        'od_w_in': nrm((O, D_MODEL, OD_IN), D_MODEL ** -0.5),
        'c_q_norm': gain((O, HEAD_DIM)),
        'c_k_norm': gain((O, HEAD_DIM)),
        'c_lambda': nrm((O, 4, HEAD_DIM), 0.1),
        'c_out_norm': gain((O, C_V_DIM)),
        'd_q_norm': gain((O, HEAD_DIM)),
        'd_k_norm': gain((O, HEAD_DIM)),
        'od_w_out': nrm((O, OD_MIX, D_MODEL), OD_MIX ** -0.5),
    }


def reference(x_prompt, x_sample, ffn1_norm, ffn1_w_in, ffn1_w_out, ffn2_norm, ffn2_w_in, ffn2_w_out,
              ev_norm, ev_w_in, a_q_norm, a_k_norm, a_sink, b_cq_norm, b_w_uq, b_ckv_norm, b_w_ukv,
              b_q_norm, b_k_norm, ev_w_out, od_norm, od_w_in, c_q_norm, c_k_norm, c_lambda, c_out_norm,
              d_q_norm, d_k_norm, od_w_out):
    p = {
        'ffn1_norm': ffn1_norm, 'ffn1_w_in': ffn1_w_in, 'ffn1_w_out': ffn1_w_out,
        'ffn2_norm': ffn2_norm, 'ffn2_w_in': ffn2_w_in, 'ffn2_w_out': ffn2_w_out,
        'ev_norm': ev_norm, 'ev_w_in': ev_w_in, 'a_q_norm': a_q_norm, 'a_k_norm': a_k_norm,
        'a_sink': a_sink, 'b_cq_norm': b_cq_norm, 'b_w_uq': b_w_uq, 'b_ckv_norm': b_ckv_norm,
        'b_w_ukv': b_w_ukv, 'b_q_norm': b_q_norm, 'b_k_norm': b_k_norm, 'ev_w_out': ev_w_out,
        'od_norm': od_norm, 'od_w_in': od_w_in, 'c_q_norm': c_q_norm, 'c_k_norm': c_k_norm,
        'c_lambda': c_lambda, 'c_out_norm': c_out_norm, 'd_q_norm': d_q_norm, 'd_k_norm': d_k_norm,
        'od_w_out': od_w_out,
    }
    y_prompt = encoder_trunk(x_prompt, p)
    y_sample = encoder_trunk(x_sample, p)
    return (y_prompt, y_sample)
```

```python
import contextlib
import math
import numpy as np
import concourse.bass as bass
import concourse.mybir as mybir
from concourse.bass_utils import run_bass_kernel_spmd

F32 = mybir.dt.float32
BF16 = mybir.dt.bfloat16
AF = mybir.ActivationFunctionType
ALU = mybir.AluOpType

D_MODEL = 1024
KD = 8
DFF = 2816
NJ = 22
T = 512
EPS = 1e-6
THETA = 10000.0
GRID_W = 64
WINDOW = 128
LAM_INIT_L1 = 0.8 - 0.6 * math.exp(-0.3 * 1)

WSLOT = 3072
NWSLOT = 3
NPG16 = 41
NPG32 = 8


class Sem:
    __slots__ = ("h", "v", "dma", "name")

    def __init__(self, h, dma, name):
        self.h = h
        self.v = 0
        self.dma = dma
        self.name = name


class Buf:
    __slots__ = ("w", "r", "name", "excl")

    def __init__(self, name="", excl=False):
        self.excl = excl
        self.w = {}
        self.r = {}
        self.name = name


class Eng:
    def __init__(self, name, e, sem, is_pe=False):
        self.name = name
        self.e = e
        self.sem = sem
        self.is_pe = is_pe
        self.seen = {}


class FW:
    def __init__(self, nc, es):
        self.nc = nc
        self.es = es
        self.dry = False
        self.nsem = 0
        self.pe = Eng("pe", nc.tensor, self.new_sem("e_pe"), is_pe=True)
        self.act = Eng("act", nc.scalar, self.new_sem("e_act"))
        self.dve = Eng("dve", nc.vector, self.new_sem("e_dve"))
        self.pool = Eng("pool", nc.gpsimd, self.new_sem("e_pool"))
        self.sp = Eng("sp", nc.sync, self.new_sem("e_sp"))
        self.engs = [self.pe, self.act, self.dve, self.pool, self.sp]
        self.dma_sems = []
        self.ninst = 0

    def new_sem(self, name, dma=None):
        h = self.es.enter_context(self.nc.semaphore(name))
        self.nsem += 1
        s = Sem(h, dma, name)
        if dma:
            self.dma_sems.append(s)
        return s

    def _wait(self, eng, tok, raw):
        s, v = tok
        if s is eng.sem and eng.is_pe:
            return
        if s.dma:
            v = s.v
        if eng.seen.get(s, 0) >= v:
            return
        eng.e.wait_ge(s.h, v)
        eng.seen[s] = v

    def _deps(self, eng, reads, writes):
        for b in reads:
            for s, v in b.w.items():
                self._wait(eng, (s, v), True)
            if b.excl:
                for s, v in b.r.items():
                    if s is not eng.sem:
                        self._wait(eng, (s, v), False)
        for b in writes:
            for s, v in b.w.items():
                self._wait(eng, (s, v), False)
            for s, v in b.r.items():
                self._wait(eng, (s, v), False)

    def _commit(self, tok, reads, writes, partial=False):
        s, v = tok
        for b in reads:
            if b.r.get(s, 0) < v:
                b.r[s] = v
        for b in writes:
            if partial:
                if b.w.get(s, 0) < v:
                    b.w[s] = v
            else:
                b.w = {s: v}
            b.r = {}

    def op(self, eng, fn, reads=(), writes=()):
        if self.dry:
            return
        self._deps(eng, reads, writes)
        ins = fn(eng.e)
        eng.sem.v += 1
        ins.then_inc(eng.sem.h, 1)
        self.ninst += 1
        self._commit((eng.sem, eng.sem.v), reads, writes)

    def dma(self, q, out, in_, sem, reads=(), writes=(), partial=False):
        if self.dry:
            return
        assert sem.dma in ("hw", "sw") and (sem.dma == "sw") == (q is self.pool), (sem.name, q.name)
        self._deps(q, reads, writes)
        ins = q.e.dma_start(out=out, in_=in_)
        sem.v += 16
        ins.then_inc(sem.h, 16)
        self.ninst += 1
        self._commit((sem, sem.v), reads, writes, partial)

    def async1(self, q, fn, sem, reads=(), writes=()):
        if self.dry:
            return
        self._deps(q, reads, writes)
        ins = fn(q.e)
        sem.v += 1
        ins.then_inc(sem.h, 1)
        self._commit((sem, sem.v), reads, writes)

    def finish(self, bufs):
        if self.dry:
            return
        for b in bufs:
            for s, v in b.w.items():
                self._wait(self.sp, (s, v), True)
        for e in self.engs:
            if e is not self.sp and e.sem.v > 0:
                self._wait(self.sp, (e.sem, e.sem.v), True)
        for s in self.dma_sems:
            if s.v > 0:
                self._wait(self.sp, (s, s.v), True)


class PagePool:
    def __init__(self, tensor, n, width, name, fw):
        self.sems = [fw.new_sem(f"pg{name}{i}", dma="hw") for i in range(n)]
        self.t = tensor
        self.n = n
        self.width = width
        self.free_ = [True] * n
        self.bufs = [Buf(f"{name}{i}") for i in range(n)]
        self.name = name
        self.rot = 0
        self.lo = 0

    def alloc(self, k=1):
        n = self.n
        if k == 1:
            for off in range(1, n + 1):
                s = (self.rot - off) % n
                if self.free_[s] and s >= self.lo:
                    self.free_[s] = False
                    self.rot = s
                    return s
            for s in range(n - 1, -1, -1):
                if self.free_[s]:
                    self.free_[s] = False
                    self.rot = s
                    return s
        else:
            for s in range(0, n - k + 1):
                if all(self.free_[s:s + k]):
                    for i in range(s, s + k):
                        self.free_[i] = False
                    return s
        raise RuntimeError(f"page pool {self.name} exhausted (want {k}, free {sum(self.free_)})")

    def alloc_n(self, n):
        return [self.alloc() for _ in range(n)]

    def free_n(self, lst):
        for p in lst:
            self.free(p)

    def free(self, s, k=1):
        for i in range(s, s + k):
            assert not self.free_[i]
            self.free_[i] = True

    def reset(self):
        assert all(self.free_), f"pool {self.name} not empty at reset"

    def ap(self, s, k=1):
        return self.t[:, s * self.width:(s + k) * self.width]

    def b(self, s, k=1):
        return self.bufs[s:s + k]


class BankPool:
    def __init__(self, tensors):
        self.t = tensors
        self.bufs = [Buf(f"ps{i}", excl=True) for i in range(len(tensors))]
        self.freeq = list(range(len(tensors)))

    def alloc(self):
        if not self.freeq:
            raise RuntimeError("PSUM banks exhausted")
        return self.freeq.pop(0)

    def free(self, i):
        assert i not in self.freeq
        self.freeq.append(i)


def _kmajor(W):
    kin, n = W.shape
    return np.ascontiguousarray(W.reshape(kin // 128, 128, n).transpose(1, 0, 2)).reshape(128, -1)


def _tile2(v64):
    return np.concatenate([v64, v64], axis=0)


def _rope_tables(pos):
    pos = np.asarray(pos)
    ntok = pos.shape[0]

    def angles(p, dim):
        inv = (np.float32(THETA) ** (-(np.arange(0, dim, 2, dtype=np.float32) / np.float32(dim)))).astype(np.float32)
        ang = p.astype(np.float32)[:, None] * inv[None, :]
        return np.cos(ang).astype(np.float32), np.sin(ang).astype(np.float32)

    c64, s64 = angles(pos, 64)
    c32, s32 = angles(pos, 32)
    cr, sr = angles(pos // GRID_W, 32)
    cc, sc = angles(pos % GRID_W, 32)
    out = np.zeros((6, 128, ntok), np.float32)
    cf = np.concatenate([c64, c64], axis=1).T
    sf = np.concatenate([-s64, s64], axis=1).T
    out[0] = np.concatenate([cf, cf], axis=0)
    out[1] = np.concatenate([sf, sf], axis=0)
    out[2, :64] = 1.0
    out[2, 64:96] = np.concatenate([c32, c32], axis=1).T
    out[3, 64:96] = np.concatenate([-s32, s32], axis=1).T
    ca = np.concatenate([cr, cr, cc, cc], axis=1).T
    sa = np.concatenate([-sr, sr, -sc, sc], axis=1).T
    out[4] = np.concatenate([ca, ca], axis=0)
    out[5] = np.concatenate([sa, sa], axis=0)
    return out


def _const_bf():
    c = np.zeros((128, 7 * 128), np.float32)
    c[:, 0:128] = 1.0
    for h in range(2):
        c[h * 64:(h + 1) * 64, 128 + h * 64:128 + (h + 1) * 64] = 1.0
    c[0:96, 256:256 + 96] = 1.0
    idx = np.arange(128)
    sw = (idx // 64) * 64 + ((idx % 64) + 32) % 64
    c[sw, 384 + idx] = 1.0
    sa = (idx // 32) * 32 + ((idx % 32) + 16) % 32
    c[sa, 512 + idx] = 1.0
    m = np.arange(64, 96)
    sm = 64 + ((m - 64) + 16) % 32
    c[sm, 640 + m] = 1.0
    c[np.arange(32), 768 + 64 + np.arange(32)] = 1.0
    return c


def _masks():
    k = np.arange(128)[:, None]
    q = np.arange(512)[None, :]
    m = np.zeros((128, 6, 512), np.float32)
    for i, d in enumerate(range(-1, 5)):
        m[:, i, :] = (np.abs(d * 128 + k - q) <= WINDOW).astype(np.float32)
    return m.reshape(128, 6 * 512)


SM_GD = 0
SM_GH = 48
SM_GB = 54
SM_GC = 56
SM_GCO = 62
SM_SINK = 63
SM_LAM = 71
SM_HALO = 327
SM_EPS = 339
NSM = 340

CB_ONES, CB_BLK64, CB_ONES96, CB_SWF, CB_SWA, CB_SWM, CB_KRSEL = [i * 128 for i in range(7)]


class Cfg:
    def __init__(self, nseg=2048):
        self.NSEG = nseg
        self.NT = 2 * nseg
        self.NTILE = self.NT // T
        self.TPS = nseg // T
        self.NC = nseg // 128
        self.ranks = (4, 2)
        self.KR = (896, 640)
        self.R = (896 + 650, 640 + 642)


class Kern:
    def __init__(self, nc, es, cfg, dry_plan=None):
        self.nc = nc
        self.cfg = cfg
        self.fw = FW(nc, es)
        fw = self.fw
        NT, NSEG, NC = cfg.NT, cfg.NSEG, cfg.NC

        import os
        tiny = os.environ.get("KTINYW") == "1"

        def din(name, shape):
            if tiny and name.startswith("w_"):
                shape = [1] * (len(shape) - 2) + list(shape[-2:])
            return nc.dram_tensor(name, list(shape), F32, kind="ExternalInput").ap()

        self.xin = din("xin", [NT, 1024])
        self.rope = din("rope", [6, 128, NT])
        self.small_d = din("small", [128, NSM])
        self.cbf_d = din("cbf", [128, 896])
        self.cf32_d = din("cf32", [128, 256])
        self.masks_d = nc.dram_tensor("masks", [128, 3072], BF16, kind="ExternalInput").ap()
        self.w_ffn_in = din("w_ffn_in", [4, NJ, 128, 2048])
        self.w_ffn_out = din("w_ffn_out", [4, 8, 128, 2816])
        self.w_ev_in_a = din("w_ev_in_a", [5, 128, 2048])
        self.w_ev_in_b = din("w_ev_in_b", [1, 128, 2304])
        self.w_uq = din("w_uq", [1, 128, 3072])
        self.w_ukv = din("w_ukv", [1, 128, 2560])
        self.w_ev_out = din("w_ev_out", [4, 128, 2048])
        self.w_od_in = din("w_od_in", [9, 128, 2048])
        self.w_od_out = din("w_od_out", [4, 128, 2048])
        self.y = nc.dram_tensor("y", [NT, 1024], F32, kind="ExternalOutput").ap()

        def dint(name, n):
            return nc.dram_tensor(name, [n], BF16, kind="Internal").ap()

        self.Qs = dint("Qs", 1280 * NT)
        self.Os = dint("Os", 1024 * NT)
        self.Qs_b = Buf("Qs")
        self.Os_b = Buf("Os")
        ulist = {0: [(("KA",), 128)] + [(("KB", h), 96) for h in range(8)] + [(("V", hv), 65) for hv in range(10)],
                 1: [(("KC", h), 128) for h in range(4)] + [(("KD",), 128)] + [(("V", hv), 128) for hv in range(4)] + [(("V", 4), 65), (("V", 5), 65)]}
        self.units = {}
        self.nchunk = {}
        chunk_rows = {}
        for L in range(2):
            j, used = 0, 0
            for name, n in ulist[L]:
                if used + n > 256:
                    chunk_rows[L, j] = used
                    j, used = j + 1, 0
                self.units[L, name] = (j, used, n)
                used += n
            chunk_rows[L, j] = used
            self.nchunk[L] = j + 1
        self.loc = {}
        self.gat = {}
        self.loc_b = {}
        self.gat_b = {}
        for L in range(2):
            for s in range(2):
                for j in range(self.nchunk[L]):
                    rows = chunk_rows[L, j]
                    self.loc[L, s, j] = nc.dram_tensor(f"loc{L}{s}_{j}", [rows, NSEG], BF16, kind="Internal").ap()
                    self.gat[L, s, j] = nc.dram_tensor(f"gat{L}{s}_{j}", [cfg.ranks[s] * rows, NSEG], BF16, kind="Internal").ap()
                    self.loc_b[L, s, j] = Buf(f"loc{L}{s}_{j}")
                    self.gat_b[L, s, j] = Buf(f"gat{L}{s}_{j}")

        def sb(name, shape, dt):
            return es.enter_context(nc.sbuf_tensor("sb_" + name, list(shape), dt))

        self.xT = sb("xT", [128, KD * NT], F32)
        self.xT_b = [[Buf(f"x{k}_{t}") for t in range(cfg.NTILE)] for k in range(KD)]
        self.small = sb("small", [128, NSM], F32)
        self.cbf = sb("cbf", [128, 896], BF16)
        self.cf32 = sb("cf32", [128, 256], F32)
        self.const_b = Buf("consts")
        self.wsl = sb("wsl", [128, NWSLOT * WSLOT], BF16)
        self.wsl_b = [Buf(f"wsl{i}") for i in range(NWSLOT)]
        self.wsl_sem = [fw.new_sem(f"wsl{i}", dma="sw") for i in range(NWSLOT)]
        p16 = sb("pg16", [128, NPG16 * T], BF16)
        p32 = sb("pg32", [128, NPG32 * T], F32)
        self.pg16 = PagePool(p16, NPG16, T, "h", fw)
        self.pg32 = PagePool(p32, NPG32, T, "f", fw)
        banks = [es.enter_context(nc.psum_tensor(f"psb{i}", [128, T], F32)) for i in range(8)]
        self.ps = BankPool(banks)
        self.s_cc = {(L, s_): fw.new_sem(f"cc{L}{s_}", dma="cc") for L in range(2) for s_ in range(2)}
        self.s_attn = {(k_, r_): fw.new_sem(f"at{k_}{r_}", dma="sw") for k_ in ("k", "v") for r_ in range(4)}
        self.s_attn["q", 0] = fw.new_sem("atq0", dma="sw")
        self.s_attn["q", 1] = fw.new_sem("atq1", dma="sw")
        self.qpar = 0
        self.s_const = fw.new_sem("const", dma="hw")
        self.s_const2 = fw.new_sem("const2", dma="hw")
        self.s_constp = fw.new_sem("constp", dma="sw")
        self.plan = dry_plan
        self.wplan = []
        self.wi = 0
        self.wloaded = 0

    def xTt(self, k, t, sub=None):
        NT = self.cfg.NT
        if sub is None:
            return self.xT[:, k * NT + t * T: k * NT + (t + 1) * T]
        return self.xT[:, k * NT + t * T + sub * 128: k * NT + t * T + (sub + 1) * 128]

    def psap(self, i, p=128, n=T):
        return self.ps.t[i][0:p, 0:n]

    def mm(self, out, lhsT, rhs, start, stop, reads, writes):
        self.fw.op(self.fw.pe, lambda e: e.matmul(out, lhsT=lhsT, rhs=rhs, start=start, stop=stop), reads, writes)

    def wget(self, key, nel):
        fw = self.fw
        if fw.dry:
            self.wplan.append((key, nel))
            return self.wsl[:, 0:nel], [self.wsl_b[0]]
        i = self.wi
        plan = self.plan
        assert plan[i][1] == nel
        while self.wloaded < min(len(plan), i + NWSLOT):
            g = self.wloaded
            s = g % NWSLOT
            kk, ne = plan[g]
            ap_d = getattr(self, kk[0])
            for ix in kk[1:]:
                ap_d = ap_d[ix]
            fw.dma(fw.pool, self.wsl[:, s * WSLOT: s * WSLOT + ne], ap_d, self.wsl_sem[s], reads=(), writes=[self.wsl_b[s]])
            self.wloaded += 1
        self.wi += 1
        s = i % NWSLOT
        return self.wsl[:, s * WSLOT: s * WSLOT + nel], [self.wsl_b[s]]

    def sm(self, col, p=128, n=1):
        return self.small[0:p, col:col + n]

    def load_consts(self):
        fw = self.fw
        fw.dma(fw.sp, self.small[:, :], self.small_d, self.s_const, writes=[self.const_b])
        fw.dma(fw.sp, self.cf32[:, :], self.cf32_d, self.s_const2, writes=[self.const_b])
        fw.dma(fw.pool, self.cbf[:, :], self.cbf_d, self.s_constp, writes=[self.const_b])
        cb = [self.const_b]
        sm = self.small
        fw.op(fw.act, lambda e: e.activation(out=sm[:, SM_SINK:SM_SINK + 8], in_=sm[:, SM_SINK:SM_SINK + 8], func=AF.Exp), cb, cb)
        lam = sm[:, SM_LAM:SM_LAM + 256]
        fw.op(fw.dve, lambda e: e.tensor_tensor(out=lam[:, 0:64], in0=lam[:, 0:64], in1=lam[:, 64:128], op=ALU.mult), cb, cb)
        fw.op(fw.dve, lambda e: e.tensor_tensor(out=lam[:, 128:192], in0=lam[:, 128:192], in1=lam[:, 192:256], op=ALU.mult), cb, cb)
        fw.op(fw.dve, lambda e: e.reduce_sum(out=lam[:, 64:65], in_=lam[:, 0:64], axis=mybir.AxisListType.X), cb, cb)
        fw.op(fw.dve, lambda e: e.reduce_sum(out=lam[:, 65:66], in_=lam[:, 128:192], axis=mybir.AxisListType.X), cb, cb)
        fw.op(fw.act, lambda e: e.activation(out=lam[:, 64:66], in_=lam[:, 64:66], func=AF.Exp), cb, cb)
        fw.op(fw.dve, lambda e: e.tensor_tensor(out=lam[:, 66:67], in0=lam[:, 65:66], in1=lam[:, 64:65], op=ALU.subtract), cb, cb)
        fw.op(fw.dve, lambda e: e.tensor_scalar(out=lam[:, 66:67], in0=lam[:, 66:67], scalar1=-LAM_INIT_L1, scalar2=None, op0=ALU.add), cb, cb)
        fw.op(fw.dve, lambda e: e.tensor_scalar(out=sm[:, SM_GCO:SM_GCO + 1], in0=sm[:, SM_GCO:SM_GCO + 1], scalar1=1.0 - LAM_INIT_L1, scalar2=None, op0=ALU.mult), cb, cb)
        self.neglam = lam[:, 66:67]

    def rstd_of(self, chunks, inv_n, ones_ap, P):
        fw = self.fw
        ss = self.ps.alloc()
        n = len(chunks)
        for i, (src, sbufs) in enumerate(chunks):
            sq = self.pg16.alloc()
            sq_ap = self.pg16.ap(sq)[0:P, :]
            fw.op(fw.act, lambda e: e.activation(out=sq_ap, in_=src, func=AF.Square), sbufs, self.pg16.b(sq))
            self.mm(self.psap(ss, P), ones_ap, sq_ap, i == 0, i == n - 1, self.pg16.b(sq) + [self.const_b], [self.ps.bufs[ss]])
            self.pg16.free(sq)
        r = self.pg32.alloc()
        r_ap = self.pg32.ap(r)[0:P, :]
        fw.op(fw.act, lambda e: e.activation(out=r_ap, in_=self.psap(ss, P), func=AF.Ln, bias=self.sm(SM_EPS, P), scale=inv_n),
              [self.ps.bufs[ss], self.const_b], self.pg32.b(r))
        self.ps.free(ss)
        fw.op(fw.act, lambda e: e.activation(out=r_ap, in_=r_ap, func=AF.Exp, scale=-0.5), self.pg32.b(r), self.pg32.b(r))
        return r

    def norm_dmodel(self, t, gi):
        fw = self.fw
        chunks = [(self.xTt(k, t), [self.xT_b[k][t]]) for k in range(KD)]
        r = self.rstd_of(chunks, 1.0 / D_MODEL, self.cbf[:, CB_ONES:CB_ONES + 128], 128)
        h0 = self.pg16.alloc_n(KD)
        for k in range(KD):
            g = self.sm(SM_GD + gi * 8 + k)
            o = self.pg16.ap(h0[k])
            fw.op(fw.dve, lambda e: e.scalar_tensor_tensor(out=o, in0=self.xTt(k, t), scalar=g, in1=self.pg32.ap(r), op0=ALU.mult, op1=ALU.mult),
                  [self.xT_b[k][t], self.const_b] + self.pg32.b(r), self.pg16.b(h0[k]))
        self.pg32.free(r)
        return h0

    def ffn(self, t, f, gi, mid_hook=None, h0=None):
        fw = self.fw
        if h0 is None:
            h0 = self.norm_dmodel(t, gi)
        a0 = self.pg16.alloc_n(NJ)
        for j in range(NJ):
            w, wb = self.wget(("w_ffn_in", f, j), 2048)
            w3 = w.rearrange("p (k n) -> p k n", n=256)
            pg_ = self.ps.alloc()
            pu_ = self.ps.alloc()
            for half, pb in ((0, pg_), (1, pu_)):
                for k in range(KD):
                    self.mm(self.psap(pb), w3[:, k, half * 128:(half + 1) * 128], self.pg16.ap(h0[k]), k == 0, k == KD - 1,
                            wb + self.pg16.b(h0[k]), [self.ps.bufs[pb]])
            s = self.pg32.alloc()
            fw.op(fw.act, lambda e: e.activation(out=self.pg32.ap(s), in_=self.psap(pg_), func=AF.Silu), [self.ps.bufs[pg_]], self.pg32.b(s))
            self.ps.free(pg_)
            fw.op(fw.dve, lambda e: e.tensor_tensor(out=self.pg16.ap(a0[j]), in0=self.psap(pu_), in1=self.pg32.ap(s), op=ALU.mult),
                  [self.ps.bufs[pu_]] + self.pg32.b(s), self.pg16.b(a0[j]))
            self.ps.free(pu_)
            self.pg32.free(s)
        self.pg16.free_n(h0)
        if mid_hook is not None:
            mid_hook()
        for m in range(KD):
            w, wb = self.wget(("w_ffn_out", f, m), 2816)
            w3 = w.rearrange("p (j n) -> p j n", n=128)
            acc = self.ps.alloc()
            for j in range(NJ):
                self.mm(self.psap(acc), w3[:, j, :], self.pg16.ap(a0[j]), j == 0, j == NJ - 1, wb + self.pg16.b(a0[j]), [self.ps.bufs[acc]])
            xk = self.xTt(m, t)
            fw.op(fw.dve, lambda e: e.scalar_tensor_tensor(out=xk, in0=self.psap(acc), scalar=0.5, in1=xk, op0=ALU.mult, op1=ALU.add),
                  [self.ps.bufs[acc], self.xT_b[m][t]], [self.xT_b[m][t]])
            self.ps.free(acc)
        self.pg16.free_n(a0)

    def unit_a(self, zp, P, ones_ap, inv_n, swap_ap, cos_ap, sin_ap, gain_ap, rope_bufs):
        fw = self.fw
        zb = [self.ps.bufs[zp]]
        z = self.psap(zp, P)
        sq = self.pg16.alloc()
        sq_ap = self.pg16.ap(sq)[0:P, :]
        fw.op(fw.act, lambda e: e.activation(out=sq_ap, in_=z, func=AF.Square), zb, self.pg16.b(sq))
        xg = self.pg16.alloc()
        xg_ap = self.pg16.ap(xg)[0:P, :]
        fw.op(fw.act, lambda e: e.activation(out=xg_ap, in_=z, func=AF.Copy, scale=gain_ap), zb + [self.const_b], self.pg16.b(xg))
        ss = self.ps.alloc()
        self.mm(self.psap(ss, P), ones_ap, sq_ap, True, True, self.pg16.b(sq) + [self.const_b], [self.ps.bufs[ss]])
        self.pg16.free(sq)
        rot = self.ps.alloc()
        self.mm(self.psap(rot, P), swap_ap, xg_ap, True, True, self.pg16.b(xg) + [self.const_b], [self.ps.bufs[rot]])
        self.pg16.free(xg)
        r = self.pg32.alloc()
        r_ap = self.pg32.ap(r)[0:P, :]
        fw.op(fw.act, lambda e: e.activation(out=r_ap, in_=self.psap(ss, P), func=AF.Ln, bias=self.sm(SM_EPS, P), scale=inv_n),
              [self.ps.bufs[ss], self.const_b], self.pg32.b(r))
        self.ps.free(ss)
        fw.op(fw.act, lambda e: e.activation(out=r_ap, in_=r_ap, func=AF.Exp, scale=-0.5), self.pg32.b(r), self.pg32.b(r))
        return (zp, P, r, rot, cos_ap, sin_ap, gain_ap, rope_bufs)

    def unit_b(self, ctx):
        fw = self.fw
        zp, P, r, rot, cos_ap, sin_ap, gain_ap, rope_bufs = ctx
        zb = [self.ps.bufs[zp]]
        z = self.psap(zp, P)
        t1 = self.pg32.alloc()
        t1_ap = self.pg32.ap(t1)[0:P, :]
        fw.op(fw.dve, lambda e: e.scalar_tensor_tensor(out=t1_ap, in0=z, scalar=gain_ap, in1=cos_ap, op0=ALU.mult, op1=ALU.mult),
              zb + [self.const_b] + rope_bufs, self.pg32.b(t1))
        self.ps.free(zp)
        t2 = self.pg32.alloc()
        t2_ap = self.pg32.ap(t2)[0:P, :]
        fw.op(fw.dve, lambda e: e.tensor_tensor(out=t2_ap, in0=self.psap(rot, P), in1=sin_ap, op=ALU.mult),
              [self.ps.bufs[rot]] + rope_bufs, self.pg32.b(t2))
        self.ps.free(rot)
        fw.op(fw.dve, lambda e: e.tensor_tensor(out=t1_ap, in0=t1_ap, in1=t2_ap, op=ALU.add), self.pg32.b(t1) + self.pg32.b(t2), self.pg32.b(t1))
        self.pg32.free(t2)
        o = self.pg16.alloc()
        o_ap = self.pg16.ap(o)[0:P, :]
        fw.op(fw.dve, lambda e: e.tensor_tensor(out=o_ap, in0=t1_ap, in1=self.pg32.ap(r)[0:P, :], op=ALU.mult),
              self.pg32.b(t1) + self.pg32.b(r), self.pg16.b(o))
        self.pg32.free(t1)
        self.pg32.free(r)
        return o

    def unit(self, *args):
        return self.unit_b(self.unit_a(*args))

    def proj_fm(self, w3, wb, col0, M, h0, nk):
        pb = self.ps.alloc()
        for k in range(nk):
            self.mm(self.psap(pb, M), w3[:, k, col0:col0 + M], self.pg16.ap(h0[k]), k == 0, k == nk - 1,
                    wb + self.pg16.b(h0[k]), [self.ps.bufs[pb]])
        return pb

    def q_rows(self, r0, P, t):
        NT = self.cfg.NT
        return self.Qs.rearrange("(r n) -> r n", n=NT)[r0:r0 + P, t * T:(t + 1) * T]

    def o_rows(self, r0, P, c0, n):
        NT = self.cfg.NT
        return self.Os.rearrange("(r n) -> r n", n=NT)[r0:r0 + P, c0:c0 + n]

    def u_loc(self, L, s, unit):
        j, off, n = self.units[L, unit]
        return self.loc[L, s, j][off:off + n, :], self.loc_b[L, s, j]

    def u_gat(self, L, s, unit):
        j, off, n = self.units[L, unit]
        R = self.cfg.ranks[s]
        return self.gat[L, s, j].rearrange("(k r) n -> k r n", k=R)[:, off:off + n, :], self.gat_b[L, s, j]

    def k_loc(self, L, s, unit, ti):
        ap, b = self.u_loc(L, s, unit)
        return ap[:, ti * T:(ti + 1) * T], b

    def v_width(self, L, hv):
        return 128 if (L == 1 and hv < 4) else 65

    def v_loc3(self, L, s, hv):
        W = self.v_width(L, hv)
        ap, b = self.u_loc(L, s, ("V", hv))
        return ap.rearrange("r n -> (r n)").rearrange("(p c e) -> p c e", p=128, e=W), b

    def v_gat4(self, L, s, hv):
        W = self.v_width(L, hv)
        ap, b = self.u_gat(L, s, ("V", hv))
        return ap.rearrange("k r n -> k (r n)").rearrange("k (p c e) -> p k c e", p=128, e=W), b, W

    def store_v(self, L, s, ti, v0, vb, tile4, hv0):
        for i in range(tile4.shape[1]):
            dst, b = self.v_loc3(L, s, hv0 + i)
            self.store(self.pg16.sems[v0], dst[:, ti * 4:(ti + 1) * 4, :], tile4[:, i, :, :], vb, b)

    def store(self, sem, dst, src, src_bufs, dst_buf):
        self.fw.dma(self.fw.sp, dst, src, sem, reads=src_bufs, writes=[dst_buf], partial=True)

    def seg_of(self, t):
        return t // self.cfg.TPS, t % self.cfg.TPS

    def load_rope(self, t, variants):
        fw = self.fw
        res = {}
        for i, v in enumerate(variants):
            c = self.pg32.alloc()
            s = self.pg32.alloc()
            fw.dma(fw.sp, self.pg32.ap(c), self.rope[2 * v, :, t * T:(t + 1) * T], self.pg32.sems[c], writes=self.pg32.b(c))
            fw.dma(fw.sp, self.pg32.ap(s), self.rope[2 * v + 1, :, t * T:(t + 1) * T], self.pg32.sems[s], writes=self.pg32.b(s))
            res[v] = (c, s)
        return res

    def free_rope(self, rp):
        for c, s in rp.values():
            self.pg32.free(c)
            self.pg32.free(s)

    def unit_v(self, zp, variant, rp, gain_col, P=128, phase_a_only=False):
        ones = {0: CB_BLK64, 1: CB_ONES96, 2: CB_BLK64}[variant]
        swp = {0: CB_SWF, 1: CB_SWM, 2: CB_SWA}[variant]
        inv_n = {0: 1.0 / 64, 1: 1.0 / 96, 2: 1.0 / 64}[variant]
        c, s = rp[variant]
        fn = self.unit_a if phase_a_only else self.unit
        return fn(zp, P, self.cbf[0:P, ones:ones + P], inv_n, self.cbf[0:P, swp:swp + P],
                  self.pg32.ap(c)[0:P, :], self.pg32.ap(s)[0:P, :], self.sm(gain_col, P), self.pg32.b(c) + self.pg32.b(s))

    def vtile_alloc(self, L):
        fw = self.fw
        v0 = self.pg16.alloc(6)
        flat = self.pg16.ap(v0, 6)
        vb = self.pg16.b(v0, 6)
        if L == 0:
            v65 = flat[:, 0:10 * 4 * 65].rearrange("p (h c e) -> p h c e", h=10, e=65)
            fw.op(fw.dve, lambda e: e.memset(v65[:, :, :, 64:65], 1.0), (), vb)
            return v0, vb, v65, None
        v128 = flat[:, 0:4 * 4 * 128].rearrange("p (h c e) -> p h c e", h=4, e=128)
        v65 = flat[:, 2048:2048 + 2 * 4 * 65].rearrange("p (h c e) -> p h c e", h=2, e=65)
        fw.op(fw.dve, lambda e: e.memset(v65[:, :, :, 64:65], 1.0), (), vb)
        return v0, vb, v65, v128

    def v_tokmajor(self, lhs_pages, nk, rhs_fn, ncols, rb):
        banks = []
        if ncols <= 128:
            pb = self.ps.alloc()
            for sub in range(4):
                for k in range(nk):
                    self.mm(self.ps.t[pb][:, sub * ncols:(sub + 1) * ncols], self.pg16.ap(lhs_pages[k])[:, sub * 128:(sub + 1) * 128], rhs_fn(k),
                            k == 0, k == nk - 1, rb + self.pg16.b(lhs_pages[k]), [self.ps.bufs[pb]])
            return [pb]
        for sub in range(4):
            pb = self.ps.alloc()
            for k in range(nk):
                self.mm(self.ps.t[pb][:, 0:ncols], self.pg16.ap(lhs_pages[k])[:, sub * 128:(sub + 1) * 128], rhs_fn(k),
                        k == 0, k == nk - 1, rb + self.pg16.b(lhs_pages[k]), [self.ps.bufs[pb]])
            banks.append(pb)
        return banks

    def run_units(self, items):
        n = len(items)
        zps, ctxs = {}, {}
        for step in range(n + 2):
            if step < n:
                zps[step] = items[step][0]()
            i = step - 1
            if 0 <= i < n:
                ctxs[i] = self.unit_v(zps.pop(i), phase_a_only=True, **items[i][1])
            i = step - 2
            if 0 <= i < n:
                o = self.unit_b(ctxs.pop(i))
                items[i][2](o)
                self.pg16.free(o)

    def inproj_ev(self, t):
        fw = self.fw
        s, ti = self.seg_of(t)
        L = 0
        h0 = self.norm_dmodel(t, 1)
        rp = self.load_rope(t, (0, 1))
        v0, vb, v65, _ = self.vtile_alloc(0)
        items = []
        wst = {}
        for g in range(2):
            for c in range(2):
                def proj(g=g, c=c):
                    if c == 0:
                        w, wb = self.wget(("w_ev_in_a", g), 2048)
                        wst[g] = (w.rearrange("p (k n) -> p k n", n=256), wb)
                    return self.proj_fm(wst[g][0], wst[g][1], c * 128, 128, h0, KD)

                def st(o, g=g, c=c):
                    self.store(self.pg16.sems[o], self.q_rows((g * 2 + c) * 128, 128, t), self.pg16.ap(o), self.pg16.b(o), self.Qs_b)
                items.append((proj, dict(variant=0, rp=rp, gain_col=SM_GH + 0), st))
        self.run_units(items)
        w, wb = self.wget(("w_ev_in_a", 2), 2048)
        w3 = w.rearrange("p (k n) -> p k n", n=256)
        zp = self.proj_fm(w3, wb, 0, 128, h0, KD)
        o = self.unit_v(zp, 0, rp, SM_GH + 1)
        self.store(self.pg16.sems[o], self.k_loc(L, s, ('KA',), ti)[0], self.pg16.ap(o), self.pg16.b(o), self.k_loc(L, s, ('KA',), ti)[1])
        self.pg16.free(o)
        (pb,) = self.v_tokmajor(h0, KD, lambda k: w3[:, k, 128:256], 128, wb)
        src = self.ps.t[pb][:, 0:512].rearrange("p (c h e) -> p h c e", c=4, h=2)
        fw.op(fw.dve, lambda e: e.tensor_copy(out=v65[:, 0:2, :, 0:64], in_=src), [self.ps.bufs[pb]], vb)
        self.ps.free(pb)
        cq = []
        for g in (3, 4):
            w, wb = self.wget(("w_ev_in_a", g), 2048)
            w3 = w.rearrange("p (k n) -> p k n", n=256)
            for c in range(2):
                cq.append(self.proj_fm(w3, wb, c * 128, 128, h0, KD))
        r = self.rstd_of([(self.psap(b), [self.ps.bufs[b]]) for b in cq], 1.0 / 512, self.cbf[:, CB_ONES:CB_ONES + 128], 128)
        cqn = self.pg16.alloc_n(4)
        for c in range(4):
            fw.op(fw.dve, lambda e: e.scalar_tensor_tensor(out=self.pg16.ap(cqn[c]), in0=self.psap(cq[c]), scalar=self.sm(SM_GC + c),
                                                            in1=self.pg32.ap(r), op0=ALU.mult, op1=ALU.mult),
                  [self.ps.bufs[cq[c]], self.const_b] + self.pg32.b(r), self.pg16.b(cqn[c]))
            self.ps.free(cq[c])
        self.pg32.free(r)
        w, wb = self.wget(("w_ev_in_b", 0), 2304)
        w3 = w.rearrange("p (k n) -> p k n", n=288)
        ck = [self.proj_fm(w3, wb, c * 128, 128, h0, KD) for c in range(2)]
        krp = self.proj_fm(w3, wb, 256, 32, h0, KD)
        self.pg16.free_n(h0)
        kr = self.pg16.alloc()
        fw.op(fw.act, lambda e: e.activation(out=self.pg16.ap(kr)[0:32, :], in_=self.psap(krp, 32), func=AF.Copy), [self.ps.bufs[krp]], self.pg16.b(kr))
        self.ps.free(krp)
        r = self.rstd_of([(self.psap(b), [self.ps.bufs[b]]) for b in ck], 1.0 / 256, self.cbf[:, CB_ONES:CB_ONES + 128], 128)
        ckn = self.pg16.alloc_n(2)
        for c in range(2):
            fw.op(fw.dve, lambda e: e.scalar_tensor_tensor(out=self.pg16.ap(ckn[c]), in0=self.psap(ck[c]), scalar=self.sm(SM_GC + 4 + c),
                                                            in1=self.pg32.ap(r), op0=ALU.mult, op1=ALU.mult),
                  [self.ps.bufs[ck[c]], self.const_b] + self.pg32.b(r), self.pg16.b(ckn[c]))
            self.ps.free(ck[c])
        self.pg32.free(r)
        w, wb = self.wget(("w_uq", 0), 3072)
        w3 = w.rearrange("p (k n) -> p k n", n=768)
        items = []
        for h in range(8):
            def proj(h=h, w3=w3, wb=wb):
                return self.proj_fm(w3, wb, h * 96, 96, cqn, 4)

            def st(o, h=h):
                self.store(self.pg16.sems[o], self.q_rows(512 + h * 96, 96, t), self.pg16.ap(o)[0:96, :], self.pg16.b(o), self.Qs_b)
            items.append((proj, dict(variant=1, rp=rp, gain_col=SM_GB + 0, P=96), st))
        self.run_units(items)
        self.pg16.free_n(cqn)
        w, wb = self.wget(("w_ukv", 0), 2560)
        wk = w[:, 0:1536].rearrange("p (k h e) -> p k h e", k=2, h=8)
        items = []
        for h in range(8):
            def proj(h=h, wk=wk, wb=wb):
                zp = self.ps.alloc()
                self.mm(self.psap(zp, 96), self.cbf[0:32, CB_KRSEL:CB_KRSEL + 96], self.pg16.ap(kr)[0:32, :], True, False,
                        self.pg16.b(kr) + [self.const_b], [self.ps.bufs[zp]])
                for c in range(2):
                    self.mm(self.psap(zp, 96), wk[:, c, h, :], self.pg16.ap(ckn[c]), False, c == 1,
                            wb + self.pg16.b(ckn[c]), [self.ps.bufs[zp]])
                return zp

            def st(o, h=h):
                self.store(self.pg16.sems[o], self.k_loc(L, s, ('KB', h), ti)[0], self.pg16.ap(o)[0:96, :], self.pg16.b(o), self.k_loc(L, s, ('KB', h), ti)[1])
            items.append((proj, dict(variant=1, rp=rp, gain_col=SM_GB + 1, P=96), st))
        self.run_units(items)
        self.pg16.free(kr)
        wv = w[:, 1536:2560].rearrange("p (k n) -> p k n", k=2)
        banks = self.v_tokmajor(ckn, 2, lambda k: wv[:, k, :], 512, wb)
        for sub, pb in enumerate(banks):
            src = self.ps.t[pb][:, 0:512].rearrange("p (h e) -> p h e", h=8)
            fw.op(fw.dve, lambda e: e.tensor_copy(out=v65[:, 2:10, sub, 0:64], in_=src), [self.ps.bufs[pb]], vb)
            self.ps.free(pb)
        self.pg16.free_n(ckn)
        self.free_rope(rp)
        self.store_v(L, s, ti, v0, vb, v65, 0)
        self.pg16.free(v0, 6)

    def inproj_od(self, t):
        fw = self.fw
        s, ti = self.seg_of(t)
        L = 1
        h0 = self.norm_dmodel(t, 4)
        rp = self.load_rope(t, (0, 2))
        v0, vb, v65, v128 = self.vtile_alloc(1)
        items = []
        wst = {}
        for g in range(4):
            for c in range(2):
                def proj(g=g, c=c):
                    if c == 0:
                        w, wb = self.wget(("w_od_in", g), 2048)
                        wst[g] = (w.rearrange("p (k n) -> p k n", n=256), wb)
                    return self.proj_fm(wst[g][0], wst[g][1], c * 128, 128, h0, KD)

                def st(o, g=g, c=c):
                    if g < 2:
                        self.store(self.pg16.sems[o], self.q_rows((g * 2 + c) * 128, 128, t), self.pg16.ap(o), self.pg16.b(o), self.Qs_b)
                    else:
                        self.store(self.pg16.sems[o], self.k_loc(L, s, ('KC', (g - 2) * 2 + c), ti)[0], self.pg16.ap(o), self.pg16.b(o), self.k_loc(L, s, ('KC', (g - 2) * 2 + c), ti)[1])
                items.append((proj, dict(variant=0, rp=rp, gain_col=SM_GH + (2 if g < 2 else 3)), st))
        self.run_units(items)
        for g in (4, 5):
            w, wb = self.wget(("w_od_in", g), 2048)
            w3 = w.rearrange("p (k n) -> p k n", n=256)
            banks = self.v_tokmajor(h0, KD, lambda k: w3[:, k, :], 256, wb)
            for sub, pb in enumerate(banks):
                src = self.ps.t[pb][:, 0:256].rearrange("p (h e) -> p h e", h=2)
                fw.op(fw.dve, lambda e: e.tensor_copy(out=v128[:, 2 * (g - 4):2 * (g - 4) + 2, sub, :], in_=src), [self.ps.bufs[pb]], vb)
                self.ps.free(pb)
        items = []
        wst = {}
        for g in (6, 7):
            for c in range(2):
                def proj(g=g, c=c):
                    if c == 0:
                        w, wb = self.wget(("w_od_in", g), 2048)
                        wst[g] = (w.rearrange("p (k n) -> p k n", n=256), wb)
                    return self.proj_fm(wst[g][0], wst[g][1], c * 128, 128, h0, KD)

                def st(o, g=g, c=c):
                    self.store(self.pg16.sems[o], self.q_rows(512 + ((g - 6) * 2 + c) * 128, 128, t), self.pg16.ap(o), self.pg16.b(o), self.Qs_b)
                items.append((proj, dict(variant=2, rp=rp, gain_col=SM_GH + 4), st))
        self.run_units(items)
        w, wb = self.wget(("w_od_in", 8), 2048)
        w3 = w.rearrange("p (k n) -> p k n", n=256)
        zp = self.proj_fm(w3, wb, 0, 128, h0, KD)
        o = self.unit_v(zp, 2, rp, SM_GH + 5)
        self.store(self.pg16.sems[o], self.k_loc(L, s, ('KD',), ti)[0], self.pg16.ap(o), self.pg16.b(o), self.k_loc(L, s, ('KD',), ti)[1])
        self.pg16.free(o)
        (pb,) = self.v_tokmajor(h0, KD, lambda k: w3[:, k, 128:256], 128, wb)
        src = self.ps.t[pb][:, 0:512].rearrange("p (c h e) -> p h c e", c=4, h=2)
        fw.op(fw.dve, lambda e: e.tensor_copy(out=v65[:, 0:2, :, 0:64], in_=src), [self.ps.bufs[pb]], vb)
        self.ps.free(pb)
        self.pg16.free_n(h0)
        self.free_rope(rp)
        self.store_v(L, s, ti, v0, vb, v128, 0)
        self.store_v(L, s, ti, v0, vb, v65, 4)
        self.pg16.free(v0, 6)

    def outproj(self, t, wname):
        fw = self.fw
        NT = self.cfg.NT
        o0 = self.pg16.alloc(KD)
        src = self.Os.rearrange("(k p n) -> p k n", p=128, n=NT)[:, :, t * T:(t + 1) * T]
        dst = self.pg16.ap(o0, KD).rearrange("p (k n) -> p k n", n=T)
        fw.dma(fw.sp, dst, src, self.pg16.sems[o0], reads=[self.Os_b], writes=self.pg16.b(o0, KD))
        for g in range(4):
            w, wb = self.wget((wname, g), 2048)
            w3 = w.rearrange("p (k n) -> p k n", n=256)
            for c in range(2):
                m = g * 2 + c
                acc = self.proj_fm(w3, wb, c * 128, 128, list(range(o0, o0 + KD)), KD)
                xk = self.xTt(m, t)
                fw.op(fw.dve, lambda e: e.tensor_tensor(out=xk, in0=self.psap(acc), in1=xk, op=ALU.add),
                      [self.ps.bufs[acc], self.xT_b[m][t]], [self.xT_b[m][t]])
                self.ps.free(acc)
        self.pg16.free(o0, KD)

    def load_x(self, t):
        fw = self.fw
        ident = self.cf32[:, 0:128]
        for sub in range(4):
            st = self.pg32.alloc(2)
            r0 = t * T + sub * 128
            fw.dma(fw.sp, self.pg32.ap(st, 2), self.xin[r0:r0 + 128, :], self.pg32.sems[st], writes=self.pg32.b(st, 2))
            for q in range(2):
                pb = self.ps.alloc()
                for j in range(4):
                    k = q * 4 + j
                    fw.op(fw.pe, lambda e: e.transpose(self.ps.t[pb][:, j * 128:(j + 1) * 128], self.pg32.ap(st, 2)[:, k * 128:(k + 1) * 128], ident),
                          self.pg32.b(st, 2) + [self.const_b], [self.ps.bufs[pb]])
                for j in range(4):
                    k = q * 4 + j
                    fw.op(fw.dve if j % 2 == 0 else fw.act,
                          (lambda e: e.tensor_copy(out=self.xTt(k, t, sub), in_=self.ps.t[pb][:, j * 128:(j + 1) * 128])) if j % 2 == 0 else
                          (lambda e: e.activation(out=self.xTt(k, t, sub), in_=self.ps.t[pb][:, j * 128:(j + 1) * 128], func=AF.Copy)),
                          [self.ps.bufs[pb]], [self.xT_b[k][t]])
                self.ps.free(pb)
            self.pg32.free(st, 2)

    def store_y(self, t, ybuf):
        fw = self.fw
        ident = self.cf32[:, 0:128]
        for sub in range(4):
            st = self.pg32.alloc(2)
            for q in range(2):
                pb = self.ps.alloc()
                for j in range(4):
                    k = q * 4 + j
                    fw.op(fw.pe, lambda e: e.transpose(self.ps.t[pb][:, j * 128:(j + 1) * 128], self.xTt(k, t, sub), ident),
                          [self.xT_b[k][t], self.const_b], [self.ps.bufs[pb]])
                dst = self.pg32.ap(st, 2)[:, q * 512:(q + 1) * 512]
                if q == 0:
                    fw.op(fw.dve, lambda e: e.tensor_copy(out=dst, in_=self.ps.t[pb][:, :]), [self.ps.bufs[pb]], self.pg32.b(st, 2))
                else:
                    fw.op(fw.act, lambda e: e.activation(out=dst, in_=self.ps.t[pb][:, :], func=AF.Copy), [self.ps.bufs[pb]], self.pg32.b(st, 2))
                self.ps.free(pb)
            r0 = t * T + sub * 128
            fw.dma(fw.sp, self.y[r0:r0 + 128, :], self.pg32.ap(st, 2), self.pg32.sems[st], reads=self.pg32.b(st, 2), writes=[ybuf], partial=True)
            self.pg32.free(st, 2)

    def aload(self, dst, src, pg0, reads, writes):
        self.fw.dma(self.fw.sp, dst, src, self.pg16.sems[pg0], reads=reads, writes=writes)

    def pload(self, dst, src, sem, reads, writes):
        self.fw.dma(self.fw.pool, dst, src, sem, reads=reads, writes=writes)

    def load_kT(self, L, s, unit, P, rows=None):
        cfg = self.cfg
        R = cfg.ranks[s]
        npr = cfg.NSEG // T
        npg = R * npr
        k0 = self.pg16.alloc(npg)
        src, gb = self.u_gat(L, s, unit)
        if rows is not None:
            src = src[:, rows[0]:rows[1], :]
        for r in range(R):
            self.pload(self.pg16.ap(k0 + r * npr, npr)[0:P, :], src[r], self.s_attn["k", r], [gb], self.pg16.b(k0 + r * npr, npr))
        return k0, npg

    def load_v(self, L, s, hv):
        cfg = self.cfg
        R, NC = cfg.ranks[s], cfg.NC
        src, gb, W = self.v_gat4(L, s, hv)
        npr = -(-(NC * W) // T)
        npg = R * npr
        v0 = self.pg16.alloc(npg)
        v4 = self.pg16.ap(v0, npg).rearrange("p (k x) -> p k x", k=R)[:, :, 0:NC * W].rearrange("p k (c e) -> p k c e", e=W)
        for r in range(R):
            self.pload(v4[:, r, :, :], src[:, r, :, :], self.s_attn["v", r], [gb], self.pg16.b(v0 + r * npr, npr))
        return v0, npg, v4

    def load_qT(self, s, r0, P, pad=False):
        cfg = self.cfg
        npg = cfg.NSEG // T
        q0 = self.pg16.alloc(npg)
        if pad:
            self.fw.op(self.fw.dve, lambda e: e.memset(self.pg16.ap(q0, npg)[64:128, :], 0.0), (), self.pg16.b(q0, npg))
        src = self.Qs.rearrange("(r n) -> r n", n=cfg.NT)[r0:r0 + P, s * cfg.NSEG:(s + 1) * cfg.NSEG]
        self.qpar ^= 1
        self.pload(self.pg16.ap(q0, npg)[0:P, :], src, self.s_attn["q", self.qpar], [self.Qs_b], self.pg16.b(q0, npg))
        return q0, npg

    def flash_run(self, iters, pair=False):
        pend = []
        self.deferred = []
        depth = 1 if pair else 4
        lag = 5 if pair else 9

        def step():
            i0, s0 = pend.pop(0)
            i0["post"](s0)
            for d in self.deferred:
                d[0] -= 1
            due = [d for d in self.deferred if d[0] <= 0]
            self.deferred = [d for d in self.deferred if d[0] > 0]
            for d in due:
                d[1]()

        self.defer_lag = lag
        for it in iters:
            sp_ = (self.ps.alloc(), self.ps.alloc()) if pair else self.ps.alloc()
            it["qk"](sp_)
            pend.append((it, sp_))
            if len(pend) > depth:
                step()
        while pend:
            step()
        while self.deferred:
            self.deferred.pop(0)[1]()

    def exp_to_page(self, sp_, scale):
        fw = self.fw
        pt = self.pg16.alloc()
        fw.op(fw.act, lambda e: e.activation(out=self.pg16.ap(pt), in_=self.psap(sp_), func=AF.Exp, scale=scale), [self.ps.bufs[sp_]], self.pg16.b(pt))
        self.ps.free(sp_)
        return pt

    def epi65(self, acc, s, orow, col0, sink_col=None):
        self.deferred.append([3, lambda: self._epi65_head(acc, s, orow, col0, sink_col)])

    def _epi65_head(self, acc, s, orow, col0, sink_col):
        fw = self.fw
        ob = self.pg32.alloc()
        ob_ap = self.pg32.ap(ob)
        fw.op(fw.dve, lambda e: e.tensor_copy(out=ob_ap[0:65, :], in_=self.psap(acc, 65)), [self.ps.bufs[acc]], self.pg32.b(ob))
        self.ps.free(acc)
        rl = self.pg32.alloc()
        rl_ap = self.pg32.ap(rl)[64:65, :]
        if sink_col is not None:
            fw.op(fw.act, lambda e: e.activation(out=rl_ap, in_=ob_ap[64:65, :], func=AF.Ln, bias=self.small[64:65, sink_col:sink_col + 1], scale=1.0),
                  self.pg32.b(ob) + [self.const_b], self.pg32.b(rl))
            fw.op(fw.act, lambda e: e.activation(out=rl_ap, in_=rl_ap, func=AF.Exp, scale=-1.0), self.pg32.b(rl), self.pg32.b(rl))
        else:
            fw.op(fw.dve, lambda e: e.reciprocal(out=rl_ap, in_=ob_ap[64:65, :]), self.pg32.b(ob), self.pg32.b(rl))

        def tail():
            rb = self.ps.alloc()
            self.mm(self.psap(rb, 64), self.cf32[64:65, 128:192], rl_ap, True, True, self.pg32.b(rl) + [self.const_b], [self.ps.bufs[rb]])
            self.pg32.free(rl)
            o = self.pg16.alloc()
            fw.op(fw.dve, lambda e: e.tensor_tensor(out=self.pg16.ap(o)[0:64, :], in0=ob_ap[0:64, :], in1=self.psap(rb, 64), op=ALU.mult),
                  self.pg32.b(ob) + [self.ps.bufs[rb]], self.pg16.b(o))
            self.ps.free(rb)
            self.pg32.free(ob)
            self.store(self.pg16.sems[o], self.o_rows(orow, 64, col0, T), self.pg16.ap(o)[0:64, :], self.pg16.b(o), self.Os_b)
            self.pg16.free(o)
        self.deferred.append([self.defer_lag, tail])

    def attn65(self, s, k_ap, kb, d, q_ap, qb_, v3, vb, nkc, scale, orow):
        cfg = self.cfg
        iters = []
        state = {}
        for qb in range(cfg.NSEG // T):
            for kc in range(nkc):
                def qk(sp_, qb=qb, kc=kc):
                    self.mm(self.psap(sp_), k_ap[0:d, kc * 128:(kc + 1) * 128], q_ap[0:d, qb * T:(qb + 1) * T], True, True, kb + qb_, [self.ps.bufs[sp_]])

                def post(sp_, qb=qb, kc=kc):
                    pt = self.exp_to_page(sp_, scale)
                    if kc == 0:
                        state["acc"] = self.ps.alloc()
                    acc = state["acc"]
                    self.mm(self.psap(acc, 65), v3[:, kc // cfg.NC, kc % cfg.NC, :], self.pg16.ap(pt), kc == 0, kc == nkc - 1, vb + self.pg16.b(pt), [self.ps.bufs[acc]])
                    self.pg16.free(pt)
                    if kc == nkc - 1:
                        self.epi65(acc, s, orow, s * cfg.NSEG + qb * T)
                iters.append({"qk": qk, "post": post})
        self.flash_run(iters)

    def attn_ev(self, s):
        fw = self.fw
        cfg = self.cfg
        L = 0
        NC, NSEG, R = cfg.NC, cfg.NSEG, cfg.ranks[s]
        self.pg16.reset()
        self.pg16.lo, self.pg32.lo = NPG16 - 5, 0
        mk = self.pg16.alloc(6)
        self.aload(self.pg16.ap(mk, 6), self.masks_d, mk, [], self.pg16.b(mk, 6))
        hoff = SM_HALO + (0 if s == 0 else 8)
        nkp = -(-((NC + 2) * 128) // T)
        nvp = -(-((NC + 2) * 65) // T)
        for kv in range(2):
            ke = self.pg16.alloc(nkp)
            ke_ap = self.pg16.ap(ke, nkp)[0:64, :]
            ke_full = self.pg16.ap(ke, nkp)
            keb = self.pg16.b(ke, nkp)
            fw.op(fw.dve, lambda e: e.memset(ke_full[64:128, :], 0.0), (), keb)
            ve = self.pg16.alloc(nvp)
            ve3 = self.pg16.ap(ve, nvp)[:, 0:(NC + 2) * 65].rearrange("p (c e) -> p c e", e=65)
            veb = self.pg16.b(ve, nvp)
            kl, klb = self.u_loc(L, s, ("KA",))
            self.aload(ke_ap[:, 128:128 + NSEG], kl[kv * 64:(kv + 1) * 64, :], ke, [klb], keb)
            vl, vlb = self.v_loc3(L, s, kv)
            self.aload(ve3[:, 1:NC + 1, :], vl, ve, [vlb], veb)
            kc_ = self.pg16.alloc(2)
            vc_ = self.pg16.alloc(2)
            vg, vgb, _ = self.v_gat4(L, s, kv)
            kg, kgb = self.u_gat(L, s, ("KA",))
            for side in range(2):
                kcand = self.pg16.ap(kc_ + side)[0:64, 0:R * 128].rearrange("p (k n) -> p k n", k=R)
                cols = slice(NSEG - 128, NSEG) if side == 0 else slice(0, 128)
                self.aload(kcand, kg[:, kv * 64:(kv + 1) * 64, cols].rearrange("k d n -> d k n"), kc_ + side, [kgb], self.pg16.b(kc_ + side))
                vcand = self.pg16.ap(vc_ + side)[:, 0:R * 65].rearrange("p (k e) -> p k e", k=R)
                self.aload(vcand, vg[:, :, NC - 1 if side == 0 else 0, :], vc_ + side, [vgb], self.pg16.b(vc_ + side))
                kdst = ke_ap[:, 0:128] if side == 0 else ke_ap[:, (NC + 1) * 128:(NC + 2) * 128]
                vdst = ve3[:, 0, :] if side == 0 else ve3[:, NC + 1, :]
                for r in range(R):
                    wcol = hoff + side * R + r
                    if r == 0:
                        fw.op(fw.dve, lambda e: e.tensor_scalar(out=kdst, in0=kcand[:, r, :], scalar1=self.sm(wcol, 64), scalar2=None, op0=ALU.mult),
                              self.pg16.b(kc_ + side) + [self.const_b], keb)
                        fw.op(fw.dve, lambda e: e.tensor_scalar(out=vdst, in0=vcand[:, r, :], scalar1=self.sm(wcol), scalar2=None, op0=ALU.mult),
                              self.pg16.b(vc_ + side) + [self.const_b], veb)
                    else:
                        fw.op(fw.dve, lambda e: e.scalar_tensor_tensor(out=kdst, in0=kcand[:, r, :], scalar=self.sm(wcol, 64), in1=kdst, op0=ALU.mult, op1=ALU.add),
                              self.pg16.b(kc_ + side) + [self.const_b] + keb, keb)
                        fw.op(fw.dve, lambda e: e.scalar_tensor_tensor(out=vdst, in0=vcand[:, r, :], scalar=self.sm(wcol), in1=vdst, op0=ALU.mult, op1=ALU.add),
                              self.pg16.b(vc_ + side) + [self.const_b] + veb, veb)
            self.pg16.free(kc_, 2)
            self.pg16.free(vc_, 2)
            for g in range(4):
                hq = kv * 4 + g
                q0, nq = self.load_qT(s, hq * 64, 64, pad=True)
                q_ap = self.pg16.ap(q0, nq)
                qbufs = self.pg16.b(q0, nq)
                iters = []
                state = {}
                for qb in range(NSEG // T):
                    for e_ in range(6):
                        ext = 4 * qb + e_

                        def qk(sp_, qb=qb, ext=ext):
                            self.mm(self.psap(sp_), ke_full[:, ext * 128:(ext + 1) * 128], q_ap[:, qb * T:(qb + 1) * T], True, True, keb + qbufs, [self.ps.bufs[sp_]])

                        def post(sp_, qb=qb, ext=ext, e_=e_, hq=hq):
                            pt = self.exp_to_page(sp_, 0.125)
                            fw.op(fw.dve, lambda e: e.tensor_tensor(out=self.pg16.ap(pt), in0=self.pg16.ap(pt), in1=self.pg16.ap(mk + e_), op=ALU.mult),
                                  self.pg16.b(pt) + self.pg16.b(mk + e_), self.pg16.b(pt))
                            if e_ == 0:
                                state["acc"] = self.ps.alloc()
                            acc = state["acc"]
                            self.mm(self.psap(acc, 65), ve3[:, ext, :], self.pg16.ap(pt), e_ == 0, e_ == 5, veb + self.pg16.b(pt), [self.ps.bufs[acc]])
                            self.pg16.free(pt)
                            if e_ == 5:
                                self.epi65(acc, s, hq * 64, s * NSEG + qb * T, sink_col=SM_SINK + hq)
                        iters.append({"qk": qk, "post": post})
                self.flash_run(iters)
                self.pg16.free(q0, nq)
            self.pg16.free(ke, nkp)
            self.pg16.free(ve, nvp)
        self.pg16.free(mk, 6)
        nkc = R * NC
        scale = 96.0 ** -0.5
        for h in range(8):
            k0, nk = self.load_kT(L, s, ('KB', h), 96)
            v0, nv, v3 = self.load_v(L, s, 2 + h)
            q0, nq = self.load_qT(s, 512 + h * 96, 96)
            self.attn65(s, self.pg16.ap(k0, nk), self.pg16.b(k0, nk), 96, self.pg16.ap(q0, nq), self.pg16.b(q0, nq), v3, self.pg16.b(v0, nv), nkc, scale, 512 + h * 64)
            self.pg16.free(k0, nk)
            self.pg16.free(v0, nv)
            self.pg16.free(q0, nq)

    def attn_od(self, s):
        fw = self.fw
        cfg = self.cfg
        L = 1
        NC, NSEG, R = cfg.NC, cfg.NSEG, cfg.ranks[s]
        nkc = R * NC
        self.pg16.reset()
        self.pg16.lo, self.pg32.lo = NPG16 - 5, 0
        ones = self.cbf[:, CB_ONES:CB_ONES + 128]
        for hd in range(4):
            k0, nk = self.load_kT(L, s, ('KC', hd), 128)
            v0, nv, v3 = self.load_v(L, s, hd)
            q0, nq = self.load_qT(s, hd * 128, 128)
            k_ap, q_ap = self.pg16.ap(k0, nk), self.pg16.ap(q0, nq)
            kb, qb_, vb = self.pg16.b(k0, nk), self.pg16.b(q0, nq), self.pg16.b(v0, nv)
            iters = []
            state = {}
            ones_f = self.cf32[:, 128:256]
            for qb in range(NSEG // T):
                for kc in range(nkc):
                    def qk(sp2, qb=qb, kc=kc):
                        for c in range(2):
                            self.mm(self.psap(sp2[c]), k_ap[c * 64:(c + 1) * 64, kc * 128:(kc + 1) * 128], q_ap[c * 64:(c + 1) * 64, qb * T:(qb + 1) * T],
                                    True, True, kb + qb_, [self.ps.bufs[sp2[c]]])

                    def post(sp2, qb=qb, kc=kc, hd=hd):
                        pts = [self.exp_to_page(sp2[c], 0.125) for c in range(2)]
                        if kc == 0:
                            state["o", 0] = self.ps.alloc()
                            state["o", 1] = self.ps.alloc()
                            state["l", 0] = self.ps.alloc()
                            state["lacc"] = self.pg32.alloc()
                        for c in range(2):
                            ao = state["o", c]
                            self.mm(self.psap(ao), v3[:, kc // NC, kc % NC, :], self.pg16.ap(pts[c]), kc == 0, kc == nkc - 1, vb + self.pg16.b(pts[c]), [self.ps.bufs[ao]])
                        al = state["l", 0]
                        self.mm(self.psap(al), ones, self.pg16.ap(pts[0]), kc == 0, kc == nkc - 1, [self.const_b] + self.pg16.b(pts[0]), [self.ps.bufs[al]])
                        la = state["lacc"]
                        if kc == 0:
                            fw.op(fw.dve, lambda e: e.tensor_copy(out=self.pg32.ap(la), in_=self.pg16.ap(pts[1])), self.pg16.b(pts[1]), self.pg32.b(la))
                        else:
                            fw.op(fw.dve, lambda e: e.tensor_tensor(out=self.pg32.ap(la), in0=self.pg32.ap(la), in1=self.pg16.ap(pts[1]), op=ALU.add),
                                  self.pg16.b(pts[1]) + self.pg32.b(la), self.pg32.b(la))
                        for c in range(2):
                            self.pg16.free(pts[c])
                        if kc == nkc - 1:
                            l1 = self.ps.alloc()
                            self.mm(self.psap(l1), ones_f, self.pg32.ap(la), True, True, [self.const_b] + self.pg32.b(la), [self.ps.bufs[l1]])
                            self.pg32.free(la)
                            state["l", 1] = l1
                            self.epi_diff(state, s, hd, s * NSEG + qb * T)
                    iters.append({"qk": qk, "post": post})
            self.flash_run(iters, pair=True)
            self.pg16.free(k0, nk)
            self.pg16.free(v0, nv)
            self.pg16.free(q0, nq)
        for kv in range(2):
            nk = nkc * 128 // T
            k0 = self.pg16.alloc(nk)
            self.pg16.free(k0, nk)
            fw.op(fw.dve, lambda e: e.memset(self.pg16.ap(k0, nk)[64:128, :], 0.0), (), self.pg16.b(k0, nk))
            k0b, nkb = self.load_kT(L, s, ("KD",), 64, rows=(kv * 64, (kv + 1) * 64))
            assert k0b == k0 and nkb == nk
            v0, nv, v3 = self.load_v(L, s, 4 + kv)
            qn = self.load_qT(s, 512 + (kv * 4) * 64, 64, pad=True)
            for g in range(4):
                hq = kv * 4 + g
                q0, nq = qn
                if g + 1 < 4:
                    qn = self.load_qT(s, 512 + (hq + 1) * 64, 64, pad=True)
                self.attn65(s, self.pg16.ap(k0, nk), self.pg16.b(k0, nk), 128, self.pg16.ap(q0, nq), self.pg16.b(q0, nq), v3, self.pg16.b(v0, nv), nkc, 0.125, 512 + hq * 64)
                self.pg16.free(q0, nq)
            self.pg16.free(k0, nk)
            self.pg16.free(v0, nv)

    def run_debug(self, stage):
        cfg = self.cfg
        ybuf = Buf("y")
        for t in range(cfg.NTILE):
            self.load_x(t)
            if stage >= 1:
                self.ffn(t, 0, 0)
            if stage >= 2:
                self.inproj_ev(t)
                if (t + 1) % cfg.TPS == 0 and stage >= 3:
                    self.allgather(0, t // cfg.TPS)
        if stage >= 4:
            for s in range(2):
                self.attn_ev(s)
        if stage >= 5:
            for t in range(cfg.NTILE):
                self.outproj(t, "w_ev_out")
        for t in range(cfg.NTILE):
            self.store_y(t, ybuf)
        self.fw.finish([ybuf])

    def epi_diff(self, state, s, hd, col0):
        fw = self.fw
        on = []
        for c in range(2):
            ao, al = state["o", c], state["l", c]
            rl = self.pg32.alloc()
            fw.op(fw.dve, lambda e: e.reciprocal(out=self.pg32.ap(rl), in_=self.psap(al)), [self.ps.bufs[al]], self.pg32.b(rl))
            self.ps.free(al)
            o_ = self.pg32.alloc()
            fw.op(fw.dve, lambda e: e.tensor_tensor(out=self.pg32.ap(o_), in0=self.psap(ao), in1=self.pg32.ap(rl), op=ALU.mult),
                  [self.ps.bufs[ao]] + self.pg32.b(rl), self.pg32.b(o_))
            self.ps.free(ao)
            self.pg32.free(rl)
            on.append(o_)
        d_ap = self.pg32.ap(on[0])
        fw.op(fw.dve, lambda e: e.scalar_tensor_tensor(out=d_ap, in0=self.pg32.ap(on[1]), scalar=self.neglam, in1=d_ap, op0=ALU.mult, op1=ALU.add),
              self.pg32.b(on[0]) + self.pg32.b(on[1]) + [self.const_b], self.pg32.b(on[0]))
        self.pg32.free(on[1])

        def tail():
            r = self.rstd_of([(d_ap, self.pg32.b(on[0]))], 1.0 / 128, self.cbf[:, CB_ONES:CB_ONES + 128], 128)
            o = self.pg16.alloc()
            fw.op(fw.dve, lambda e: e.scalar_tensor_tensor(out=self.pg16.ap(o), in0=d_ap, scalar=self.sm(SM_GCO), in1=self.pg32.ap(r), op0=ALU.mult, op1=ALU.mult),
                  self.pg32.b(on[0]) + self.pg32.b(r) + [self.const_b], self.pg16.b(o))
            self.pg32.free(r)
            self.pg32.free(on[0])
            self.store(self.pg16.sems[o], self.o_rows(hd * 128, 128, col0, T), self.pg16.ap(o), self.pg16.b(o), self.Os_b)
            self.pg16.free(o)
        self.deferred.append([self.defer_lag, tail])

    def allgather(self, L, s):
        cfg = self.cfg
        R = cfg.ranks[s]
        groups = [[g * R + i for i in range(R)] for g in range(8 // R)]
        for j in range(self.nchunk[L]):
            src, dst = self.loc[L, s, j], self.gat[L, s, j]
            self.fw.async1(self.fw.pool, lambda e: e.collective_compute("AllGather", ALU.bypass, replica_groups=groups, ins=[src], outs=[dst]),
                           self.s_cc[L, s], reads=[self.loc_b[L, s, j]], writes=[self.gat_b[L, s, j]])

    def run(self):
        import os
        stage = int(os.environ.get("KSTAGE", "99"))
        cfg = self.cfg
        if stage == -2:
            self.fw.dma(self.fw.sp, self.cf32[:, :], self.cf32_d, self.s_const2, writes=[self.const_b])
            return self.run_debug(0)
        self.load_consts()
        if stage == -1:
            return self.fw.finish([])
        if stage < 99:
            return self.run_debug(stage)
        self.pg16.lo, self.pg32.lo = 8, 2
        self.load_x(0)
        for t in range(cfg.NTILE):
            self.ffn(t, 0, 0, mid_hook=(lambda t=t: self.load_x(t + 1)) if t + 1 < cfg.NTILE else None)
            self.inproj_ev(t)
            if (t + 1) % cfg.TPS == 0:
                self.allgather(0, t // cfg.TPS)
        for s in range(2):
            self.attn_ev(s)
        self.pg16.lo, self.pg32.lo = 8, 2
        for t in range(cfg.NTILE):
            self.outproj(t, "w_ev_out")
            self.ffn(t, 1, 2)
            self.ffn(t, 2, 3)
            self.inproj_od(t)
            if (t + 1) % cfg.TPS == 0:
                self.allgather(1, t // cfg.TPS)
        for s in range(2):
            self.attn_od(s)
        ybuf = Buf("y")
        self.pg16.lo, self.pg32.lo = 8, 2
        for t in range(cfg.NTILE):
            self.outproj(t, "w_od_out")
            self.ffn(t, 3, 5)
            self.store_y(t, ybuf)
        self.fw.finish([ybuf])


_PROG_CACHE = {}


def build_program(nseg):
    cfg = Cfg(nseg)
    nc0 = bass.Bass("TRN2", target_bir_lowering=False)
    with contextlib.ExitStack() as es0:
        k0 = Kern(nc0, es0, cfg)
        k0.fw.dry = True
        k0.run()
        plan = list(k0.wplan)
    nc = bass.Bass("TRN2", target_bir_lowering=False)
    with contextlib.ExitStack() as es:
        k = Kern(nc, es, cfg, dry_plan=plan)
        k.run()
        assert k.wi == len(plan)
        print('instructions:', k.fw.ninst, 'sems:', k.fw.nsem)
    return nc, cfg


def _f32(a):
    return np.ascontiguousarray(np.asarray(a, dtype=np.float32))


def make_in_maps(cfg, inp):
    NSEG = cfg.NSEG
    g = {k: np.asarray(v, dtype=np.float32) for k, v in inp.items()}
    shared = {}
    w_in = np.zeros((4, NJ, 128, 2048), np.float32)
    w_out = np.zeros((4, 8, 128, 2816), np.float32)
    for f, (nm, l) in enumerate((("ffn1", 0), ("ffn2", 0), ("ffn1", 1), ("ffn2", 1))):
        Wi = g[nm + "_w_in"][l]
        Wo = g[nm + "_w_out"][l]
        for j in range(NJ):
            cols = np.concatenate([np.arange(j * 128, (j + 1) * 128), DFF + np.arange(j * 128, (j + 1) * 128)])
            w_in[f, j] = _kmajor(Wi[:, cols])
        for m in range(8):
            w_out[f, m] = _kmajor(Wo[:, m * 128:(m + 1) * 128])
    shared["w_ffn_in"] = w_in
    shared["w_ffn_out"] = w_out
    Wev = g["ev_w_in"][0]
    shared["w_ev_in_a"] = np.stack([_kmajor(Wev[:, i * 256:(i + 1) * 256]) for i in range(5)])
    shared["w_ev_in_b"] = _kmajor(Wev[:, 1280:1568])[None]
    shared["w_uq"] = _kmajor(g["b_w_uq"][0])[None]
    Wkv = g["b_w_ukv"][0].reshape(2, 128, 8, 128)
    wkp = np.zeros((128, 2, 8, 96), np.float32)
    wkp[:, :, :, 0:64] = Wkv[:, :, :, 0:64].transpose(1, 0, 2, 3)
    wvp = np.ascontiguousarray(Wkv[:, :, :, 64:128].transpose(1, 0, 2, 3)).reshape(128, 1024)
    shared["w_ukv"] = np.concatenate([wkp.reshape(128, 1536), wvp], axis=1)[None]
    shared["w_ev_out"] = np.stack([_kmajor(g["ev_w_out"][0][:, i * 256:(i + 1) * 256]) for i in range(4)])
    Wod = g["od_w_in"][0]
    shared["w_od_in"] = np.stack([_kmajor(Wod[:, i * 256:(i + 1) * 256]) for i in range(9)])
    shared["w_od_out"] = np.stack([_kmajor(g["od_w_out"][0][:, i * 256:(i + 1) * 256]) for i in range(4)])
    shared["cbf"] = _const_bf()
    cf = np.zeros((128, 256), np.float32)
    cf[:, 0:128] = np.eye(128, dtype=np.float32)
    cf[:, 128:256] = 1.0
    shared["cf32"] = cf
    import ml_dtypes
    shared["masks"] = _masks().astype(ml_dtypes.bfloat16)
    small = np.zeros((128, NSM), np.float32)
    for i, v in enumerate((g["ffn1_norm"][0], g["ev_norm"][0], g["ffn2_norm"][0], g["ffn1_norm"][1], g["od_norm"][0], g["ffn2_norm"][1])):
        small[:, SM_GD + i * 8: SM_GD + (i + 1) * 8] = v.reshape(8, 128).T
    for i, nm in enumerate(("a_q_norm", "a_k_norm", "c_q_norm", "c_k_norm", "d_q_norm", "d_k_norm")):
        small[:, SM_GH + i] = _tile2(g[nm][0])
    small[0:96, SM_GB + 0] = g["b_q_norm"][0]
    small[0:96, SM_GB + 1] = g["b_k_norm"][0]
    small[:, SM_GC:SM_GC + 4] = g["b_cq_norm"][0].reshape(4, 128).T
    small[:, SM_GC + 4:SM_GC + 6] = g["b_ckv_norm"][0].reshape(2, 128).T
    small[:, SM_GCO] = g["c_out_norm"][0]
    small[:, SM_SINK:SM_SINK + 8] = g["a_sink"][0][None, :]
    small[:, SM_LAM:SM_LAM + 256] = g["c_lambda"][0].reshape(1, 256)
    small[:, SM_EPS] = EPS
    maps = []
    for c in range(8):
        m = dict(shared)
        qi, hi = c % 4, c % 2
        xin = np.concatenate([g["x_prompt"][c // 4, qi * NSEG:(qi + 1) * NSEG], g["x_sample"][c // 2, hi * NSEG:(hi + 1) * NSEG]], axis=0)
        m["xin"] = _f32(xin)
        pos = np.concatenate([qi * NSEG + np.arange(NSEG), hi * NSEG + np.arange(NSEG)])
        m["rope"] = _rope_tables(pos)
        sm = small.copy()
        for r in range(4):
            sm[:, SM_HALO + r] = 1.0 if r == qi - 1 else 0.0
            sm[:, SM_HALO + 4 + r] = 1.0 if r == qi + 1 else 0.0
        for r in range(2):
            sm[:, SM_HALO + 8 + r] = 1.0 if r == hi - 1 else 0.0
            sm[:, SM_HALO + 10 + r] = 1.0 if r == hi + 1 else 0.0
        m["small"] = sm
        import os
        if os.environ.get("KTINYW") == "1":
            for k in list(m):
                if k.startswith("w_"):
                    a = m[k]
                    m[k] = a.reshape((-1,) + a.shape[-2:])[0:1].reshape((1,) * (a.ndim - 2) + a.shape[-2:])
        maps.append({k: (v if k == "masks" else _f32(v)) for k, v in m.items()})
    return maps


def run_kernel(inp, nseg):
    if nseg not in _PROG_CACHE:
        _PROG_CACHE[nseg] = build_program(nseg)
    nc, cfg = _PROG_CACHE[nseg]
    maps = make_in_maps(cfg, inp)
    res = run_bass_kernel_spmd(nc, maps, core_ids=list(range(8)))
    NSEG = cfg.NSEG
    yp = np.zeros((2, 4 * NSEG, D_MODEL), np.float32)
    ys = np.zeros((4, 2 * NSEG, D_MODEL), np.float32)
    for c in range(8):
        y = np.asarray(res.results[c]["y"], dtype=np.float32)
        yp[c // 4, (c % 4) * NSEG:(c % 4 + 1) * NSEG] = y[0:NSEG]
        ys[c // 2, (c % 2) * NSEG:(c % 2 + 1) * NSEG] = y[NSEG:2 * NSEG]
    return yp, ys


def kernel(**inputs):
    return run_kernel(inputs, 2048)
```

```python
import contextlib
import math
import numpy as np
import concourse.bass as bass
import concourse.mybir as mybir
from concourse.bass_utils import run_bass_kernel_spmd

F32 = mybir.dt.float32
BF16 = mybir.dt.bfloat16
AF = mybir.ActivationFunctionType
ALU = mybir.AluOpType

D_MODEL = 1024
KD = 8
DFF = 2816
NJ = 22
T = 512
EPS = 1e-6
THETA = 10000.0
GRID_W = 64
WINDOW = 128
LAM_INIT_L1 = 0.8 - 0.6 * math.exp(-0.3 * 1)

WSLOT = 3072
NWSLOT = 3
NPG16 = 41
NPG32 = 8


class Sem:
    __slots__ = ("h", "v", "dma", "name")

    def __init__(self, h, dma, name):
        self.h = h
        self.v = 0
        self.dma = dma
        self.name = name


class Buf:
    __slots__ = ("w", "r", "name", "excl")

    def __init__(self, name="", excl=False):
        self.excl = excl
        self.w = {}
        self.r = {}
        self.name = name


class Eng:
    def __init__(self, name, e, sem, is_pe=False):
        self.name = name
        self.e = e
        self.sem = sem
        self.is_pe = is_pe
        self.seen = {}


class FW:
    def __init__(self, nc, es):
        self.nc = nc
        self.es = es
        self.dry = False
        self.nsem = 0
        self.pe = Eng("pe", nc.tensor, self.new_sem("e_pe"), is_pe=True)
        self.act = Eng("act", nc.scalar, self.new_sem("e_act"))
        self.dve = Eng("dve", nc.vector, self.new_sem("e_dve"))
        self.pool = Eng("pool", nc.gpsimd, self.new_sem("e_pool"))
        self.sp = Eng("sp", nc.sync, self.new_sem("e_sp"))
        self.engs = [self.pe, self.act, self.dve, self.pool, self.sp]
        self.dma_sems = []
        self.ninst = 0

    def new_sem(self, name, dma=None):
        h = self.es.enter_context(self.nc.semaphore(name))
        self.nsem += 1
        s = Sem(h, dma, name)
        if dma:
            self.dma_sems.append(s)
        return s

    def _wait(self, eng, tok, raw):
        s, v = tok
        if s is eng.sem and eng.is_pe:
            return
        if s.dma:
            v = s.v
        if eng.seen.get(s, 0) >= v:
            return
        eng.e.wait_ge(s.h, v)
        eng.seen[s] = v

    def _deps(self, eng, reads, writes):
        for b in reads:
            for s, v in b.w.items():
                self._wait(eng, (s, v), True)
            if b.excl:
                for s, v in b.r.items():
                    if s is not eng.sem:
                        self._wait(eng, (s, v), False)
        for b in writes:
            for s, v in b.w.items():
                self._wait(eng, (s, v), False)
            for s, v in b.r.items():
                self._wait(eng, (s, v), False)

    def _commit(self, tok, reads, writes, partial=False):
        s, v = tok
        for b in reads:
            if b.r.get(s, 0) < v:
                b.r[s] = v
        for b in writes:
            if partial:
                if b.w.get(s, 0) < v:
                    b.w[s] = v
            else:
                b.w = {s: v}
            b.r = {}

    def op(self, eng, fn, reads=(), writes=()):
        if self.dry:
            return
        self._deps(eng, reads, writes)
        ins = fn(eng.e)
        eng.sem.v += 1
        ins.then_inc(eng.sem.h, 1)
        self.ninst += 1
        self._commit((eng.sem, eng.sem.v), reads, writes)

    def dma(self, q, out, in_, sem, reads=(), writes=(), partial=False):
        if self.dry:
            return
        assert sem.dma in ("hw", "sw") and (sem.dma == "sw") == (q is self.pool), (sem.name, q.name)
        self._deps(q, reads, writes)
        ins = q.e.dma_start(out=out, in_=in_)
        sem.v += 16
        ins.then_inc(sem.h, 16)
        self.ninst += 1
        self._commit((sem, sem.v), reads, writes, partial)

    def async1(self, q, fn, sem, reads=(), writes=()):
        if self.dry:
            return
        self._deps(q, reads, writes)
        ins = fn(q.e)
        sem.v += 1
        ins.then_inc(sem.h, 1)
        self._commit((sem, sem.v), reads, writes)

    def finish(self, bufs):
        if self.dry:
            return
        for b in bufs:
            for s, v in b.w.items():
                self._wait(self.sp, (s, v), True)
        for e in self.engs:
            if e is not self.sp and e.sem.v > 0:
                self._wait(self.sp, (e.sem, e.sem.v), True)
        for s in self.dma_sems:
            if s.v > 0:
                self._wait(self.sp, (s, s.v), True)


class PagePool:
    def __init__(self, tensor, n, width, name, fw):
        self.sems = [fw.new_sem(f"pg{name}{i}", dma="hw") for i in range(n)]
        self.t = tensor
        self.n = n
        self.width = width
        self.free_ = [True] * n
        self.bufs = [Buf(f"{name}{i}") for i in range(n)]
        self.name = name
        self.rot = 0
        self.lo = 0

    def alloc(self, k=1):
        n = self.n
        if k == 1:
            for off in range(1, n + 1):
                s = (self.rot - off) % n
                if self.free_[s] and s >= self.lo:
                    self.free_[s] = False
                    self.rot = s
                    return s
            for s in range(n - 1, -1, -1):
                if self.free_[s]:
                    self.free_[s] = False
                    self.rot = s
                    return s
        else:
            for s in range(0, n - k + 1):
                if all(self.free_[s:s + k]):
                    for i in range(s, s + k):
                        self.free_[i] = False
                    return s
        raise RuntimeError(f"page pool {self.name} exhausted (want {k}, free {sum(self.free_)})")

    def alloc_n(self, n):
        return [self.alloc() for _ in range(n)]

    def free_n(self, lst):
        for p in lst:
            self.free(p)

    def free(self, s, k=1):
        for i in range(s, s + k):
            assert not self.free_[i]
            self.free_[i] = True

    def reset(self):
        assert all(self.free_), f"pool {self.name} not empty at reset"

    def ap(self, s, k=1):
        return self.t[:, s * self.width:(s + k) * self.width]

    def b(self, s, k=1):
        return self.bufs[s:s + k]


class BankPool:
    def __init__(self, tensors):
        self.t = tensors
        self.bufs = [Buf(f"ps{i}", excl=True) for i in range(len(tensors))]
        self.freeq = list(range(len(tensors)))

    def alloc(self):
        if not self.freeq:
            raise RuntimeError("PSUM banks exhausted")
        return self.freeq.pop(0)

    def free(self, i):
        assert i not in self.freeq
        self.freeq.append(i)


def _kmajor(W):
    kin, n = W.shape
    return np.ascontiguousarray(W.reshape(kin // 128, 128, n).transpose(1, 0, 2)).reshape(128, -1)


def _tile2(v64):
    return np.concatenate([v64, v64], axis=0)


def _rope_tables(pos):
    pos = np.asarray(pos)
    ntok = pos.shape[0]

    def angles(p, dim):
        inv = (np.float32(THETA) ** (-(np.arange(0, dim, 2, dtype=np.float32) / np.float32(dim)))).astype(np.float32)
        ang = p.astype(np.float32)[:, None] * inv[None, :]
        return np.cos(ang).astype(np.float32), np.sin(ang).astype(np.float32)

    c64, s64 = angles(pos, 64)
    c32, s32 = angles(pos, 32)
    cr, sr = angles(pos // GRID_W, 32)
    cc, sc = angles(pos % GRID_W, 32)
    out = np.zeros((6, 128, ntok), np.float32)
    cf = np.concatenate([c64, c64], axis=1).T
    sf = np.concatenate([-s64, s64], axis=1).T
    out[0] = np.concatenate([cf, cf], axis=0)
    out[1] = np.concatenate([sf, sf], axis=0)
    out[2, :64] = 1.0
    out[2, 64:96] = np.concatenate([c32, c32], axis=1).T
    out[3, 64:96] = np.concatenate([-s32, s32], axis=1).T
    ca = np.concatenate([cr, cr, cc, cc], axis=1).T
    sa = np.concatenate([-sr, sr, -sc, sc], axis=1).T
    out[4] = np.concatenate([ca, ca], axis=0)
    out[5] = np.concatenate([sa, sa], axis=0)
    return out


def _const_bf():
    c = np.zeros((128, 7 * 128), np.float32)
    c[:, 0:128] = 1.0
    for h in range(2):
        c[h * 64:(h + 1) * 64, 128 + h * 64:128 + (h + 1) * 64] = 1.0
    c[0:96, 256:256 + 96] = 1.0
    idx = np.arange(128)
    sw = (idx // 64) * 64 + ((idx % 64) + 32) % 64
    c[sw, 384 + idx] = 1.0
    sa = (idx // 32) * 32 + ((idx % 32) + 16) % 32
    c[sa, 512 + idx] = 1.0
    m = np.arange(64, 96)
    sm = 64 + ((m - 64) + 16) % 32
    c[sm, 640 + m] = 1.0
    c[np.arange(32), 768 + 64 + np.arange(32)] = 1.0
    return c


def _masks():
    k = np.arange(128)[:, None]
    q = np.arange(512)[None, :]
    m = np.zeros((128, 6, 512), np.float32)
    for i, d in enumerate(range(-1, 5)):
        m[:, i, :] = (np.abs(d * 128 + k - q) <= WINDOW).astype(np.float32)
    return m.reshape(128, 6 * 512)


SM_GD = 0
SM_GH = 48
SM_GB = 54
SM_GC = 56
SM_GCO = 62
SM_SINK = 63
SM_LAM = 71
SM_HALO = 327
SM_EPS = 339
NSM = 340

CB_ONES, CB_BLK64, CB_ONES96, CB_SWF, CB_SWA, CB_SWM, CB_KRSEL = [i * 128 for i in range(7)]


class Cfg:
    def __init__(self, nseg=2048):
        self.NSEG = nseg
        self.NT = 2 * nseg
        self.NTILE = self.NT // T
        self.TPS = nseg // T
        self.NC = nseg // 128
        self.ranks = (4, 2)
        self.KR = (896, 640)
        self.R = (896 + 650, 640 + 642)


class Kern:
    def __init__(self, nc, es, cfg, dry_plan=None):
        self.nc = nc
        self.cfg = cfg
        self.fw = FW(nc, es)
        fw = self.fw
        NT, NSEG, NC = cfg.NT, cfg.NSEG, cfg.NC

        import os
        tiny = os.environ.get("KTINYW") == "1"

        def din(name, shape):
            if tiny and name.startswith("w_"):
                shape = [1] * (len(shape) - 2) + list(shape[-2:])
            return nc.dram_tensor(name, list(shape), F32, kind="ExternalInput").ap()

        self.xin = din("xin", [NT, 1024])
        self.rope = din("rope", [6, 128, NT])
        self.small_d = din("small", [128, NSM])
        self.cbf_d = din("cbf", [128, 896])
        self.cf32_d = din("cf32", [128, 256])
        self.masks_d = nc.dram_tensor("masks", [128, 3072], BF16, kind="ExternalInput").ap()
        self.w_ffn_in = din("w_ffn_in", [4, NJ, 128, 2048])
        self.w_ffn_out = din("w_ffn_out", [4, 8, 128, 2816])
        self.w_ev_in_a = din("w_ev_in_a", [5, 128, 2048])
        self.w_ev_in_b = din("w_ev_in_b", [1, 128, 2304])
        self.w_uq = din("w_uq", [1, 128, 3072])
        self.w_ukv = din("w_ukv", [1, 128, 2560])
        self.w_ev_out = din("w_ev_out", [4, 128, 2048])
        self.w_od_in = din("w_od_in", [9, 128, 2048])
        self.w_od_out = din("w_od_out", [4, 128, 2048])
        self.y = nc.dram_tensor("y", [NT, 1024], F32, kind="ExternalOutput").ap()

        def dint(name, n):
            return nc.dram_tensor(name, [n], BF16, kind="Internal").ap()

        self.Qs = dint("Qs", 1280 * NT)
        self.Os = dint("Os", 1024 * NT)
        self.Qs_b = Buf("Qs")
        self.Os_b = Buf("Os")
        ulist = {0: [(("KA",), 128)] + [(("KB", h), 96) for h in range(8)] + [(("V", hv), 65) for hv in range(10)],
                 1: [(("KC", h), 128) for h in range(4)] + [(("KD",), 128)] + [(("V", hv), 128) for hv in range(4)] + [(("V", 4), 65), (("V", 5), 65)]}
        self.units = {}
        self.nchunk = {}
        chunk_rows = {}
        for L in range(2):
            j, used = 0, 0
            for name, n in ulist[L]:
                if used + n > 256:
                    chunk_rows[L, j] = used
                    j, used = j + 1, 0
                self.units[L, name] = (j, used, n)
                used += n
            chunk_rows[L, j] = used
            self.nchunk[L] = j + 1
        self.loc = {}
        self.gat = {}
        self.loc_b = {}
        self.gat_b = {}
        for L in range(2):
            for s in range(2):
                for j in range(self.nchunk[L]):
                    rows = chunk_rows[L, j]
                    self.loc[L, s, j] = nc.dram_tensor(f"loc{L}{s}_{j}", [rows, NSEG], BF16, kind="Internal").ap()
                    self.gat[L, s, j] = nc.dram_tensor(f"gat{L}{s}_{j}", [cfg.ranks[s] * rows, NSEG], BF16, kind="Internal").ap()
                    self.loc_b[L, s, j] = Buf(f"loc{L}{s}_{j}")
                    self.gat_b[L, s, j] = Buf(f"gat{L}{s}_{j}")

        def sb(name, shape, dt):
            return es.enter_context(nc.sbuf_tensor("sb_" + name, list(shape), dt))

        self.xT = sb("xT", [128, KD * NT], F32)
        self.xT_b = [[Buf(f"x{k}_{t}") for t in range(cfg.NTILE)] for k in range(KD)]
        self.small = sb("small", [128, NSM], F32)
        self.cbf = sb("cbf", [128, 896], BF16)
        self.cf32 = sb("cf32", [128, 256], F32)
        self.const_b = Buf("consts")
        self.wsl = sb("wsl", [128, NWSLOT * WSLOT], BF16)
        self.wsl_b = [Buf(f"wsl{i}") for i in range(NWSLOT)]
        self.wsl_sem = [fw.new_sem(f"wsl{i}", dma="sw") for i in range(NWSLOT)]
        p16 = sb("pg16", [128, NPG16 * T], BF16)
        p32 = sb("pg32", [128, NPG32 * T], F32)
        self.pg16 = PagePool(p16, NPG16, T, "h", fw)
        self.pg32 = PagePool(p32, NPG32, T, "f", fw)
        banks = [es.enter_context(nc.psum_tensor(f"psb{i}", [128, T], F32)) for i in range(8)]
        self.ps = BankPool(banks)
        self.s_cc = {(L, s_): fw.new_sem(f"cc{L}{s_}", dma="cc") for L in range(2) for s_ in range(2)}
        self.s_attn = {(k_, r_): fw.new_sem(f"at{k_}{r_}", dma="sw") for k_ in ("k", "v") for r_ in range(4)}
        self.s_attn["q", 0] = fw.new_sem("atq0", dma="sw")
        self.s_attn["q", 1] = fw.new_sem("atq1", dma="sw")
        self.qpar = 0
        self.s_const = fw.new_sem("const", dma="hw")
        self.s_const2 = fw.new_sem("const2", dma="hw")
        self.s_constp = fw.new_sem("constp", dma="sw")
        self.plan = dry_plan
        self.wplan = []
        self.wi = 0
        self.wloaded = 0

    def xTt(self, k, t, sub=None):
        NT = self.cfg.NT
        if sub is None:
            return self.xT[:, k * NT + t * T: k * NT + (t + 1) * T]
        return self.xT[:, k * NT + t * T + sub * 128: k * NT + t * T + (sub + 1) * 128]

    def psap(self, i, p=128, n=T):
        return self.ps.t[i][0:p, 0:n]

    def mm(self, out, lhsT, rhs, start, stop, reads, writes):
        self.fw.op(self.fw.pe, lambda e: e.matmul(out, lhsT=lhsT, rhs=rhs, start=start, stop=stop), reads, writes)

    def wget(self, key, nel):
        fw = self.fw
        if fw.dry:
            self.wplan.append((key, nel))
            return self.wsl[:, 0:nel], [self.wsl_b[0]]
        i = self.wi
        plan = self.plan
        assert plan[i][1] == nel
        while self.wloaded < min(len(plan), i + NWSLOT):
            g = self.wloaded
            s = g % NWSLOT
            kk, ne = plan[g]
            ap_d = getattr(self, kk[0])
            for ix in kk[1:]:
                ap_d = ap_d[ix]
            fw.dma(fw.pool, self.wsl[:, s * WSLOT: s * WSLOT + ne], ap_d, self.wsl_sem[s], reads=(), writes=[self.wsl_b[s]])
            self.wloaded += 1
        self.wi += 1
        s = i % NWSLOT
        return self.wsl[:, s * WSLOT: s * WSLOT + nel], [self.wsl_b[s]]

    def sm(self, col, p=128, n=1):
        return self.small[0:p, col:col + n]

    def load_consts(self):
        fw = self.fw
        fw.dma(fw.sp, self.small[:, :], self.small_d, self.s_const, writes=[self.const_b])
        fw.dma(fw.sp, self.cf32[:, :], self.cf32_d, self.s_const2, writes=[self.const_b])
        fw.dma(fw.pool, self.cbf[:, :], self.cbf_d, self.s_constp, writes=[self.const_b])
        cb = [self.const_b]
        sm = self.small
        fw.op(fw.act, lambda e: e.activation(out=sm[:, SM_SINK:SM_SINK + 8], in_=sm[:, SM_SINK:SM_SINK + 8], func=AF.Exp), cb, cb)
        lam = sm[:, SM_LAM:SM_LAM + 256]
        fw.op(fw.dve, lambda e: e.tensor_tensor(out=lam[:, 0:64], in0=lam[:, 0:64], in1=lam[:, 64:128], op=ALU.mult), cb, cb)
        fw.op(fw.dve, lambda e: e.tensor_tensor(out=lam[:, 128:192], in0=lam[:, 128:192], in1=lam[:, 192:256], op=ALU.mult), cb, cb)
        fw.op(fw.dve, lambda e: e.reduce_sum(out=lam[:, 64:65], in_=lam[:, 0:64], axis=mybir.AxisListType.X), cb, cb)
        fw.op(fw.dve, lambda e: e.reduce_sum(out=lam[:, 65:66], in_=lam[:, 128:192], axis=mybir.AxisListType.X), cb, cb)
        fw.op(fw.act, lambda e: e.activation(out=lam[:, 64:66], in_=lam[:, 64:66], func=AF.Exp), cb, cb)
        fw.op(fw.dve, lambda e: e.tensor_tensor(out=lam[:, 66:67], in0=lam[:, 65:66], in1=lam[:, 64:65], op=ALU.subtract), cb, cb)
        fw.op(fw.dve, lambda e: e.tensor_scalar(out=lam[:, 66:67], in0=lam[:, 66:67], scalar1=-LAM_INIT_L1, scalar2=None, op0=ALU.add), cb, cb)
        fw.op(fw.dve, lambda e: e.tensor_scalar(out=sm[:, SM_GCO:SM_GCO + 1], in0=sm[:, SM_GCO:SM_GCO + 1], scalar1=1.0 - LAM_INIT_L1, scalar2=None, op0=ALU.mult), cb, cb)
        self.neglam = lam[:, 66:67]

    def rstd_of(self, chunks, inv_n, ones_ap, P):
        fw = self.fw
        ss = self.ps.alloc()
        n = len(chunks)
        for i, (src, sbufs) in enumerate(chunks):
            sq = self.pg16.alloc()
            sq_ap = self.pg16.ap(sq)[0:P, :]
            fw.op(fw.act, lambda e: e.activation(out=sq_ap, in_=src, func=AF.Square), sbufs, self.pg16.b(sq))
            self.mm(self.psap(ss, P), ones_ap, sq_ap, i == 0, i == n - 1, self.pg16.b(sq) + [self.const_b], [self.ps.bufs[ss]])
            self.pg16.free(sq)
        r = self.pg32.alloc()
        r_ap = self.pg32.ap(r)[0:P, :]
        fw.op(fw.act, lambda e: e.activation(out=r_ap, in_=self.psap(ss, P), func=AF.Ln, bias=self.sm(SM_EPS, P), scale=inv_n),
              [self.ps.bufs[ss], self.const_b], self.pg32.b(r))
        self.ps.free(ss)
        fw.op(fw.act, lambda e: e.activation(out=r_ap, in_=r_ap, func=AF.Exp, scale=-0.5), self.pg32.b(r), self.pg32.b(r))
        return r

    def norm_dmodel(self, t, gi):
        fw = self.fw
        chunks = [(self.xTt(k, t), [self.xT_b[k][t]]) for k in range(KD)]
        r = self.rstd_of(chunks, 1.0 / D_MODEL, self.cbf[:, CB_ONES:CB_ONES + 128], 128)
        h0 = self.pg16.alloc_n(KD)
        for k in range(KD):
            g = self.sm(SM_GD + gi * 8 + k)
            o = self.pg16.ap(h0[k])
            fw.op(fw.dve, lambda e: e.scalar_tensor_tensor(out=o, in0=self.xTt(k, t), scalar=g, in1=self.pg32.ap(r), op0=ALU.mult, op1=ALU.mult),
                  [self.xT_b[k][t], self.const_b] + self.pg32.b(r), self.pg16.b(h0[k]))
        self.pg32.free(r)
        return h0

    def ffn(self, t, f, gi, mid_hook=None, h0=None):
        fw = self.fw
        if h0 is None:
            h0 = self.norm_dmodel(t, gi)
        a0 = self.pg16.alloc_n(NJ)
        for j in range(NJ):
            w, wb = self.wget(("w_ffn_in", f, j), 2048)
            w3 = w.rearrange("p (k n) -> p k n", n=256)
            pg_ = self.ps.alloc()
            pu_ = self.ps.alloc()
            for half, pb in ((0, pg_), (1, pu_)):
                for k in range(KD):
                    self.mm(self.psap(pb), w3[:, k, half * 128:(half + 1) * 128], self.pg16.ap(h0[k]), k == 0, k == KD - 1,
                            wb + self.pg16.b(h0[k]), [self.ps.bufs[pb]])
            s = self.pg32.alloc()
            fw.op(fw.act, lambda e: e.activation(out=self.pg32.ap(s), in_=self.psap(pg_), func=AF.Silu), [self.ps.bufs[pg_]], self.pg32.b(s))
            self.ps.free(pg_)
            fw.op(fw.dve, lambda e: e.tensor_tensor(out=self.pg16.ap(a0[j]), in0=self.psap(pu_), in1=self.pg32.ap(s), op=ALU.mult),
                  [self.ps.bufs[pu_]] + self.pg32.b(s), self.pg16.b(a0[j]))
            self.ps.free(pu_)
            self.pg32.free(s)
        self.pg16.free_n(h0)
        if mid_hook is not None:
            mid_hook()
        for m in range(KD):
            w, wb = self.wget(("w_ffn_out", f, m), 2816)
            w3 = w.rearrange("p (j n) -> p j n", n=128)
            acc = self.ps.alloc()
            for j in range(NJ):
                self.mm(self.psap(acc), w3[:, j, :], self.pg16.ap(a0[j]), j == 0, j == NJ - 1, wb + self.pg16.b(a0[j]), [self.ps.bufs[acc]])
            xk = self.xTt(m, t)
            fw.op(fw.dve, lambda e: e.scalar_tensor_tensor(out=xk, in0=self.psap(acc), scalar=0.5, in1=xk, op0=ALU.mult, op1=ALU.add),
                  [self.ps.bufs[acc], self.xT_b[m][t]], [self.xT_b[m][t]])
            self.ps.free(acc)
        self.pg16.free_n(a0)

    def unit_a(self, zp, P, ones_ap, inv_n, swap_ap, cos_ap, sin_ap, gain_ap, rope_bufs):
        fw = self.fw
        zb = [self.ps.bufs[zp]]
        z = self.psap(zp, P)
        sq = self.pg16.alloc()
        sq_ap = self.pg16.ap(sq)[0:P, :]
        fw.op(fw.act, lambda e: e.activation(out=sq_ap, in_=z, func=AF.Square), zb, self.pg16.b(sq))
        xg = self.pg16.alloc()
        xg_ap = self.pg16.ap(xg)[0:P, :]
        fw.op(fw.act, lambda e: e.activation(out=xg_ap, in_=z, func=AF.Copy, scale=gain_ap), zb + [self.const_b], self.pg16.b(xg))
        ss = self.ps.alloc()
        self.mm(self.psap(ss, P), ones_ap, sq_ap, True, True, self.pg16.b(sq) + [self.const_b], [self.ps.bufs[ss]])
        self.pg16.free(sq)
        rot = self.ps.alloc()
        self.mm(self.psap(rot, P), swap_ap, xg_ap, True, True, self.pg16.b(xg) + [self.const_b], [self.ps.bufs[rot]])
        self.pg16.free(xg)
        r = self.pg32.alloc()
        r_ap = self.pg32.ap(r)[0:P, :]
        fw.op(fw.act, lambda e: e.activation(out=r_ap, in_=self.psap(ss, P), func=AF.Ln, bias=self.sm(SM_EPS, P), scale=inv_n),
              [self.ps.bufs[ss], self.const_b], self.pg32.b(r))
        self.ps.free(ss)
        fw.op(fw.act, lambda e: e.activation(out=r_ap, in_=r_ap, func=AF.Exp, scale=-0.5), self.pg32.b(r), self.pg32.b(r))
        return (zp, P, r, rot, cos_ap, sin_ap, gain_ap, rope_bufs)

    def unit_b(self, ctx):
        fw = self.fw
        zp, P, r, rot, cos_ap, sin_ap, gain_ap, rope_bufs = ctx
        zb = [self.ps.bufs[zp]]
        z = self.psap(zp, P)
        t1 = self.pg32.alloc()
        t1_ap = self.pg32.ap(t1)[0:P, :]
        fw.op(fw.dve, lambda e: e.scalar_tensor_tensor(out=t1_ap, in0=z, scalar=gain_ap, in1=cos_ap, op0=ALU.mult, op1=ALU.mult),
              zb + [self.const_b] + rope_bufs, self.pg32.b(t1))
        self.ps.free(zp)
        t2 = self.pg32.alloc()
        t2_ap = self.pg32.ap(t2)[0:P, :]
        fw.op(fw.dve, lambda e: e.tensor_tensor(out=t2_ap, in0=self.psap(rot, P), in1=sin_ap, op=ALU.mult),
              [self.ps.bufs[rot]] + rope_bufs, self.pg32.b(t2))
        self.ps.free(rot)
        fw.op(fw.dve, lambda e: e.tensor_tensor(out=t1_ap, in0=t1_ap, in1=t2_ap, op=ALU.add), self.pg32.b(t1) + self.pg32.b(t2), self.pg32.b(t1))
        self.pg32.free(t2)
        o = self.pg16.alloc()
        o_ap = self.pg16.ap(o)[0:P, :]
        fw.op(fw.dve, lambda e: e.tensor_tensor(out=o_ap, in0=t1_ap, in1=self.pg32.ap(r)[0:P, :], op=ALU.mult),
              self.pg32.b(t1) + self.pg32.b(r), self.pg16.b(o))
        self.pg32.free(t1)
        self.pg32.free(r)
        return o

    def unit(self, *args):
        return self.unit_b(self.unit_a(*args))

    def proj_fm(self, w3, wb, col0, M, h0, nk):
        pb = self.ps.alloc()
        for k in range(nk):
            self.mm(self.psap(pb, M), w3[:, k, col0:col0 + M], self.pg16.ap(h0[k]), k == 0, k == nk - 1,
                    wb + self.pg16.b(h0[k]), [self.ps.bufs[pb]])
        return pb

    def q_rows(self, r0, P, t):
        NT = self.cfg.NT
        return self.Qs.rearrange("(r n) -> r n", n=NT)[r0:r0 + P, t * T:(t + 1) * T]

    def o_rows(self, r0, P, c0, n):
        NT = self.cfg.NT
        return self.Os.rearrange("(r n) -> r n", n=NT)[r0:r0 + P, c0:c0 + n]

    def u_loc(self, L, s, unit):
        j, off, n = self.units[L, unit]
        return self.loc[L, s, j][off:off + n, :], self.loc_b[L, s, j]

    def u_gat(self, L, s, unit):
        j, off, n = self.units[L, unit]
        R = self.cfg.ranks[s]
        return self.gat[L, s, j].rearrange("(k r) n -> k r n", k=R)[:, off:off + n, :], self.gat_b[L, s, j]

    def k_loc(self, L, s, unit, ti):
        ap, b = self.u_loc(L, s, unit)
        return ap[:, ti * T:(ti + 1) * T], b

    def v_width(self, L, hv):
        return 128 if (L == 1 and hv < 4) else 65

    def v_loc3(self, L, s, hv):
        W = self.v_width(L, hv)
        ap, b = self.u_loc(L, s, ("V", hv))
        return ap.rearrange("r n -> (r n)").rearrange("(p c e) -> p c e", p=128, e=W), b

    def v_gat4(self, L, s, hv):
        W = self.v_width(L, hv)
        ap, b = self.u_gat(L, s, ("V", hv))
        return ap.rearrange("k r n -> k (r n)").rearrange("k (p c e) -> p k c e", p=128, e=W), b, W

    def store_v(self, L, s, ti, v0, vb, tile4, hv0):
        for i in range(tile4.shape[1]):
            dst, b = self.v_loc3(L, s, hv0 + i)
            self.store(self.pg16.sems[v0], dst[:, ti * 4:(ti + 1) * 4, :], tile4[:, i, :, :], vb, b)

    def store(self, sem, dst, src, src_bufs, dst_buf):
        self.fw.dma(self.fw.sp, dst, src, sem, reads=src_bufs, writes=[dst_buf], partial=True)

    def seg_of(self, t):
        return t // self.cfg.TPS, t % self.cfg.TPS

    def load_rope(self, t, variants):
        fw = self.fw
        res = {}
        for i, v in enumerate(variants):
            c = self.pg32.alloc()
            s = self.pg32.alloc()
            fw.dma(fw.sp, self.pg32.ap(c), self.rope[2 * v, :, t * T:(t + 1) * T], self.pg32.sems[c], writes=self.pg32.b(c))
            fw.dma(fw.sp, self.pg32.ap(s), self.rope[2 * v + 1, :, t * T:(t + 1) * T], self.pg32.sems[s], writes=self.pg32.b(s))
            res[v] = (c, s)
        return res

    def free_rope(self, rp):
        for c, s in rp.values():
            self.pg32.free(c)
            self.pg32.free(s)

    def unit_v(self, zp, variant, rp, gain_col, P=128, phase_a_only=False):
        ones = {0: CB_BLK64, 1: CB_ONES96, 2: CB_BLK64}[variant]
        swp = {0: CB_SWF, 1: CB_SWM, 2: CB_SWA}[variant]
        inv_n = {0: 1.0 / 64, 1: 1.0 / 96, 2: 1.0 / 64}[variant]
        c, s = rp[variant]
        fn = self.unit_a if phase_a_only else self.unit
        return fn(zp, P, self.cbf[0:P, ones:ones + P], inv_n, self.cbf[0:P, swp:swp + P],
                  self.pg32.ap(c)[0:P, :], self.pg32.ap(s)[0:P, :], self.sm(gain_col, P), self.pg32.b(c) + self.pg32.b(s))

    def vtile_alloc(self, L):
        fw = self.fw
        v0 = self.pg16.alloc(6)
        flat = self.pg16.ap(v0, 6)
        vb = self.pg16.b(v0, 6)
        if L == 0:
            v65 = flat[:, 0:10 * 4 * 65].rearrange("p (h c e) -> p h c e", h=10, e=65)
            fw.op(fw.dve, lambda e: e.memset(v65[:, :, :, 64:65], 1.0), (), vb)
            return v0, vb, v65, None
        v128 = flat[:, 0:4 * 4 * 128].rearrange("p (h c e) -> p h c e", h=4, e=128)
        v65 = flat[:, 2048:2048 + 2 * 4 * 65].rearrange("p (h c e) -> p h c e", h=2, e=65)
        fw.op(fw.dve, lambda e: e.memset(v65[:, :, :, 64:65], 1.0), (), vb)
        return v0, vb, v65, v128

    def v_tokmajor(self, lhs_pages, nk, rhs_fn, ncols, rb):
        banks = []
        if ncols <= 128:
            pb = self.ps.alloc()
            for sub in range(4):
                for k in range(nk):
                    self.mm(self.ps.t[pb][:, sub * ncols:(sub + 1) * ncols], self.pg16.ap(lhs_pages[k])[:, sub * 128:(sub + 1) * 128], rhs_fn(k),
                            k == 0, k == nk - 1, rb + self.pg16.b(lhs_pages[k]), [self.ps.bufs[pb]])
            return [pb]
        for sub in range(4):
            pb = self.ps.alloc()
            for k in range(nk):
                self.mm(self.ps.t[pb][:, 0:ncols], self.pg16.ap(lhs_pages[k])[:, sub * 128:(sub + 1) * 128], rhs_fn(k),
                        k == 0, k == nk - 1, rb + self.pg16.b(lhs_pages[k]), [self.ps.bufs[pb]])
            banks.append(pb)
        return banks

    def run_units(self, items):
        n = len(items)
        zps, ctxs = {}, {}
        for step in range(n + 2):
            if step < n:
                zps[step] = items[step][0]()
            i = step - 1
            if 0 <= i < n:
                ctxs[i] = self.unit_v(zps.pop(i), phase_a_only=True, **items[i][1])
            i = step - 2
            if 0 <= i < n:
                o = self.unit_b(ctxs.pop(i))
                items[i][2](o)
                self.pg16.free(o)

    def inproj_ev(self, t):
        fw = self.fw
        s, ti = self.seg_of(t)
        L = 0
        h0 = self.norm_dmodel(t, 1)
        rp = self.load_rope(t, (0, 1))
        v0, vb, v65, _ = self.vtile_alloc(0)
        items = []
        wst = {}
        for g in range(2):
            for c in range(2):
                def proj(g=g, c=c):
                    if c == 0:
                        w, wb = self.wget(("w_ev_in_a", g), 2048)
                        wst[g] = (w.rearrange("p (k n) -> p k n", n=256), wb)
                    return self.proj_fm(wst[g][0], wst[g][1], c * 128, 128, h0, KD)

                def st(o, g=g, c=c):
                    self.store(self.pg16.sems[o], self.q_rows((g * 2 + c) * 128, 128, t), self.pg16.ap(o), self.pg16.b(o), self.Qs_b)
                items.append((proj, dict(variant=0, rp=rp, gain_col=SM_GH + 0), st))
        self.run_units(items)
        w, wb = self.wget(("w_ev_in_a", 2), 2048)
        w3 = w.rearrange("p (k n) -> p k n", n=256)
        zp = self.proj_fm(w3, wb, 0, 128, h0, KD)
        o = self.unit_v(zp, 0, rp, SM_GH + 1)
        self.store(self.pg16.sems[o], self.k_loc(L, s, ('KA',), ti)[0], self.pg16.ap(o), self.pg16.b(o), self.k_loc(L, s, ('KA',), ti)[1])
        self.pg16.free(o)
        (pb,) = self.v_tokmajor(h0, KD, lambda k: w3[:, k, 128:256], 128, wb)
        src = self.ps.t[pb][:, 0:512].rearrange("p (c h e) -> p h c e", c=4, h=2)
        fw.op(fw.dve, lambda e: e.tensor_copy(out=v65[:, 0:2, :, 0:64], in_=src), [self.ps.bufs[pb]], vb)
        self.ps.free(pb)
        cq = []
        for g in (3, 4):
            w, wb = self.wget(("w_ev_in_a", g), 2048)
            w3 = w.rearrange("p (k n) -> p k n", n=256)
            for c in range(2):
                cq.append(self.proj_fm(w3, wb, c * 128, 128, h0, KD))
        r = self.rstd_of([(self.psap(b), [self.ps.bufs[b]]) for b in cq], 1.0 / 512, self.cbf[:, CB_ONES:CB_ONES + 128], 128)
        cqn = self.pg16.alloc_n(4)
        for c in range(4):
            fw.op(fw.dve, lambda e: e.scalar_tensor_tensor(out=self.pg16.ap(cqn[c]), in0=self.psap(cq[c]), scalar=self.sm(SM_GC + c),
                                                            in1=self.pg32.ap(r), op0=ALU.mult, op1=ALU.mult),
                  [self.ps.bufs[cq[c]], self.const_b] + self.pg32.b(r), self.pg16.b(cqn[c]))
            self.ps.free(cq[c])
        self.pg32.free(r)
        w, wb = self.wget(("w_ev_in_b", 0), 2304)
        w3 = w.rearrange("p (k n) -> p k n", n=288)
        ck = [self.proj_fm(w3, wb, c * 128, 128, h0, KD) for c in range(2)]
        krp = self.proj_fm(w3, wb, 256, 32, h0, KD)
        self.pg16.free_n(h0)
        kr = self.pg16.alloc()
        fw.op(fw.act, lambda e: e.activation(out=self.pg16.ap(kr)[0:32, :], in_=self.psap(krp, 32), func=AF.Copy), [self.ps.bufs[krp]], self.pg16.b(kr))
        self.ps.free(krp)
        r = self.rstd_of([(self.psap(b), [self.ps.bufs[b]]) for b in ck], 1.0 / 256, self.cbf[:, CB_ONES:CB_ONES + 128], 128)
        ckn = self.pg16.alloc_n(2)
        for c in range(2):
            fw.op(fw.dve, lambda e: e.scalar_tensor_tensor(out=self.pg16.ap(ckn[c]), in0=self.psap(ck[c]), scalar=self.sm(SM_GC + 4 + c),
                                                            in1=self.pg32.ap(r), op0=ALU.mult, op1=ALU.mult),
                  [self.ps.bufs[ck[c]], self.const_b] + self.pg32.b(r), self.pg16.b(ckn[c]))
            self.ps.free(ck[c])
        self.pg32.free(r)
        w, wb = self.wget(("w_uq", 0), 3072)
        w3 = w.rearrange("p (k n) -> p k n", n=768)
        items = []
        for h in range(8):
            def proj(h=h, w3=w3, wb=wb):
                return self.proj_fm(w3, wb, h * 96, 96, cqn, 4)

            def st(o, h=h):
                self.store(self.pg16.sems[o], self.q_rows(512 + h * 96, 96, t), self.pg16.ap(o)[0:96, :], self.pg16.b(o), self.Qs_b)
            items.append((proj, dict(variant=1, rp=rp, gain_col=SM_GB + 0, P=96), st))
        self.run_units(items)
        self.pg16.free_n(cqn)
        w, wb = self.wget(("w_ukv", 0), 2560)
        wk = w[:, 0:1536].rearrange("p (k h e) -> p k h e", k=2, h=8)
        items = []
        for h in range(8):
            def proj(h=h, wk=wk, wb=wb):
                zp = self.ps.alloc()
                self.mm(self.psap(zp, 96), self.cbf[0:32, CB_KRSEL:CB_KRSEL + 96], self.pg16.ap(kr)[0:32, :], True, False,
                        self.pg16.b(kr) + [self.const_b], [self.ps.bufs[zp]])
                for c in range(2):
                    self.mm(self.psap(zp, 96), wk[:, c, h, :], self.pg16.ap(ckn[c]), False, c == 1,
                            wb + self.pg16.b(ckn[c]), [self.ps.bufs[zp]])
                return zp

            def st(o, h=h):
                self.store(self.pg16.sems[o], self.k_loc(L, s, ('KB', h), ti)[0], self.pg16.ap(o)[0:96, :], self.pg16.b(o), self.k_loc(L, s, ('KB', h), ti)[1])
            items.append((proj, dict(variant=1, rp=rp, gain_col=SM_GB + 1, P=96), st))
        self.run_units(items)
        self.pg16.free(kr)
        wv = w[:, 1536:2560].rearrange("p (k n) -> p k n", k=2)
        banks = self.v_tokmajor(ckn, 2, lambda k: wv[:, k, :], 512, wb)
        for sub, pb in enumerate(banks):
            src = self.ps.t[pb][:, 0:512].rearrange("p (h e) -> p h e", h=8)
            fw.op(fw.dve, lambda e: e.tensor_copy(out=v65[:, 2:10, sub, 0:64], in_=src), [self.ps.bufs[pb]], vb)
            self.ps.free(pb)
        self.pg16.free_n(ckn)
        self.free_rope(rp)
        self.store_v(L, s, ti, v0, vb, v65, 0)
        self.pg16.free(v0, 6)

    def inproj_od(self, t):
        fw = self.fw
        s, ti = self.seg_of(t)
        L = 1
        h0 = self.norm_dmodel(t, 4)
        rp = self.load_rope(t, (0, 2))
        v0, vb, v65, v128 = self.vtile_alloc(1)
        items = []
        wst = {}
        for g in range(4):
            for c in range(2):
                def proj(g=g, c=c):
                    if c == 0:
                        w, wb = self.wget(("w_od_in", g), 2048)
                        wst[g] = (w.rearrange("p (k n) -> p k n", n=256), wb)
                    return self.proj_fm(wst[g][0], wst[g][1], c * 128, 128, h0, KD)

                def st(o, g=g, c=c):
                    if g < 2:
                        self.store(self.pg16.sems[o], self.q_rows((g * 2 + c) * 128, 128, t), self.pg16.ap(o), self.pg16.b(o), self.Qs_b)
                    else:
                        self.store(self.pg16.sems[o], self.k_loc(L, s, ('KC', (g - 2) * 2 + c), ti)[0], self.pg16.ap(o), self.pg16.b(o), self.k_loc(L, s, ('KC', (g - 2) * 2 + c), ti)[1])
                items.append((proj, dict(variant=0, rp=rp, gain_col=SM_GH + (2 if g < 2 else 3)), st))
        self.run_units(items)
        for g in (4, 5):
            w, wb = self.wget(("w_od_in", g), 2048)
            w3 = w.rearrange("p (k n) -> p k n", n=256)
            banks = self.v_tokmajor(h0, KD, lambda k: w3[:, k, :], 256, wb)
            for sub, pb in enumerate(banks):
                src = self.ps.t[pb][:, 0:256].rearrange("p (h e) -> p h e", h=2)
                fw.op(fw.dve, lambda e: e.tensor_copy(out=v128[:, 2 * (g - 4):2 * (g - 4) + 2, sub, :], in_=src), [self.ps.bufs[pb]], vb)
                self.ps.free(pb)
        items = []
        wst = {}
        for g in (6, 7):
            for c in range(2):
                def proj(g=g, c=c):
                    if c == 0:
                        w, wb = self.wget(("w_od_in", g), 2048)
                        wst[g] = (w.rearrange("p (k n) -> p k n", n=256), wb)
                    return self.proj_fm(wst[g][0], wst[g][1], c * 128, 128, h0, KD)

                def st(o, g=g, c=c):
                    self.store(self.pg16.sems[o], self.q_rows(512 + ((g - 6) * 2 + c) * 128, 128, t), self.pg16.ap(o), self.pg16.b(o), self.Qs_b)
                items.append((proj, dict(variant=2, rp=rp, gain_col=SM_GH + 4), st))
        self.run_units(items)
        w, wb = self.wget(("w_od_in", 8), 2048)
        w3 = w.rearrange("p (k n) -> p k n", n=256)
        zp = self.proj_fm(w3, wb, 0, 128, h0, KD)
        o = self.unit_v(zp, 2, rp, SM_GH + 5)
        self.store(self.pg16.sems[o], self.k_loc(L, s, ('KD',), ti)[0], self.pg16.ap(o), self.pg16.b(o), self.k_loc(L, s, ('KD',), ti)[1])
        self.pg16.free(o)
        (pb,) = self.v_tokmajor(h0, KD, lambda k: w3[:, k, 128:256], 128, wb)
        src = self.ps.t[pb][:, 0:512].rearrange("p (c h e) -> p h c e", c=4, h=2)
        fw.op(fw.dve, lambda e: e.tensor_copy(out=v65[:, 0:2, :, 0:64], in_=src), [self.ps.bufs[pb]], vb)
        self.ps.free(pb)
        self.pg16.free_n(h0)
        self.free_rope(rp)
        self.store_v(L, s, ti, v0, vb, v128, 0)
        self.store_v(L, s, ti, v0, vb, v65, 4)
        self.pg16.free(v0, 6)

    def outproj(self, t, wname):
        fw = self.fw
        NT = self.cfg.NT
        o0 = self.pg16.alloc(KD)
        src = self.Os.rearrange("(k p n) -> p k n", p=128, n=NT)[:, :, t * T:(t + 1) * T]
        dst = self.pg16.ap(o0, KD).rearrange("p (k n) -> p k n", n=T)
        fw.dma(fw.sp, dst, src, self.pg16.sems[o0], reads=[self.Os_b], writes=self.pg16.b(o0, KD))
        for g in range(4):
            w, wb = self.wget((wname, g), 2048)
            w3 = w.rearrange("p (k n) -> p k n", n=256)
            for c in range(2):
                m = g * 2 + c
                acc = self.proj_fm(w3, wb, c * 128, 128, list(range(o0, o0 + KD)), KD)
                xk = self.xTt(m, t)
                fw.op(fw.dve, lambda e: e.tensor_tensor(out=xk, in0=self.psap(acc), in1=xk, op=ALU.add),
                      [self.ps.bufs[acc], self.xT_b[m][t]], [self.xT_b[m][t]])
                self.ps.free(acc)
        self.pg16.free(o0, KD)

    def load_x(self, t):
        fw = self.fw
        ident = self.cf32[:, 0:128]
        for sub in range(4):
            st = self.pg32.alloc(2)
            r0 = t * T + sub * 128
            fw.dma(fw.sp, self.pg32.ap(st, 2), self.xin[r0:r0 + 128, :], self.pg32.sems[st], writes=self.pg32.b(st, 2))
            for q in range(2):
                pb = self.ps.alloc()
                for j in range(4):
                    k = q * 4 + j
                    fw.op(fw.pe, lambda e: e.transpose(self.ps.t[pb][:, j * 128:(j + 1) * 128], self.pg32.ap(st, 2)[:, k * 128:(k + 1) * 128], ident),
                          self.pg32.b(st, 2) + [self.const_b], [self.ps.bufs[pb]])
                for j in range(4):
                    k = q * 4 + j
                    fw.op(fw.dve if j % 2 == 0 else fw.act,
                          (lambda e: e.tensor_copy(out=self.xTt(k, t, sub), in_=self.ps.t[pb][:, j * 128:(j + 1) * 128])) if j % 2 == 0 else
                          (lambda e: e.activation(out=self.xTt(k, t, sub), in_=self.ps.t[pb][:, j * 128:(j + 1) * 128], func=AF.Copy)),
                          [self.ps.bufs[pb]], [self.xT_b[k][t]])
                self.ps.free(pb)
            self.pg32.free(st, 2)

    def store_y(self, t, ybuf):
        fw = self.fw
        ident = self.cf32[:, 0:128]
        for sub in range(4):
            st = self.pg32.alloc(2)
            for q in range(2):
                pb = self.ps.alloc()
                for j in range(4):
                    k = q * 4 + j
                    fw.op(fw.pe, lambda e: e.transpose(self.ps.t[pb][:, j * 128:(j + 1) * 128], self.xTt(k, t, sub), ident),
                          [self.xT_b[k][t], self.const_b], [self.ps.bufs[pb]])
                dst = self.pg32.ap(st, 2)[:, q * 512:(q + 1) * 512]
                if q == 0:
                    fw.op(fw.dve, lambda e: e.tensor_copy(out=dst, in_=self.ps.t[pb][:, :]), [self.ps.bufs[pb]], self.pg32.b(st, 2))
                else:
                    fw.op(fw.act, lambda e: e.activation(out=dst, in_=self.ps.t[pb][:, :], func=AF.Copy), [self.ps.bufs[pb]], self.pg32.b(st, 2))
                self.ps.free(pb)
            r0 = t * T + sub * 128
            fw.dma(fw.sp, self.y[r0:r0 + 128, :], self.pg32.ap(st, 2), self.pg32.sems[st], reads=self.pg32.b(st, 2), writes=[ybuf], partial=True)
            self.pg32.free(st, 2)

    def aload(self, dst, src, pg0, reads, writes):
        self.fw.dma(self.fw.sp, dst, src, self.pg16.sems[pg0], reads=reads, writes=writes)

    def pload(self, dst, src, sem, reads, writes):
        self.fw.dma(self.fw.pool, dst, src, sem, reads=reads, writes=writes)

    def load_kT(self, L, s, unit, P, rows=None):
        cfg = self.cfg
        R = cfg.ranks[s]
        npr = cfg.NSEG // T
        npg = R * npr
        k0 = self.pg16.alloc(npg)
        src, gb = self.u_gat(L, s, unit)
        if rows is not None:
            src = src[:, rows[0]:rows[1], :]
        for r in range(R):
            self.pload(self.pg16.ap(k0 + r * npr, npr)[0:P, :], src[r], self.s_attn["k", r], [gb], self.pg16.b(k0 + r * npr, npr))
        return k0, npg

    def load_v(self, L, s, hv):
        cfg = self.cfg
        R, NC = cfg.ranks[s], cfg.NC
        src, gb, W = self.v_gat4(L, s, hv)
        npr = -(-(NC * W) // T)
        npg = R * npr
        v0 = self.pg16.alloc(npg)
        v4 = self.pg16.ap(v0, npg).rearrange("p (k x) -> p k x", k=R)[:, :, 0:NC * W].rearrange("p k (c e) -> p k c e", e=W)
        for r in range(R):
            self.pload(v4[:, r, :, :], src[:, r, :, :], self.s_attn["v", r], [gb], self.pg16.b(v0 + r * npr, npr))
        return v0, npg, v4

    def load_qT(self, s, r0, P, pad=False):
        cfg = self.cfg
        npg = cfg.NSEG // T
        q0 = self.pg16.alloc(npg)
        if pad:
            self.fw.op(self.fw.dve, lambda e: e.memset(self.pg16.ap(q0, npg)[64:128, :], 0.0), (), self.pg16.b(q0, npg))
        src = self.Qs.rearrange("(r n) -> r n", n=cfg.NT)[r0:r0 + P, s * cfg.NSEG:(s + 1) * cfg.NSEG]
        self.qpar ^= 1
        self.pload(self.pg16.ap(q0, npg)[0:P, :], src, self.s_attn["q", self.qpar], [self.Qs_b], self.pg16.b(q0, npg))
        return q0, npg

    def flash_run(self, iters, pair=False):
        pend = []
        self.deferred = []
        depth = 1 if pair else 3
        lag = 5 if pair else 9

        def step():
            i0, s0 = pend.pop(0)
            i0["post"](s0)
            for d in self.deferred:
                d[0] -= 1
            due = [d for d in self.deferred if d[0] <= 0]
            self.deferred = [d for d in self.deferred if d[0] > 0]
            for d in due:
                d[1]()

        self.defer_lag = lag
        for it in iters:
            sp_ = (self.ps.alloc(), self.ps.alloc()) if pair else self.ps.alloc()
            it["qk"](sp_)
            pend.append((it, sp_))
            if len(pend) > depth:
                step()
        while pend:
            step()
        while self.deferred:
            self.deferred.pop(0)[1]()

    def exp_to_page(self, sp_, scale):
        fw = self.fw
        pt = self.pg16.alloc()
        fw.op(fw.act, lambda e: e.activation(out=self.pg16.ap(pt), in_=self.psap(sp_), func=AF.Exp, scale=scale), [self.ps.bufs[sp_]], self.pg16.b(pt))
        self.ps.free(sp_)
        return pt

    def epi65(self, acc, s, orow, col0, sink_col=None):
        self.deferred.append([3, lambda: self._epi65_head(acc, s, orow, col0, sink_col)])

    def _epi65_head(self, acc, s, orow, col0, sink_col):
        fw = self.fw
        ob = self.pg32.alloc()
        ob_ap = self.pg32.ap(ob)
        fw.op(fw.dve, lambda e: e.tensor_copy(out=ob_ap[0:65, :], in_=self.psap(acc, 65)), [self.ps.bufs[acc]], self.pg32.b(ob))
        self.ps.free(acc)
        rl = self.pg32.alloc()
        rl_ap = self.pg32.ap(rl)[64:65, :]
        if sink_col is not None:
            fw.op(fw.act, lambda e: e.activation(out=rl_ap, in_=ob_ap[64:65, :], func=AF.Ln, bias=self.small[64:65, sink_col:sink_col + 1], scale=1.0),
                  self.pg32.b(ob) + [self.const_b], self.pg32.b(rl))
            fw.op(fw.act, lambda e: e.activation(out=rl_ap, in_=rl_ap, func=AF.Exp, scale=-1.0), self.pg32.b(rl), self.pg32.b(rl))
        else:
            fw.op(fw.dve, lambda e: e.reciprocal(out=rl_ap, in_=ob_ap[64:65, :]), self.pg32.b(ob), self.pg32.b(rl))

        def tail():
            rb = self.ps.alloc()
            self.mm(self.psap(rb, 64), self.cf32[64:65, 128:192], rl_ap, True, True, self.pg32.b(rl) + [self.const_b], [self.ps.bufs[rb]])
            self.pg32.free(rl)
            o = self.pg16.alloc()
            fw.op(fw.dve, lambda e: e.tensor_tensor(out=self.pg16.ap(o)[0:64, :], in0=ob_ap[0:64, :], in1=self.psap(rb, 64), op=ALU.mult),
                  self.pg32.b(ob) + [self.ps.bufs[rb]], self.pg16.b(o))
            self.ps.free(rb)
            self.pg32.free(ob)
            self.store(self.pg16.sems[o], self.o_rows(orow, 64, col0, T), self.pg16.ap(o)[0:64, :], self.pg16.b(o), self.Os_b)
            self.pg16.free(o)
        self.deferred.append([self.defer_lag, tail])

    def attn65(self, s, k_ap, kb, d, q_ap, qb_, v3, vb, nkc, scale, orow):
        cfg = self.cfg
        iters = []
        state = {}
        for qb in range(cfg.NSEG // T):
            for kc in range(nkc):
                def qk(sp_, qb=qb, kc=kc):
                    self.mm(self.psap(sp_), k_ap[0:d, kc * 128:(kc + 1) * 128], q_ap[0:d, qb * T:(qb + 1) * T], True, True, kb + qb_, [self.ps.bufs[sp_]])

                def post(sp_, qb=qb, kc=kc):
                    pt = self.exp_to_page(sp_, scale)
                    if kc == 0:
                        state["acc"] = self.ps.alloc()
                    acc = state["acc"]
                    self.mm(self.psap(acc, 65), v3[:, kc // cfg.NC, kc % cfg.NC, :], self.pg16.ap(pt), kc == 0, kc == nkc - 1, vb + self.pg16.b(pt), [self.ps.bufs[acc]])
                    self.pg16.free(pt)
                    if kc == nkc - 1:
                        self.epi65(acc, s, orow, s * cfg.NSEG + qb * T)
                iters.append({"qk": qk, "post": post})
        self.flash_run(iters)

    def attn_ev(self, s):
        fw = self.fw
        cfg = self.cfg
        L = 0
        NC, NSEG, R = cfg.NC, cfg.NSEG, cfg.ranks[s]
        self.pg16.reset()
        self.pg16.lo, self.pg32.lo = NPG16 - 5, 0
        mk = self.pg16.alloc(6)
        self.aload(self.pg16.ap(mk, 6), self.masks_d, mk, [], self.pg16.b(mk, 6))
        hoff = SM_HALO + (0 if s == 0 else 8)
        nkp = -(-((NC + 2) * 128) // T)
        nvp = -(-((NC + 2) * 65) // T)
        for kv in range(2):
            ke = self.pg16.alloc(nkp)
            ke_ap = self.pg16.ap(ke, nkp)[0:64, :]
            ke_full = self.pg16.ap(ke, nkp)
            keb = self.pg16.b(ke, nkp)
            fw.op(fw.dve, lambda e: e.memset(ke_full[64:128, :], 0.0), (), keb)
            ve = self.pg16.alloc(nvp)
            ve3 = self.pg16.ap(ve, nvp)[:, 0:(NC + 2) * 65].rearrange("p (c e) -> p c e", e=65)
            veb = self.pg16.b(ve, nvp)
            kl, klb = self.u_loc(L, s, ("KA",))
            self.aload(ke_ap[:, 128:128 + NSEG], kl[kv * 64:(kv + 1) * 64, :], ke, [klb], keb)
            vl, vlb = self.v_loc3(L, s, kv)
            self.aload(ve3[:, 1:NC + 1, :], vl, ve, [vlb], veb)
            kc_ = self.pg16.alloc(2)
            vc_ = self.pg16.alloc(2)
            vg, vgb, _ = self.v_gat4(L, s, kv)
            kg, kgb = self.u_gat(L, s, ("KA",))
            for side in range(2):
                kcand = self.pg16.ap(kc_ + side)[0:64, 0:R * 128].rearrange("p (k n) -> p k n", k=R)
                cols = slice(NSEG - 128, NSEG) if side == 0 else slice(0, 128)
                self.aload(kcand, kg[:, kv * 64:(kv + 1) * 64, cols].rearrange("k d n -> d k n"), kc_ + side, [kgb], self.pg16.b(kc_ + side))
                vcand = self.pg16.ap(vc_ + side)[:, 0:R * 65].rearrange("p (k e) -> p k e", k=R)
                self.aload(vcand, vg[:, :, NC - 1 if side == 0 else 0, :], vc_ + side, [vgb], self.pg16.b(vc_ + side))
                kdst = ke_ap[:, 0:128] if side == 0 else ke_ap[:, (NC + 1) * 128:(NC + 2) * 128]
                vdst = ve3[:, 0, :] if side == 0 else ve3[:, NC + 1, :]
                for r in range(R):
                    wcol = hoff + side * R + r
                    if r == 0:
                        fw.op(fw.dve, lambda e: e.tensor_scalar(out=kdst, in0=kcand[:, r, :], scalar1=self.sm(wcol, 64), scalar2=None, op0=ALU.mult),
                              self.pg16.b(kc_ + side) + [self.const_b], keb)
                        fw.op(fw.dve, lambda e: e.tensor_scalar(out=vdst, in0=vcand[:, r, :], scalar1=self.sm(wcol), scalar2=None, op0=ALU.mult),
                              self.pg16.b(vc_ + side) + [self.const_b], veb)
                    else:
                        fw.op(fw.dve, lambda e: e.scalar_tensor_tensor(out=kdst, in0=kcand[:, r, :], scalar=self.sm(wcol, 64), in1=kdst, op0=ALU.mult, op1=ALU.add),
                              self.pg16.b(kc_ + side) + [self.const_b] + keb, keb)
                        fw.op(fw.dve, lambda e: e.scalar_tensor_tensor(out=vdst, in0=vcand[:, r, :], scalar=self.sm(wcol), in1=vdst, op0=ALU.mult, op1=ALU.add),
                              self.pg16.b(vc_ + side) + [self.const_b] + veb, veb)
            self.pg16.free(kc_, 2)
            self.pg16.free(vc_, 2)
            for g in range(4):
                hq = kv * 4 + g
                q0, nq = self.load_qT(s, hq * 64, 64, pad=True)
                q_ap = self.pg16.ap(q0, nq)
                qbufs = self.pg16.b(q0, nq)
                iters = []
                state = {}
                for qb in range(NSEG // T):
                    for e_ in range(6):
                        ext = 4 * qb + e_

                        def qk(sp_, qb=qb, ext=ext):
                            self.mm(self.psap(sp_), ke_full[:, ext * 128:(ext + 1) * 128], q_ap[:, qb * T:(qb + 1) * T], True, True, keb + qbufs, [self.ps.bufs[sp_]])

                        def post(sp_, qb=qb, ext=ext, e_=e_, hq=hq):
                            pt = self.exp_to_page(sp_, 0.125)
                            fw.op(fw.dve, lambda e: e.tensor_tensor(out=self.pg16.ap(pt), in0=self.pg16.ap(pt), in1=self.pg16.ap(mk + e_), op=ALU.mult),
                                  self.pg16.b(pt) + self.pg16.b(mk + e_), self.pg16.b(pt))
                            if e_ == 0:
                                state["acc"] = self.ps.alloc()
                            acc = state["acc"]
                            self.mm(self.psap(acc, 65), ve3[:, ext, :], self.pg16.ap(pt), e_ == 0, e_ == 5, veb + self.pg16.b(pt), [self.ps.bufs[acc]])
                            self.pg16.free(pt)
                            if e_ == 5:
                                self.epi65(acc, s, hq * 64, s * NSEG + qb * T, sink_col=SM_SINK + hq)
                        iters.append({"qk": qk, "post": post})
                self.flash_run(iters)
                self.pg16.free(q0, nq)
            self.pg16.free(ke, nkp)
            self.pg16.free(ve, nvp)
        self.pg16.free(mk, 6)
        nkc = R * NC
        scale = 96.0 ** -0.5
        for h in range(8):
            k0, nk = self.load_kT(L, s, ('KB', h), 96)
            v0, nv, v3 = self.load_v(L, s, 2 + h)
            q0, nq = self.load_qT(s, 512 + h * 96, 96)
            self.attn65(s, self.pg16.ap(k0, nk), self.pg16.b(k0, nk), 96, self.pg16.ap(q0, nq), self.pg16.b(q0, nq), v3, self.pg16.b(v0, nv), nkc, scale, 512 + h * 64)
            self.pg16.free(k0, nk)
            self.pg16.free(v0, nv)
            self.pg16.free(q0, nq)

    def attn_od(self, s):
        fw = self.fw
        cfg = self.cfg
        L = 1
        NC, NSEG, R = cfg.NC, cfg.NSEG, cfg.ranks[s]
        nkc = R * NC
        self.pg16.reset()
        self.pg16.lo, self.pg32.lo = NPG16 - 5, 0
        ones = self.cbf[:, CB_ONES:CB_ONES + 128]
        for hd in range(4):
            k0, nk = self.load_kT(L, s, ('KC', hd), 128)
            v0, nv, v3 = self.load_v(L, s, hd)
            q0, nq = self.load_qT(s, hd * 128, 128)
            k_ap, q_ap = self.pg16.ap(k0, nk), self.pg16.ap(q0, nq)
            kb, qb_, vb = self.pg16.b(k0, nk), self.pg16.b(q0, nq), self.pg16.b(v0, nv)
            iters = []
            state = {}
            ones_f = self.cf32[:, 128:256]
            for qb in range(NSEG // T):
                for kc in range(nkc):
                    def qk(sp2, qb=qb, kc=kc):
                        for c in range(2):
                            self.mm(self.psap(sp2[c]), k_ap[c * 64:(c + 1) * 64, kc * 128:(kc + 1) * 128], q_ap[c * 64:(c + 1) * 64, qb * T:(qb + 1) * T],
                                    True, True, kb + qb_, [self.ps.bufs[sp2[c]]])

                    def post(sp2, qb=qb, kc=kc, hd=hd):
                        pts = [self.exp_to_page(sp2[c], 0.125) for c in range(2)]
                        if kc == 0:
                            state["o", 0] = self.ps.alloc()
                            state["o", 1] = self.ps.alloc()
                            state["l", 0] = self.ps.alloc()
                            state["lacc"] = self.pg32.alloc()
                        for c in range(2):
                            ao = state["o", c]
                            self.mm(self.psap(ao), v3[:, kc // NC, kc % NC, :], self.pg16.ap(pts[c]), kc == 0, kc == nkc - 1, vb + self.pg16.b(pts[c]), [self.ps.bufs[ao]])
                        al = state["l", 0]
                        self.mm(self.psap(al), ones, self.pg16.ap(pts[0]), kc == 0, kc == nkc - 1, [self.const_b] + self.pg16.b(pts[0]), [self.ps.bufs[al]])
                        la = state["lacc"]
                        if kc == 0:
                            fw.op(fw.dve, lambda e: e.tensor_copy(out=self.pg32.ap(la), in_=self.pg16.ap(pts[1])), self.pg16.b(pts[1]), self.pg32.b(la))
                        else:
                            fw.op(fw.dve, lambda e: e.tensor_tensor(out=self.pg32.ap(la), in0=self.pg32.ap(la), in1=self.pg16.ap(pts[1]), op=ALU.add),
                                  self.pg16.b(pts[1]) + self.pg32.b(la), self.pg32.b(la))
                        for c in range(2):
                            self.pg16.free(pts[c])
                        if kc == nkc - 1:
                            l1 = self.ps.alloc()
                            self.mm(self.psap(l1), ones_f, self.pg32.ap(la), True, True, [self.const_b] + self.pg32.b(la), [self.ps.bufs[l1]])
                            self.pg32.free(la)
                            state["l", 1] = l1
                            self.epi_diff(state, s, hd, s * NSEG + qb * T)
                    iters.append({"qk": qk, "post": post})
            self.flash_run(iters, pair=True)
            self.pg16.free(k0, nk)
            self.pg16.free(v0, nv)
            self.pg16.free(q0, nq)
        for kv in range(2):
            nk = nkc * 128 // T
            k0 = self.pg16.alloc(nk)
            self.pg16.free(k0, nk)
            fw.op(fw.dve, lambda e: e.memset(self.pg16.ap(k0, nk)[64:128, :], 0.0), (), self.pg16.b(k0, nk))
            k0b, nkb = self.load_kT(L, s, ("KD",), 64, rows=(kv * 64, (kv + 1) * 64))
            assert k0b == k0 and nkb == nk
            v0, nv, v3 = self.load_v(L, s, 4 + kv)
            qn = self.load_qT(s, 512 + (kv * 4) * 64, 64, pad=True)
            for g in range(4):
                hq = kv * 4 + g
                q0, nq = qn
                if g + 1 < 4:
                    qn = self.load_qT(s, 512 + (hq + 1) * 64, 64, pad=True)
                self.attn65(s, self.pg16.ap(k0, nk), self.pg16.b(k0, nk), 128, self.pg16.ap(q0, nq), self.pg16.b(q0, nq), v3, self.pg16.b(v0, nv), nkc, 0.125, 512 + hq * 64)
                self.pg16.free(q0, nq)
            self.pg16.free(k0, nk)
            self.pg16.free(v0, nv)

    def run_debug(self, stage):
        cfg = self.cfg
        ybuf = Buf("y")
        for t in range(cfg.NTILE):
            self.load_x(t)
            if stage >= 1:
                self.ffn(t, 0, 0)
            if stage >= 2:
                self.inproj_ev(t)
                if (t + 1) % cfg.TPS == 0 and stage >= 3:
                    self.allgather(0, t // cfg.TPS)
        if stage >= 4:
            for s in range(2):
                self.attn_ev(s)
        if stage >= 5:
            for t in range(cfg.NTILE):
                self.outproj(t, "w_ev_out")
        for t in range(cfg.NTILE):
            self.store_y(t, ybuf)
        self.fw.finish([ybuf])

    def epi_diff(self, state, s, hd, col0):
        fw = self.fw
        on = []
        for c in range(2):
            ao, al = state["o", c], state["l", c]
            rl = self.pg32.alloc()
            fw.op(fw.act, lambda e: e.activation(out=self.pg32.ap(rl), in_=self.psap(al), func=AF.Ln), [self.ps.bufs[al]], self.pg32.b(rl))
            self.ps.free(al)
            fw.op(fw.act, lambda e: e.activation(out=self.pg32.ap(rl), in_=self.pg32.ap(rl), func=AF.Exp, scale=-1.0), self.pg32.b(rl), self.pg32.b(rl))
            o_ = self.pg32.alloc()
            fw.op(fw.dve, lambda e: e.tensor_tensor(out=self.pg32.ap(o_), in0=self.psap(ao), in1=self.pg32.ap(rl), op=ALU.mult),
                  [self.ps.bufs[ao]] + self.pg32.b(rl), self.pg32.b(o_))
            self.ps.free(ao)
            self.pg32.free(rl)
            on.append(o_)
        d_ap = self.pg32.ap(on[0])
        fw.op(fw.dve, lambda e: e.scalar_tensor_tensor(out=d_ap, in0=self.pg32.ap(on[1]), scalar=self.neglam, in1=d_ap, op0=ALU.mult, op1=ALU.add),
              self.pg32.b(on[0]) + self.pg32.b(on[1]) + [self.const_b], self.pg32.b(on[0]))
        self.pg32.free(on[1])

        def tail():
            r = self.rstd_of([(d_ap, self.pg32.b(on[0]))], 1.0 / 128, self.cbf[:, CB_ONES:CB_ONES + 128], 128)
            o = self.pg16.alloc()
            fw.op(fw.dve, lambda e: e.scalar_tensor_tensor(out=self.pg16.ap(o), in0=d_ap, scalar=self.sm(SM_GCO), in1=self.pg32.ap(r), op0=ALU.mult, op1=ALU.mult),
                  self.pg32.b(on[0]) + self.pg32.b(r) + [self.const_b], self.pg16.b(o))
            self.pg32.free(r)
            self.pg32.free(on[0])
            self.store(self.pg16.sems[o], self.o_rows(hd * 128, 128, col0, T), self.pg16.ap(o), self.pg16.b(o), self.Os_b)
            self.pg16.free(o)
        self.deferred.append([self.defer_lag, tail])

    def allgather(self, L, s):
        cfg = self.cfg
        R = cfg.ranks[s]
        groups = [[g * R + i for i in range(R)] for g in range(8 // R)]
        for j in range(self.nchunk[L]):
            src, dst = self.loc[L, s, j], self.gat[L, s, j]
            self.fw.async1(self.fw.pool, lambda e: e.collective_compute("AllGather", ALU.bypass, replica_groups=groups, ins=[src], outs=[dst]),
                           self.s_cc[L, s], reads=[self.loc_b[L, s, j]], writes=[self.gat_b[L, s, j]])

    def run(self):
        import os
        stage = int(os.environ.get("KSTAGE", "99"))
        cfg = self.cfg
        if stage == -2:
            self.fw.dma(self.fw.sp, self.cf32[:, :], self.cf32_d, self.s_const2, writes=[self.const_b])
            return self.run_debug(0)
        self.load_consts()
        if stage == -1:
            return self.fw.finish([])
        if stage < 99:
            return self.run_debug(stage)
        self.pg16.lo, self.pg32.lo = 8, 2
        self.load_x(0)
        for t in range(cfg.NTILE):
            self.ffn(t, 0, 0, mid_hook=(lambda t=t: self.load_x(t + 1)) if t + 1 < cfg.NTILE else None)
            self.inproj_ev(t)
            if (t + 1) % cfg.TPS == 0:
                self.allgather(0, t // cfg.TPS)
        for s in range(2):
            self.attn_ev(s)
        self.pg16.lo, self.pg32.lo = 8, 2
        for t in range(cfg.NTILE):
            self.outproj(t, "w_ev_out")
            self.ffn(t, 1, 2)
            self.ffn(t, 2, 3)
            self.inproj_od(t)
            if (t + 1) % cfg.TPS == 0:
                self.allgather(1, t // cfg.TPS)
        for s in range(2):
            self.attn_od(s)
        ybuf = Buf("y")
        self.pg16.lo, self.pg32.lo = 8, 2
        for t in range(cfg.NTILE):
            self.outproj(t, "w_od_out")
            self.ffn(t, 3, 5)
            self.store_y(t, ybuf)
        self.fw.finish([ybuf])


_PROG_CACHE = {}


def build_program(nseg):
    cfg = Cfg(nseg)
    nc0 = bass.Bass("TRN2", target_bir_lowering=False)
    with contextlib.ExitStack() as es0:
        k0 = Kern(nc0, es0, cfg)
        k0.fw.dry = True
        k0.run()
        plan = list(k0.wplan)
    nc = bass.Bass("TRN2", target_bir_lowering=False)
    with contextlib.ExitStack() as es:
        k = Kern(nc, es, cfg, dry_plan=plan)
        k.run()
        assert k.wi == len(plan)
        print('instructions:', k.fw.ninst, 'sems:', k.fw.nsem)
    return nc, cfg


def _f32(a):
    return np.ascontiguousarray(np.asarray(a, dtype=np.float32))


def make_in_maps(cfg, inp):
    NSEG = cfg.NSEG
    g = {k: np.asarray(v, dtype=np.float32) for k, v in inp.items()}
    shared = {}
    w_in = np.zeros((4, NJ, 128, 2048), np.float32)
    w_out = np.zeros((4, 8, 128, 2816), np.float32)
    for f, (nm, l) in enumerate((("ffn1", 0), ("ffn2", 0), ("ffn1", 1), ("ffn2", 1))):
        Wi = g[nm + "_w_in"][l]
        Wo = g[nm + "_w_out"][l]
        for j in range(NJ):
            cols = np.concatenate([np.arange(j * 128, (j + 1) * 128), DFF + np.arange(j * 128, (j + 1) * 128)])
            w_in[f, j] = _kmajor(Wi[:, cols])
        for m in range(8):
            w_out[f, m] = _kmajor(Wo[:, m * 128:(m + 1) * 128])
    shared["w_ffn_in"] = w_in
    shared["w_ffn_out"] = w_out
    Wev = g["ev_w_in"][0]
    shared["w_ev_in_a"] = np.stack([_kmajor(Wev[:, i * 256:(i + 1) * 256]) for i in range(5)])
    shared["w_ev_in_b"] = _kmajor(Wev[:, 1280:1568])[None]
    shared["w_uq"] = _kmajor(g["b_w_uq"][0])[None]
    Wkv = g["b_w_ukv"][0].reshape(2, 128, 8, 128)
    wkp = np.zeros((128, 2, 8, 96), np.float32)
    wkp[:, :, :, 0:64] = Wkv[:, :, :, 0:64].transpose(1, 0, 2, 3)
    wvp = np.ascontiguousarray(Wkv[:, :, :, 64:128].transpose(1, 0, 2, 3)).reshape(128, 1024)
    shared["w_ukv"] = np.concatenate([wkp.reshape(128, 1536), wvp], axis=1)[None]
    shared["w_ev_out"] = np.stack([_kmajor(g["ev_w_out"][0][:, i * 256:(i + 1) * 256]) for i in range(4)])
    Wod = g["od_w_in"][0]
    shared["w_od_in"] = np.stack([_kmajor(Wod[:, i * 256:(i + 1) * 256]) for i in range(9)])
    shared["w_od_out"] = np.stack([_kmajor(g["od_w_out"][0][:, i * 256:(i + 1) * 256]) for i in range(4)])
    shared["cbf"] = _const_bf()
    cf = np.zeros((128, 256), np.float32)
    cf[:, 0:128] = np.eye(128, dtype=np.float32)
    cf[:, 128:256] = 1.0
    shared["cf32"] = cf
    import ml_dtypes
    shared["masks"] = _masks().astype(ml_dtypes.bfloat16)
    small = np.zeros((128, NSM), np.float32)
    for i, v in enumerate((g["ffn1_norm"][0], g["ev_norm"][0], g["ffn2_norm"][0], g["ffn1_norm"][1], g["od_norm"][0], g["ffn2_norm"][1])):
        small[:, SM_GD + i * 8: SM_GD + (i + 1) * 8] = v.reshape(8, 128).T
    for i, nm in enumerate(("a_q_norm", "a_k_norm", "c_q_norm", "c_k_norm", "d_q_norm", "d_k_norm")):
        small[:, SM_GH + i] = _tile2(g[nm][0])
    small[0:96, SM_GB + 0] = g["b_q_norm"][0]
    small[0:96, SM_GB + 1] = g["b_k_norm"][0]
    small[:, SM_GC:SM_GC + 4] = g["b_cq_norm"][0].reshape(4, 128).T
    small[:, SM_GC + 4:SM_GC + 6] = g["b_ckv_norm"][0].reshape(2, 128).T
    small[:, SM_GCO] = g["c_out_norm"][0]
    small[:, SM_SINK:SM_SINK + 8] = g["a_sink"][0][None, :]
    small[:, SM_LAM:SM_LAM + 256] = g["c_lambda"][0].reshape(1, 256)
    small[:, SM_EPS] = EPS
    maps = []
    for c in range(8):
        m = dict(shared)
        qi, hi = c % 4, c % 2
        xin = np.concatenate([g["x_prompt"][c // 4, qi * NSEG:(qi + 1) * NSEG], g["x_sample"][c // 2, hi * NSEG:(hi + 1) * NSEG]], axis=0)
        m["xin"] = _f32(xin)
        pos = np.concatenate([qi * NSEG + np.arange(NSEG), hi * NSEG + np.arange(NSEG)])
        m["rope"] = _rope_tables(pos)
        sm = small.copy()
        for r in range(4):
            sm[:, SM_HALO + r] = 1.0 if r == qi - 1 else 0.0
            sm[:, SM_HALO + 4 + r] = 1.0 if r == qi + 1 else 0.0
        for r in range(2):
            sm[:, SM_HALO + 8 + r] = 1.0 if r == hi - 1 else 0.0
            sm[:, SM_HALO + 10 + r] = 1.0 if r == hi + 1 else 0.0
        m["small"] = sm
        import os
        if os.environ.get("KTINYW") == "1":
            for k in list(m):
                if k.startswith("w_"):
                    a = m[k]
                    m[k] = a.reshape((-1,) + a.shape[-2:])[0:1].reshape((1,) * (a.ndim - 2) + a.shape[-2:])
        maps.append({k: (v if k == "masks" else _f32(v)) for k, v in m.items()})
    return maps


def run_kernel(inp, nseg):
    if nseg not in _PROG_CACHE:
        _PROG_CACHE[nseg] = build_program(nseg)
    nc, cfg = _PROG_CACHE[nseg]
    maps = make_in_maps(cfg, inp)
    res = run_bass_kernel_spmd(nc, maps, core_ids=list(range(8)))
    NSEG = cfg.NSEG
    yp = np.zeros((2, 4 * NSEG, D_MODEL), np.float32)
    ys = np.zeros((4, 2 * NSEG, D_MODEL), np.float32)
    for c in range(8):
        y = np.asarray(res.results[c]["y"], dtype=np.float32)
        yp[c // 4, (c % 4) * NSEG:(c % 4 + 1) * NSEG] = y[0:NSEG]
        ys[c // 2, (c % 2) * NSEG:(c % 2 + 1) * NSEG] = y[NSEG:2 * NSEG]
    return yp, ys


def kernel(**inputs):
    return run_kernel(inputs, 2048)
```

```python
import contextlib
import math
import numpy as np
import concourse.bass as bass
import concourse.mybir as mybir
from concourse.bass_utils import run_bass_kernel_spmd

F32 = mybir.dt.float32
BF16 = mybir.dt.bfloat16
AF = mybir.ActivationFunctionType
ALU = mybir.AluOpType

D_MODEL = 1024
KD = 8
DFF = 2816
NJ = 22
T = 512
EPS = 1e-6
THETA = 10000.0
GRID_W = 64
WINDOW = 128
LAM_INIT_L1 = 0.8 - 0.6 * math.exp(-0.3 * 1)

WSLOT = 3072
NWSLOT = 3
NPG16 = 41
NPG32 = 8


class Sem:
    __slots__ = ("h", "v", "dma", "name")

    def __init__(self, h, dma, name):
        self.h = h
        self.v = 0
        self.dma = dma
        self.name = name


class Buf:
    __slots__ = ("w", "r", "name", "excl")

    def __init__(self, name="", excl=False):
        self.excl = excl
        self.w = {}
        self.r = {}
        self.name = name


class Eng:
    def __init__(self, name, e, sem, is_pe=False):
        self.name = name
        self.e = e
        self.sem = sem
        self.is_pe = is_pe
        self.seen = {}


class FW:
    def __init__(self, nc, es):
        self.nc = nc
        self.es = es
        self.dry = False
        self.nsem = 0
        self.pe = Eng("pe", nc.tensor, self.new_sem("e_pe"), is_pe=True)
        self.act = Eng("act", nc.scalar, self.new_sem("e_act"))
        self.dve = Eng("dve", nc.vector, self.new_sem("e_dve"))
        self.pool = Eng("pool", nc.gpsimd, self.new_sem("e_pool"))
        self.sp = Eng("sp", nc.sync, self.new_sem("e_sp"))
        self.engs = [self.pe, self.act, self.dve, self.pool, self.sp]
        self.dma_sems = []
        self.ninst = 0

    def new_sem(self, name, dma=None):
        h = self.es.enter_context(self.nc.semaphore(name))
        self.nsem += 1
        s = Sem(h, dma, name)
        if dma:
            self.dma_sems.append(s)
        return s

    def _wait(self, eng, tok, raw):
        s, v = tok
        if s is eng.sem and eng.is_pe:
            return
        if s.dma:
            v = s.v
        if eng.seen.get(s, 0) >= v:
            return
        eng.e.wait_ge(s.h, v)
        eng.seen[s] = v

    def _deps(self, eng, reads, writes):
        for b in reads:
            for s, v in b.w.items():
                self._wait(eng, (s, v), True)
            if b.excl:
                for s, v in b.r.items():
                    if s is not eng.sem:
                        self._wait(eng, (s, v), False)
        for b in writes:
            for s, v in b.w.items():
                self._wait(eng, (s, v), False)
            for s, v in b.r.items():
                self._wait(eng, (s, v), False)

    def _commit(self, tok, reads, writes, partial=False):
        s, v = tok
        for b in reads:
            if b.r.get(s, 0) < v:
                b.r[s] = v
        for b in writes:
            if partial:
                if b.w.get(s, 0) < v:
                    b.w[s] = v
            else:
                b.w = {s: v}
            b.r = {}

    def op(self, eng, fn, reads=(), writes=()):
        if self.dry:
            return
        self._deps(eng, reads, writes)
        ins = fn(eng.e)
        eng.sem.v += 1
        ins.then_inc(eng.sem.h, 1)
        self.ninst += 1
        self._commit((eng.sem, eng.sem.v), reads, writes)

    def dma(self, q, out, in_, sem, reads=(), writes=(), partial=False):
        if self.dry:
            return
        assert sem.dma in ("hw", "sw") and (sem.dma == "sw") == (q is self.pool), (sem.name, q.name)
        self._deps(q, reads, writes)
        ins = q.e.dma_start(out=out, in_=in_)
        sem.v += 16
        ins.then_inc(sem.h, 16)
        self.ninst += 1
        self._commit((sem, sem.v), reads, writes, partial)

    def async1(self, q, fn, sem, reads=(), writes=()):
        if self.dry:
            return
        self._deps(q, reads, writes)
        ins = fn(q.e)
        sem.v += 1
        ins.then_inc(sem.h, 1)
        self._commit((sem, sem.v), reads, writes)

    def finish(self, bufs):
        if self.dry:
            return
        for b in bufs:
            for s, v in b.w.items():
                self._wait(self.sp, (s, v), True)
        for e in self.engs:
            if e is not self.sp and e.sem.v > 0:
                self._wait(self.sp, (e.sem, e.sem.v), True)
        for s in self.dma_sems:
            if s.v > 0:
                self._wait(self.sp, (s, s.v), True)


class PagePool:
    def __init__(self, tensor, n, width, name, fw):
        self.sems = [fw.new_sem(f"pg{name}{i}", dma="hw") for i in range(n)]
        self.t = tensor
        self.n = n
        self.width = width
        self.free_ = [True] * n
        self.bufs = [Buf(f"{name}{i}") for i in range(n)]
        self.name = name
        self.rot = 0
        self.lo = 0

    def alloc(self, k=1):
        n = self.n
        if k == 1:
            for off in range(1, n + 1):
                s = (self.rot - off) % n
                if self.free_[s] and s >= self.lo:
                    self.free_[s] = False
                    self.rot = s
                    return s
            for s in range(n - 1, -1, -1):
                if self.free_[s]:
                    self.free_[s] = False
                    self.rot = s
                    return s
        else:
            for s in range(0, n - k + 1):
                if all(self.free_[s:s + k]):
                    for i in range(s, s + k):
                        self.free_[i] = False
                    return s
        raise RuntimeError(f"page pool {self.name} exhausted (want {k}, free {sum(self.free_)})")

    def alloc_n(self, n):
        return [self.alloc() for _ in range(n)]

    def free_n(self, lst):
        for p in lst:
            self.free(p)

    def free(self, s, k=1):
        for i in range(s, s + k):
            assert not self.free_[i]
            self.free_[i] = True

    def reset(self):
        assert all(self.free_), f"pool {self.name} not empty at reset"

    def ap(self, s, k=1):
        return self.t[:, s * self.width:(s + k) * self.width]

    def b(self, s, k=1):
        return self.bufs[s:s + k]


class BankPool:
    def __init__(self, tensors):
        self.t = tensors
        self.bufs = [Buf(f"ps{i}", excl=True) for i in range(len(tensors))]
        self.freeq = list(range(len(tensors)))

    def alloc(self):
        if not self.freeq:
            raise RuntimeError("PSUM banks exhausted")
        return self.freeq.pop(0)

    def free(self, i):
        assert i not in self.freeq
        self.freeq.append(i)


def _kmajor(W):
    kin, n = W.shape
    return np.ascontiguousarray(W.reshape(kin // 128, 128, n).transpose(1, 0, 2)).reshape(128, -1)


def _tile2(v64):
    return np.concatenate([v64, v64], axis=0)


def _rope_tables(pos):
    pos = np.asarray(pos)
    ntok = pos.shape[0]

    def angles(p, dim):
        inv = (np.float32(THETA) ** (-(np.arange(0, dim, 2, dtype=np.float32) / np.float32(dim)))).astype(np.float32)
        ang = p.astype(np.float32)[:, None] * inv[None, :]
        return np.cos(ang).astype(np.float32), np.sin(ang).astype(np.float32)

    c64, s64 = angles(pos, 64)
    c32, s32 = angles(pos, 32)
    cr, sr = angles(pos // GRID_W, 32)
    cc, sc = angles(pos % GRID_W, 32)
    out = np.zeros((6, 128, ntok), np.float32)
    cf = np.concatenate([c64, c64], axis=1).T
    sf = np.concatenate([-s64, s64], axis=1).T
    out[0] = np.concatenate([cf, cf], axis=0)
    out[1] = np.concatenate([sf, sf], axis=0)
    out[2, :64] = 1.0
    out[2, 64:96] = np.concatenate([c32, c32], axis=1).T
    out[3, 64:96] = np.concatenate([-s32, s32], axis=1).T
    ca = np.concatenate([cr, cr, cc, cc], axis=1).T
    sa = np.concatenate([-sr, sr, -sc, sc], axis=1).T
    out[4] = np.concatenate([ca, ca], axis=0)
    out[5] = np.concatenate([sa, sa], axis=0)
    return out


def _const_bf():
    c = np.zeros((128, 7 * 128), np.float32)
    c[:, 0:128] = 1.0
    for h in range(2):
        c[h * 64:(h + 1) * 64, 128 + h * 64:128 + (h + 1) * 64] = 1.0
    c[0:96, 256:256 + 96] = 1.0
    idx = np.arange(128)
    sw = (idx // 64) * 64 + ((idx % 64) + 32) % 64
    c[sw, 384 + idx] = 1.0
    sa = (idx // 32) * 32 + ((idx % 32) + 16) % 32
    c[sa, 512 + idx] = 1.0
    m = np.arange(64, 96)
    sm = 64 + ((m - 64) + 16) % 32
    c[sm, 640 + m] = 1.0
    c[np.arange(32), 768 + 64 + np.arange(32)] = 1.0
    return c


def _masks():
    k = np.arange(128)[:, None]
    q = np.arange(512)[None, :]
    m = np.zeros((128, 6, 512), np.float32)
    for i, d in enumerate(range(-1, 5)):
        m[:, i, :] = (np.abs(d * 128 + k - q) <= WINDOW).astype(np.float32)
    return m.reshape(128, 6 * 512)


SM_GD = 0
SM_GH = 48
SM_GB = 54
SM_GC = 56
SM_GCO = 62
SM_SINK = 63
SM_LAM = 71
SM_HALO = 327
SM_EPS = 339
NSM = 340

CB_ONES, CB_BLK64, CB_ONES96, CB_SWF, CB_SWA, CB_SWM, CB_KRSEL = [i * 128 for i in range(7)]


class Cfg:
    def __init__(self, nseg=2048):
        self.NSEG = nseg
        self.NT = 2 * nseg
        self.NTILE = self.NT // T
        self.TPS = nseg // T
        self.NC = nseg // 128
        self.ranks = (4, 2)
        self.KR = (896, 640)
        self.R = (896 + 650, 640 + 642)


class Kern:
    def __init__(self, nc, es, cfg, dry_plan=None):
        self.nc = nc
        self.cfg = cfg
        self.fw = FW(nc, es)
        fw = self.fw
        NT, NSEG, NC = cfg.NT, cfg.NSEG, cfg.NC

        import os
        tiny = os.environ.get("KTINYW") == "1"

        def din(name, shape):
            if tiny and name.startswith("w_"):
                shape = [1] * (len(shape) - 2) + list(shape[-2:])
            return nc.dram_tensor(name, list(shape), F32, kind="ExternalInput").ap()

        self.xin = din("xin", [NT, 1024])
        self.rope = din("rope", [6, 128, NT])
        self.small_d = din("small", [128, NSM])
        self.cbf_d = din("cbf", [128, 896])
        self.cf32_d = din("cf32", [128, 256])
        self.masks_d = nc.dram_tensor("masks", [128, 3072], BF16, kind="ExternalInput").ap()
        self.w_ffn_in = din("w_ffn_in", [4, NJ, 128, 2048])
        self.w_ffn_out = din("w_ffn_out", [4, 8, 128, 2816])
        self.w_ev_in_a = din("w_ev_in_a", [5, 128, 2048])
        self.w_ev_in_b = din("w_ev_in_b", [1, 128, 2304])
        self.w_uq = din("w_uq", [1, 128, 3072])
        self.w_ukv = din("w_ukv", [1, 128, 2560])
        self.w_ev_out = din("w_ev_out", [4, 128, 2048])
        self.w_od_in = din("w_od_in", [9, 128, 2048])
        self.w_od_out = din("w_od_out", [4, 128, 2048])
        self.y = nc.dram_tensor("y", [NT, 1024], F32, kind="ExternalOutput").ap()

        def dint(name, n):
            return nc.dram_tensor(name, [n], BF16, kind="Internal").ap()

        self.Qs = dint("Qs", 1280 * NT)
        self.Os = dint("Os", 1024 * NT)
        self.Qs_b = Buf("Qs")
        self.Os_b = Buf("Os")
        ulist = {0: [(("KA",), 128)] + [(("KB", h), 96) for h in range(8)] + [(("V", hv), 65) for hv in range(10)],
                 1: [(("KC", h), 128) for h in range(4)] + [(("KD",), 128)] + [(("V", hv), 128) for hv in range(4)] + [(("V", 4), 65), (("V", 5), 65)]}
        self.units = {}
        self.nchunk = {}
        chunk_rows = {}
        for L in range(2):
            j, used = 0, 0
            for name, n in ulist[L]:
                if used + n > 256:
                    chunk_rows[L, j] = used
                    j, used = j + 1, 0
                self.units[L, name] = (j, used, n)
                used += n
            chunk_rows[L, j] = used
            self.nchunk[L] = j + 1
        self.loc = {}
        self.gat = {}
        self.loc_b = {}
        self.gat_b = {}
        for L in range(2):
            for s in range(2):
                for j in range(self.nchunk[L]):
                    rows = chunk_rows[L, j]
                    self.loc[L, s, j] = nc.dram_tensor(f"loc{L}{s}_{j}", [rows, NSEG], BF16, kind="Internal").ap()
                    self.gat[L, s, j] = nc.dram_tensor(f"gat{L}{s}_{j}", [cfg.ranks[s] * rows, NSEG], BF16, kind="Internal").ap()
                    self.loc_b[L, s, j] = Buf(f"loc{L}{s}_{j}")
                    self.gat_b[L, s, j] = Buf(f"gat{L}{s}_{j}")

        def sb(name, shape, dt):
            return es.enter_context(nc.sbuf_tensor("sb_" + name, list(shape), dt))

        self.xT = sb("xT", [128, KD * NT], F32)
        self.xT_b = [[Buf(f"x{k}_{t}") for t in range(cfg.NTILE)] for k in range(KD)]
        self.small = sb("small", [128, NSM], F32)
        self.cbf = sb("cbf", [128, 896], BF16)
        self.cf32 = sb("cf32", [128, 256], F32)
        self.const_b = Buf("consts")
        self.wsl = sb("wsl", [128, NWSLOT * WSLOT], BF16)
        self.wsl_b = [Buf(f"wsl{i}") for i in range(NWSLOT)]
        self.wsl_sem = [fw.new_sem(f"wsl{i}", dma="sw") for i in range(NWSLOT)]
        p16 = sb("pg16", [128, NPG16 * T], BF16)
        p32 = sb("pg32", [128, NPG32 * T], F32)
        self.pg16 = PagePool(p16, NPG16, T, "h", fw)
        self.pg32 = PagePool(p32, NPG32, T, "f", fw)
        banks = [es.enter_context(nc.psum_tensor(f"psb{i}", [128, T], F32)) for i in range(8)]
        self.ps = BankPool(banks)
        self.s_cc = {(L, s_): fw.new_sem(f"cc{L}{s_}", dma="cc") for L in range(2) for s_ in range(2)}
        self.s_attn = {(k_, r_): fw.new_sem(f"at{k_}{r_}", dma="sw") for k_ in ("k", "v") for r_ in range(4)}
        self.s_attn["q", 0] = fw.new_sem("atq0", dma="sw")
        self.s_attn["q", 1] = fw.new_sem("atq1", dma="sw")
        self.qpar = 0
        self.s_const = fw.new_sem("const", dma="hw")
        self.s_const2 = fw.new_sem("const2", dma="hw")
        self.s_constp = fw.new_sem("constp", dma="sw")
        self.plan = dry_plan
        self.wplan = []
        self.wi = 0
        self.wloaded = 0

    def xTt(self, k, t, sub=None):
        NT = self.cfg.NT
        if sub is None:
            return self.xT[:, k * NT + t * T: k * NT + (t + 1) * T]
        return self.xT[:, k * NT + t * T + sub * 128: k * NT + t * T + (sub + 1) * 128]

    def psap(self, i, p=128, n=T):
        return self.ps.t[i][0:p, 0:n]

    def mm(self, out, lhsT, rhs, start, stop, reads, writes):
        self.fw.op(self.fw.pe, lambda e: e.matmul(out, lhsT=lhsT, rhs=rhs, start=start, stop=stop), reads, writes)

    def wget(self, key, nel):
        fw = self.fw
        if fw.dry:
            self.wplan.append((key, nel))
            return self.wsl[:, 0:nel], [self.wsl_b[0]]
        i = self.wi
        plan = self.plan
        assert plan[i][1] == nel
        while self.wloaded < min(len(plan), i + NWSLOT):
            g = self.wloaded
            s = g % NWSLOT
            kk, ne = plan[g]
            ap_d = getattr(self, kk[0])
            for ix in kk[1:]:
                ap_d = ap_d[ix]
            fw.dma(fw.pool, self.wsl[:, s * WSLOT: s * WSLOT + ne], ap_d, self.wsl_sem[s], reads=(), writes=[self.wsl_b[s]])
            self.wloaded += 1
        self.wi += 1
        s = i % NWSLOT
        return self.wsl[:, s * WSLOT: s * WSLOT + nel], [self.wsl_b[s]]

    def sm(self, col, p=128, n=1):
        return self.small[0:p, col:col + n]

    def load_consts(self):
        fw = self.fw
        fw.dma(fw.sp, self.small[:, :], self.small_d, self.s_const, writes=[self.const_b])
        fw.dma(fw.sp, self.cf32[:, :], self.cf32_d, self.s_const2, writes=[self.const_b])
        fw.dma(fw.pool, self.cbf[:, :], self.cbf_d, self.s_constp, writes=[self.const_b])
        cb = [self.const_b]
        sm = self.small
        fw.op(fw.act, lambda e: e.activation(out=sm[:, SM_SINK:SM_SINK + 8], in_=sm[:, SM_SINK:SM_SINK + 8], func=AF.Exp), cb, cb)
        lam = sm[:, SM_LAM:SM_LAM + 256]
        fw.op(fw.dve, lambda e: e.tensor_tensor(out=lam[:, 0:64], in0=lam[:, 0:64], in1=lam[:, 64:128], op=ALU.mult), cb, cb)
        fw.op(fw.dve, lambda e: e.tensor_tensor(out=lam[:, 128:192], in0=lam[:, 128:192], in1=lam[:, 192:256], op=ALU.mult), cb, cb)
        fw.op(fw.dve, lambda e: e.reduce_sum(out=lam[:, 64:65], in_=lam[:, 0:64], axis=mybir.AxisListType.X), cb, cb)
        fw.op(fw.dve, lambda e: e.reduce_sum(out=lam[:, 65:66], in_=lam[:, 128:192], axis=mybir.AxisListType.X), cb, cb)
        fw.op(fw.act, lambda e: e.activation(out=lam[:, 64:66], in_=lam[:, 64:66], func=AF.Exp), cb, cb)
        fw.op(fw.dve, lambda e: e.tensor_tensor(out=lam[:, 66:67], in0=lam[:, 65:66], in1=lam[:, 64:65], op=ALU.subtract), cb, cb)
        fw.op(fw.dve, lambda e: e.tensor_scalar(out=lam[:, 66:67], in0=lam[:, 66:67], scalar1=-LAM_INIT_L1, scalar2=None, op0=ALU.add), cb, cb)
        fw.op(fw.dve, lambda e: e.tensor_scalar(out=sm[:, SM_GCO:SM_GCO + 1], in0=sm[:, SM_GCO:SM_GCO + 1], scalar1=1.0 - LAM_INIT_L1, scalar2=None, op0=ALU.mult), cb, cb)
        self.neglam = lam[:, 66:67]

    def rstd_of(self, chunks, inv_n, ones_ap, P):
        fw = self.fw
        ss = self.ps.alloc()
        n = len(chunks)
        for i, (src, sbufs) in enumerate(chunks):
            sq = self.pg16.alloc()
            sq_ap = self.pg16.ap(sq)[0:P, :]
            fw.op(fw.act, lambda e: e.activation(out=sq_ap, in_=src, func=AF.Square), sbufs, self.pg16.b(sq))
            self.mm(self.psap(ss, P), ones_ap, sq_ap, i == 0, i == n - 1, self.pg16.b(sq) + [self.const_b], [self.ps.bufs[ss]])
            self.pg16.free(sq)
        r = self.pg32.alloc()
        r_ap = self.pg32.ap(r)[0:P, :]
        fw.op(fw.act, lambda e: e.activation(out=r_ap, in_=self.psap(ss, P), func=AF.Ln, bias=self.sm(SM_EPS, P), scale=inv_n),
              [self.ps.bufs[ss], self.const_b], self.pg32.b(r))
        self.ps.free(ss)
        fw.op(fw.act, lambda e: e.activation(out=r_ap, in_=r_ap, func=AF.Exp, scale=-0.5), self.pg32.b(r), self.pg32.b(r))
        return r

    def norm_dmodel(self, t, gi):
        fw = self.fw
        chunks = [(self.xTt(k, t), [self.xT_b[k][t]]) for k in range(KD)]
        r = self.rstd_of(chunks, 1.0 / D_MODEL, self.cbf[:, CB_ONES:CB_ONES + 128], 128)
        h0 = self.pg16.alloc_n(KD)
        for k in range(KD):
            g = self.sm(SM_GD + gi * 8 + k)
            o = self.pg16.ap(h0[k])
            fw.op(fw.dve, lambda e: e.scalar_tensor_tensor(out=o, in0=self.xTt(k, t), scalar=g, in1=self.pg32.ap(r), op0=ALU.mult, op1=ALU.mult),
                  [self.xT_b[k][t], self.const_b] + self.pg32.b(r), self.pg16.b(h0[k]))
        self.pg32.free(r)
        return h0

    def ffn(self, t, f, gi, mid_hook=None, h0=None):
        fw = self.fw
        if h0 is None:
            h0 = self.norm_dmodel(t, gi)
        a0 = self.pg16.alloc_n(NJ)
        for j in range(NJ):
            w, wb = self.wget(("w_ffn_in", f, j), 2048)
            w3 = w.rearrange("p (k n) -> p k n", n=256)
            pg_ = self.ps.alloc()
            pu_ = self.ps.alloc()
            for half, pb in ((0, pg_), (1, pu_)):
                for k in range(KD):
                    self.mm(self.psap(pb), w3[:, k, half * 128:(half + 1) * 128], self.pg16.ap(h0[k]), k == 0, k == KD - 1,
                            wb + self.pg16.b(h0[k]), [self.ps.bufs[pb]])
            s = self.pg32.alloc()
            fw.op(fw.act, lambda e: e.activation(out=self.pg32.ap(s), in_=self.psap(pg_), func=AF.Silu), [self.ps.bufs[pg_]], self.pg32.b(s))
            self.ps.free(pg_)
            fw.op(fw.dve, lambda e: e.tensor_tensor(out=self.pg16.ap(a0[j]), in0=self.psap(pu_), in1=self.pg32.ap(s), op=ALU.mult),
                  [self.ps.bufs[pu_]] + self.pg32.b(s), self.pg16.b(a0[j]))
            self.ps.free(pu_)
            self.pg32.free(s)
        self.pg16.free_n(h0)
        if mid_hook is not None:
            mid_hook()
        for m in range(KD):
            w, wb = self.wget(("w_ffn_out", f, m), 2816)
            w3 = w.rearrange("p (j n) -> p j n", n=128)
            acc = self.ps.alloc()
            for j in range(NJ):
                self.mm(self.psap(acc), w3[:, j, :], self.pg16.ap(a0[j]), j == 0, j == NJ - 1, wb + self.pg16.b(a0[j]), [self.ps.bufs[acc]])
            xk = self.xTt(m, t)
            fw.op(fw.dve, lambda e: e.scalar_tensor_tensor(out=xk, in0=self.psap(acc), scalar=0.5, in1=xk, op0=ALU.mult, op1=ALU.add),
                  [self.ps.bufs[acc], self.xT_b[m][t]], [self.xT_b[m][t]])
            self.ps.free(acc)
        self.pg16.free_n(a0)

    def unit_a(self, zp, P, ones_ap, inv_n, swap_ap, cos_ap, sin_ap, gain_ap, rope_bufs):
        fw = self.fw
        zb = [self.ps.bufs[zp]]
        z = self.psap(zp, P)
        sq = self.pg16.alloc()
        sq_ap = self.pg16.ap(sq)[0:P, :]
        fw.op(fw.act, lambda e: e.activation(out=sq_ap, in_=z, func=AF.Square), zb, self.pg16.b(sq))
        xg = self.pg16.alloc()
        xg_ap = self.pg16.ap(xg)[0:P, :]
        fw.op(fw.act, lambda e: e.activation(out=xg_ap, in_=z, func=AF.Copy, scale=gain_ap), zb + [self.const_b], self.pg16.b(xg))
        ss = self.ps.alloc()
        self.mm(self.psap(ss, P), ones_ap, sq_ap, True, True, self.pg16.b(sq) + [self.const_b], [self.ps.bufs[ss]])
        self.pg16.free(sq)
        rot = self.ps.alloc()
        self.mm(self.psap(rot, P), swap_ap, xg_ap, True, True, self.pg16.b(xg) + [self.const_b], [self.ps.bufs[rot]])
        self.pg16.free(xg)
        r = self.pg32.alloc()
        r_ap = self.pg32.ap(r)[0:P, :]
        fw.op(fw.act, lambda e: e.activation(out=r_ap, in_=self.psap(ss, P), func=AF.Ln, bias=self.sm(SM_EPS, P), scale=inv_n),
              [self.ps.bufs[ss], self.const_b], self.pg32.b(r))
        self.ps.free(ss)
        fw.op(fw.act, lambda e: e.activation(out=r_ap, in_=r_ap, func=AF.Exp, scale=-0.5), self.pg32.b(r), self.pg32.b(r))
        return (zp, P, r, rot, cos_ap, sin_ap, gain_ap, rope_bufs)

    def unit_b(self, ctx):
        fw = self.fw
        zp, P, r, rot, cos_ap, sin_ap, gain_ap, rope_bufs = ctx
        zb = [self.ps.bufs[zp]]
        z = self.psap(zp, P)
        t1 = self.pg32.alloc()
        t1_ap = self.pg32.ap(t1)[0:P, :]
        fw.op(fw.dve, lambda e: e.scalar_tensor_tensor(out=t1_ap, in0=z, scalar=gain_ap, in1=cos_ap, op0=ALU.mult, op1=ALU.mult),
              zb + [self.const_b] + rope_bufs, self.pg32.b(t1))
        self.ps.free(zp)
        t2 = self.pg32.alloc()
        t2_ap = self.pg32.ap(t2)[0:P, :]
        fw.op(fw.dve, lambda e: e.tensor_tensor(out=t2_ap, in0=self.psap(rot, P), in1=sin_ap, op=ALU.mult),
              [self.ps.bufs[rot]] + rope_bufs, self.pg32.b(t2))
        self.ps.free(rot)
        fw.op(fw.dve, lambda e: e.tensor_tensor(out=t1_ap, in0=t1_ap, in1=t2_ap, op=ALU.add), self.pg32.b(t1) + self.pg32.b(t2), self.pg32.b(t1))
        self.pg32.free(t2)
        o = self.pg16.alloc()
        o_ap = self.pg16.ap(o)[0:P, :]
        fw.op(fw.dve, lambda e: e.tensor_tensor(out=o_ap, in0=t1_ap, in1=self.pg32.ap(r)[0:P, :], op=ALU.mult),
              self.pg32.b(t1) + self.pg32.b(r), self.pg16.b(o))
        self.pg32.free(t1)
        self.pg32.free(r)
        return o

    def unit(self, *args):
        return self.unit_b(self.unit_a(*args))

    def proj_fm(self, w3, wb, col0, M, h0, nk):
        pb = self.ps.alloc()
        for k in range(nk):
            self.mm(self.psap(pb, M), w3[:, k, col0:col0 + M], self.pg16.ap(h0[k]), k == 0, k == nk - 1,
                    wb + self.pg16.b(h0[k]), [self.ps.bufs[pb]])
        return pb

    def q_rows(self, r0, P, t):
        NT = self.cfg.NT
        return self.Qs.rearrange("(r n) -> r n", n=NT)[r0:r0 + P, t * T:(t + 1) * T]

    def o_rows(self, r0, P, c0, n):
        NT = self.cfg.NT
        return self.Os.rearrange("(r n) -> r n", n=NT)[r0:r0 + P, c0:c0 + n]

    def u_loc(self, L, s, unit):
        j, off, n = self.units[L, unit]
        return self.loc[L, s, j][off:off + n, :], self.loc_b[L, s, j]

    def u_gat(self, L, s, unit):
        j, off, n = self.units[L, unit]
        R = self.cfg.ranks[s]
        return self.gat[L, s, j].rearrange("(k r) n -> k r n", k=R)[:, off:off + n, :], self.gat_b[L, s, j]

    def k_loc(self, L, s, unit, ti):
        ap, b = self.u_loc(L, s, unit)
        return ap[:, ti * T:(ti + 1) * T], b

    def v_width(self, L, hv):
        return 128 if (L == 1 and hv < 4) else 65

    def v_loc3(self, L, s, hv):
        W = self.v_width(L, hv)
        ap, b = self.u_loc(L, s, ("V", hv))
        return ap.rearrange("r n -> (r n)").rearrange("(p c e) -> p c e", p=128, e=W), b

    def v_gat4(self, L, s, hv):
        W = self.v_width(L, hv)
        ap, b = self.u_gat(L, s, ("V", hv))
        return ap.rearrange("k r n -> k (r n)").rearrange("k (p c e) -> p k c e", p=128, e=W), b, W

    def store_v(self, L, s, ti, v0, vb, tile4, hv0):
        for i in range(tile4.shape[1]):
            dst, b = self.v_loc3(L, s, hv0 + i)
            self.store(self.pg16.sems[v0], dst[:, ti * 4:(ti + 1) * 4, :], tile4[:, i, :, :], vb, b)

    def store(self, sem, dst, src, src_bufs, dst_buf):
        self.fw.dma(self.fw.sp, dst, src, sem, reads=src_bufs, writes=[dst_buf], partial=True)

    def seg_of(self, t):
        return t // self.cfg.TPS, t % self.cfg.TPS

    def load_rope(self, t, variants):
        fw = self.fw
        res = {}
        for i, v in enumerate(variants):
            c = self.pg32.alloc()
            s = self.pg32.alloc()
            fw.dma(fw.sp, self.pg32.ap(c), self.rope[2 * v, :, t * T:(t + 1) * T], self.pg32.sems[c], writes=self.pg32.b(c))
            fw.dma(fw.sp, self.pg32.ap(s), self.rope[2 * v + 1, :, t * T:(t + 1) * T], self.pg32.sems[s], writes=self.pg32.b(s))
            res[v] = (c, s)
        return res

    def free_rope(self, rp):
        for c, s in rp.values():
            self.pg32.free(c)
            self.pg32.free(s)

    def unit_v(self, zp, variant, rp, gain_col, P=128, phase_a_only=False):
        ones = {0: CB_BLK64, 1: CB_ONES96, 2: CB_BLK64}[variant]
        swp = {0: CB_SWF, 1: CB_SWM, 2: CB_SWA}[variant]
        inv_n = {0: 1.0 / 64, 1: 1.0 / 96, 2: 1.0 / 64}[variant]
        c, s = rp[variant]
        fn = self.unit_a if phase_a_only else self.unit
        return fn(zp, P, self.cbf[0:P, ones:ones + P], inv_n, self.cbf[0:P, swp:swp + P],
                  self.pg32.ap(c)[0:P, :], self.pg32.ap(s)[0:P, :], self.sm(gain_col, P), self.pg32.b(c) + self.pg32.b(s))

    def vtile_alloc(self, L):
        fw = self.fw
        v0 = self.pg16.alloc(6)
        flat = self.pg16.ap(v0, 6)
        vb = self.pg16.b(v0, 6)
        if L == 0:
            v65 = flat[:, 0:10 * 4 * 65].rearrange("p (h c e) -> p h c e", h=10, e=65)
            fw.op(fw.dve, lambda e: e.memset(v65[:, :, :, 64:65], 1.0), (), vb)
            return v0, vb, v65, None
        v128 = flat[:, 0:4 * 4 * 128].rearrange("p (h c e) -> p h c e", h=4, e=128)
        v65 = flat[:, 2048:2048 + 2 * 4 * 65].rearrange("p (h c e) -> p h c e", h=2, e=65)
        fw.op(fw.dve, lambda e: e.memset(v65[:, :, :, 64:65], 1.0), (), vb)
        return v0, vb, v65, v128

    def v_tokmajor(self, lhs_pages, nk, rhs_fn, ncols, rb):
        banks = []
        if ncols <= 128:
            pb = self.ps.alloc()
            for sub in range(4):
                for k in range(nk):
                    self.mm(self.ps.t[pb][:, sub * ncols:(sub + 1) * ncols], self.pg16.ap(lhs_pages[k])[:, sub * 128:(sub + 1) * 128], rhs_fn(k),
                            k == 0, k == nk - 1, rb + self.pg16.b(lhs_pages[k]), [self.ps.bufs[pb]])
            return [pb]
        for sub in range(4):
            pb = self.ps.alloc()
            for k in range(nk):
                self.mm(self.ps.t[pb][:, 0:ncols], self.pg16.ap(lhs_pages[k])[:, sub * 128:(sub + 1) * 128], rhs_fn(k),
                        k == 0, k == nk - 1, rb + self.pg16.b(lhs_pages[k]), [self.ps.bufs[pb]])
            banks.append(pb)
        return banks

    def run_units(self, items):
        n = len(items)
        zps, ctxs = {}, {}
        for step in range(n + 2):
            if step < n:
                zps[step] = items[step][0]()
            i = step - 1
            if 0 <= i < n:
                ctxs[i] = self.unit_v(zps.pop(i), phase_a_only=True, **items[i][1])
            i = step - 2
            if 0 <= i < n:
                o = self.unit_b(ctxs.pop(i))
                items[i][2](o)
                self.pg16.free(o)

    def inproj_ev(self, t):
        fw = self.fw
        s, ti = self.seg_of(t)
        L = 0
        h0 = self.norm_dmodel(t, 1)
        rp = self.load_rope(t, (0, 1))
        v0, vb, v65, _ = self.vtile_alloc(0)
        items = []
        wst = {}
        for g in range(2):
            for c in range(2):
                def proj(g=g, c=c):
                    if c == 0:
                        w, wb = self.wget(("w_ev_in_a", g), 2048)
                        wst[g] = (w.rearrange("p (k n) -> p k n", n=256), wb)
                    return self.proj_fm(wst[g][0], wst[g][1], c * 128, 128, h0, KD)

                def st(o, g=g, c=c):
                    self.store(self.pg16.sems[o], self.q_rows((g * 2 + c) * 128, 128, t), self.pg16.ap(o), self.pg16.b(o), self.Qs_b)
                items.append((proj, dict(variant=0, rp=rp, gain_col=SM_GH + 0), st))
        self.run_units(items)
        w, wb = self.wget(("w_ev_in_a", 2), 2048)
        w3 = w.rearrange("p (k n) -> p k n", n=256)
        zp = self.proj_fm(w3, wb, 0, 128, h0, KD)
        o = self.unit_v(zp, 0, rp, SM_GH + 1)
        self.store(self.pg16.sems[o], self.k_loc(L, s, ('KA',), ti)[0], self.pg16.ap(o), self.pg16.b(o), self.k_loc(L, s, ('KA',), ti)[1])
        self.pg16.free(o)
        (pb,) = self.v_tokmajor(h0, KD, lambda k: w3[:, k, 128:256], 128, wb)
        src = self.ps.t[pb][:, 0:512].rearrange("p (c h e) -> p h c e", c=4, h=2)
        fw.op(fw.dve, lambda e: e.tensor_copy(out=v65[:, 0:2, :, 0:64], in_=src), [self.ps.bufs[pb]], vb)
        self.ps.free(pb)
        cq = []
        for g in (3, 4):
            w, wb = self.wget(("w_ev_in_a", g), 2048)
            w3 = w.rearrange("p (k n) -> p k n", n=256)
            for c in range(2):
                cq.append(self.proj_fm(w3, wb, c * 128, 128, h0, KD))
        r = self.rstd_of([(self.psap(b), [self.ps.bufs[b]]) for b in cq], 1.0 / 512, self.cbf[:, CB_ONES:CB_ONES + 128], 128)
        cqn = self.pg16.alloc_n(4)
        for c in range(4):
            fw.op(fw.dve, lambda e: e.scalar_tensor_tensor(out=self.pg16.ap(cqn[c]), in0=self.psap(cq[c]), scalar=self.sm(SM_GC + c),
                                                            in1=self.pg32.ap(r), op0=ALU.mult, op1=ALU.mult),
                  [self.ps.bufs[cq[c]], self.const_b] + self.pg32.b(r), self.pg16.b(cqn[c]))
            self.ps.free(cq[c])
        self.pg32.free(r)
        w, wb = self.wget(("w_ev_in_b", 0), 2304)
        w3 = w.rearrange("p (k n) -> p k n", n=288)
        ck = [self.proj_fm(w3, wb, c * 128, 128, h0, KD) for c in range(2)]
        krp = self.proj_fm(w3, wb, 256, 32, h0, KD)
        self.pg16.free_n(h0)
        kr = self.pg16.alloc()
        fw.op(fw.act, lambda e: e.activation(out=self.pg16.ap(kr)[0:32, :], in_=self.psap(krp, 32), func=AF.Copy), [self.ps.bufs[krp]], self.pg16.b(kr))
        self.ps.free(krp)
        r = self.rstd_of([(self.psap(b), [self.ps.bufs[b]]) for b in ck], 1.0 / 256, self.cbf[:, CB_ONES:CB_ONES + 128], 128)
        ckn = self.pg16.alloc_n(2)
        for c in range(2):
            fw.op(fw.dve, lambda e: e.scalar_tensor_tensor(out=self.pg16.ap(ckn[c]), in0=self.psap(ck[c]), scalar=self.sm(SM_GC + 4 + c),
                                                            in1=self.pg32.ap(r), op0=ALU.mult, op1=ALU.mult),
                  [self.ps.bufs[ck[c]], self.const_b] + self.pg32.b(r), self.pg16.b(ckn[c]))
            self.ps.free(ck[c])
        self.pg32.free(r)
        w, wb = self.wget(("w_uq", 0), 3072)
        w3 = w.rearrange("p (k n) -> p k n", n=768)
        items = []
        for h in range(8):
            def proj(h=h, w3=w3, wb=wb):
                return self.proj_fm(w3, wb, h * 96, 96, cqn, 4)

            def st(o, h=h):
                self.store(self.pg16.sems[o], self.q_rows(512 + h * 96, 96, t), self.pg16.ap(o)[0:96, :], self.pg16.b(o), self.Qs_b)
            items.append((proj, dict(variant=1, rp=rp, gain_col=SM_GB + 0, P=96), st))
        self.run_units(items)
        self.pg16.free_n(cqn)
        w, wb = self.wget(("w_ukv", 0), 2560)
        wk = w[:, 0:1536].rearrange("p (k h e) -> p k h e", k=2, h=8)
        items = []
        for h in range(8):
            def proj(h=h, wk=wk, wb=wb):
                zp = self.ps.alloc()
                self.mm(self.psap(zp, 96), self.cbf[0:32, CB_KRSEL:CB_KRSEL + 96], self.pg16.ap(kr)[0:32, :], True, False,
                        self.pg16.b(kr) + [self.const_b], [self.ps.bufs[zp]])
                for c in range(2):
                    self.mm(self.psap(zp, 96), wk[:, c, h, :], self.pg16.ap(ckn[c]), False, c == 1,
                            wb + self.pg16.b(ckn[c]), [self.ps.bufs[zp]])
                return zp

            def st(o, h=h):
                self.store(self.pg16.sems[o], self.k_loc(L, s, ('KB', h), ti)[0], self.pg16.ap(o)[0:96, :], self.pg16.b(o), self.k_loc(L, s, ('KB', h), ti)[1])
            items.append((proj, dict(variant=1, rp=rp, gain_col=SM_GB + 1, P=96), st))
        self.run_units(items)
        self.pg16.free(kr)
        wv = w[:, 1536:2560].rearrange("p (k n) -> p k n", k=2)
        banks = self.v_tokmajor(ckn, 2, lambda k: wv[:, k, :], 512, wb)
        for sub, pb in enumerate(banks):
            src = self.ps.t[pb][:, 0:512].rearrange("p (h e) -> p h e", h=8)
            fw.op(fw.dve, lambda e: e.tensor_copy(out=v65[:, 2:10, sub, 0:64], in_=src), [self.ps.bufs[pb]], vb)
            self.ps.free(pb)
        self.pg16.free_n(ckn)
        self.free_rope(rp)
        self.store_v(L, s, ti, v0, vb, v65, 0)
        self.pg16.free(v0, 6)

    def inproj_od(self, t):
        fw = self.fw
        s, ti = self.seg_of(t)
        L = 1
        h0 = self.norm_dmodel(t, 4)
        rp = self.load_rope(t, (0, 2))
        v0, vb, v65, v128 = self.vtile_alloc(1)
        items = []
        wst = {}
        for g in range(4):
            for c in range(2):
                def proj(g=g, c=c):
                    if c == 0:
                        w, wb = self.wget(("w_od_in", g), 2048)
                        wst[g] = (w.rearrange("p (k n) -> p k n", n=256), wb)
                    return self.proj_fm(wst[g][0], wst[g][1], c * 128, 128, h0, KD)

                def st(o, g=g, c=c):
                    if g < 2:
                        self.store(self.pg16.sems[o], self.q_rows((g * 2 + c) * 128, 128, t), self.pg16.ap(o), self.pg16.b(o), self.Qs_b)
                    else:
                        self.store(self.pg16.sems[o], self.k_loc(L, s, ('KC', (g - 2) * 2 + c), ti)[0], self.pg16.ap(o), self.pg16.b(o), self.k_loc(L, s, ('KC', (g - 2) * 2 + c), ti)[1])
                items.append((proj, dict(variant=0, rp=rp, gain_col=SM_GH + (2 if g < 2 else 3)), st))
        self.run_units(items)
        for g in (4, 5):
            w, wb = self.wget(("w_od_in", g), 2048)
            w3 = w.rearrange("p (k n) -> p k n", n=256)
            banks = self.v_tokmajor(h0, KD, lambda k: w3[:, k, :], 256, wb)
            for sub, pb in enumerate(banks):
                src = self.ps.t[pb][:, 0:256].rearrange("p (h e) -> p h e", h=2)
                fw.op(fw.dve, lambda e: e.tensor_copy(out=v128[:, 2 * (g - 4):2 * (g - 4) + 2, sub, :], in_=src), [self.ps.bufs[pb]], vb)
                self.ps.free(pb)
        items = []
        wst = {}
        for g in (6, 7):
            for c in range(2):
                def proj(g=g, c=c):
                    if c == 0:
                        w, wb = self.wget(("w_od_in", g), 2048)
                        wst[g] = (w.rearrange("p (k n) -> p k n", n=256), wb)
                    return self.proj_fm(wst[g][0], wst[g][1], c * 128, 128, h0, KD)

                def st(o, g=g, c=c):
                    self.store(self.pg16.sems[o], self.q_rows(512 + ((g - 6) * 2 + c) * 128, 128, t), self.pg16.ap(o), self.pg16.b(o), self.Qs_b)
                items.append((proj, dict(variant=2, rp=rp, gain_col=SM_GH + 4), st))
        self.run_units(items)
        w, wb = self.wget(("w_od_in", 8), 2048)
        w3 = w.rearrange("p (k n) -> p k n", n=256)
        zp = self.proj_fm(w3, wb, 0, 128, h0, KD)
        o = self.unit_v(zp, 2, rp, SM_GH + 5)
        self.store(self.pg16.sems[o], self.k_loc(L, s, ('KD',), ti)[0], self.pg16.ap(o), self.pg16.b(o), self.k_loc(L, s, ('KD',), ti)[1])
        self.pg16.free(o)
        (pb,) = self.v_tokmajor(h0, KD, lambda k: w3[:, k, 128:256], 128, wb)
        src = self.ps.t[pb][:, 0:512].rearrange("p (c h e) -> p h c e", c=4, h=2)
        fw.op(fw.dve, lambda e: e.tensor_copy(out=v65[:, 0:2, :, 0:64], in_=src), [self.ps.bufs[pb]], vb)
        self.ps.free(pb)
        self.pg16.free_n(h0)
        self.free_rope(rp)
        self.store_v(L, s, ti, v0, vb, v128, 0)
        self.store_v(L, s, ti, v0, vb, v65, 4)
        self.pg16.free(v0, 6)

    def outproj(self, t, wname):
        fw = self.fw
        NT = self.cfg.NT
        o0 = self.pg16.alloc(KD)
        src = self.Os.rearrange("(k p n) -> p k n", p=128, n=NT)[:, :, t * T:(t + 1) * T]
        dst = self.pg16.ap(o0, KD).rearrange("p (k n) -> p k n", n=T)
        fw.dma(fw.sp, dst, src, self.pg16.sems[o0], reads=[self.Os_b], writes=self.pg16.b(o0, KD))
        for g in range(4):
            w, wb = self.wget((wname, g), 2048)
            w3 = w.rearrange("p (k n) -> p k n", n=256)
            for c in range(2):
                m = g * 2 + c
                acc = self.proj_fm(w3, wb, c * 128, 128, list(range(o0, o0 + KD)), KD)
                xk = self.xTt(m, t)
                fw.op(fw.dve, lambda e: e.tensor_tensor(out=xk, in0=self.psap(acc), in1=xk, op=ALU.add),
                      [self.ps.bufs[acc], self.xT_b[m][t]], [self.xT_b[m][t]])
                self.ps.free(acc)
        self.pg16.free(o0, KD)

    def load_x(self, t):
        fw = self.fw
        ident = self.cf32[:, 0:128]
        for sub in range(4):
            st = self.pg32.alloc(2)
            r0 = t * T + sub * 128
            fw.dma(fw.sp, self.pg32.ap(st, 2), self.xin[r0:r0 + 128, :], self.pg32.sems[st], writes=self.pg32.b(st, 2))
            for q in range(2):
                pb = self.ps.alloc()
                for j in range(4):
                    k = q * 4 + j
                    fw.op(fw.pe, lambda e: e.transpose(self.ps.t[pb][:, j * 128:(j + 1) * 128], self.pg32.ap(st, 2)[:, k * 128:(k + 1) * 128], ident),
                          self.pg32.b(st, 2) + [self.const_b], [self.ps.bufs[pb]])
                for j in range(4):
                    k = q * 4 + j
                    fw.op(fw.dve if j % 2 == 0 else fw.act,
                          (lambda e: e.tensor_copy(out=self.xTt(k, t, sub), in_=self.ps.t[pb][:, j * 128:(j + 1) * 128])) if j % 2 == 0 else
                          (lambda e: e.activation(out=self.xTt(k, t, sub), in_=self.ps.t[pb][:, j * 128:(j + 1) * 128], func=AF.Copy)),
                          [self.ps.bufs[pb]], [self.xT_b[k][t]])
                self.ps.free(pb)
            self.pg32.free(st, 2)

    def store_y(self, t, ybuf):
        fw = self.fw
        ident = self.cf32[:, 0:128]
        for sub in range(4):
            st = self.pg32.alloc(2)
            for q in range(2):
                pb = self.ps.alloc()
                for j in range(4):
                    k = q * 4 + j
                    fw.op(fw.pe, lambda e: e.transpose(self.ps.t[pb][:, j * 128:(j + 1) * 128], self.xTt(k, t, sub), ident),
                          [self.xT_b[k][t], self.const_b], [self.ps.bufs[pb]])
                dst = self.pg32.ap(st, 2)[:, q * 512:(q + 1) * 512]
                if q == 0:
                    fw.op(fw.dve, lambda e: e.tensor_copy(out=dst, in_=self.ps.t[pb][:, :]), [self.ps.bufs[pb]], self.pg32.b(st, 2))
                else:
                    fw.op(fw.act, lambda e: e.activation(out=dst, in_=self.ps.t[pb][:, :], func=AF.Copy), [self.ps.bufs[pb]], self.pg32.b(st, 2))
                self.ps.free(pb)
            r0 = t * T + sub * 128
            fw.dma(fw.sp, self.y[r0:r0 + 128, :], self.pg32.ap(st, 2), self.pg32.sems[st], reads=self.pg32.b(st, 2), writes=[ybuf], partial=True)
            self.pg32.free(st, 2)

    def aload(self, dst, src, pg0, reads, writes):
        self.fw.dma(self.fw.sp, dst, src, self.pg16.sems[pg0], reads=reads, writes=writes)

    def pload(self, dst, src, sem, reads, writes):
        self.fw.dma(self.fw.pool, dst, src, sem, reads=reads, writes=writes)

    def load_kT(self, L, s, unit, P, rows=None):
        cfg = self.cfg
        R = cfg.ranks[s]
        npr = cfg.NSEG // T
        npg = R * npr
        k0 = self.pg16.alloc(npg)
        src, gb = self.u_gat(L, s, unit)
        if rows is not None:
            src = src[:, rows[0]:rows[1], :]
        for r in range(R):
            self.pload(self.pg16.ap(k0 + r * npr, npr)[0:P, :], src[r], self.s_attn["k", r], [gb], self.pg16.b(k0 + r * npr, npr))
        return k0, npg

    def load_v(self, L, s, hv):
        cfg = self.cfg
        R, NC = cfg.ranks[s], cfg.NC
        src, gb, W = self.v_gat4(L, s, hv)
        npr = -(-(NC * W) // T)
        npg = R * npr
        v0 = self.pg16.alloc(npg)
        v4 = self.pg16.ap(v0, npg).rearrange("p (k x) -> p k x", k=R)[:, :, 0:NC * W].rearrange("p k (c e) -> p k c e", e=W)
        for r in range(R):
            self.pload(v4[:, r, :, :], src[:, r, :, :], self.s_attn["v", r], [gb], self.pg16.b(v0 + r * npr, npr))
        return v0, npg, v4

    def load_qT(self, s, r0, P, pad=False):
        cfg = self.cfg
        npg = cfg.NSEG // T
        q0 = self.pg16.alloc(npg)
        if pad:
            self.fw.op(self.fw.dve, lambda e: e.memset(self.pg16.ap(q0, npg)[64:128, :], 0.0), (), self.pg16.b(q0, npg))
        src = self.Qs.rearrange("(r n) -> r n", n=cfg.NT)[r0:r0 + P, s * cfg.NSEG:(s + 1) * cfg.NSEG]
        self.qpar ^= 1
        self.pload(self.pg16.ap(q0, npg)[0:P, :], src, self.s_attn["q", self.qpar], [self.Qs_b], self.pg16.b(q0, npg))
        return q0, npg

    def flash_run(self, iters, pair=False):
        pend = []
        self.deferred = []
        depth = 1 if pair else 3
        lag = 5 if pair else 9

        def step():
            i0, s0 = pend.pop(0)
            i0["post"](s0)
            for d in self.deferred:
                d[0] -= 1
            due = [d for d in self.deferred if d[0] <= 0]
            self.deferred = [d for d in self.deferred if d[0] > 0]
            for d in due:
                d[1]()

        self.defer_lag = lag
        for it in iters:
            sp_ = (self.ps.alloc(), self.ps.alloc()) if pair else self.ps.alloc()
            it["qk"](sp_)
            pend.append((it, sp_))
            if len(pend) > depth:
                step()
        while pend:
            step()
        while self.deferred:
            self.deferred.pop(0)[1]()

    def exp_to_page(self, sp_, scale):
        fw = self.fw
        pt = self.pg16.alloc()
        fw.op(fw.act, lambda e: e.activation(out=self.pg16.ap(pt), in_=self.psap(sp_), func=AF.Exp, scale=scale), [self.ps.bufs[sp_]], self.pg16.b(pt))
        self.ps.free(sp_)
        return pt

    def epi65(self, acc, s, orow, col0, sink_col=None):
        self.deferred.append([3, lambda: self._epi65_head(acc, s, orow, col0, sink_col)])

    def _epi65_head(self, acc, s, orow, col0, sink_col):
        fw = self.fw
        ob = self.pg32.alloc()
        ob_ap = self.pg32.ap(ob)
        fw.op(fw.dve, lambda e: e.tensor_copy(out=ob_ap[0:65, :], in_=self.psap(acc, 65)), [self.ps.bufs[acc]], self.pg32.b(ob))
        self.ps.free(acc)
        rl = self.pg32.alloc()
        rl_ap = self.pg32.ap(rl)[64:65, :]
        if sink_col is not None:
            fw.op(fw.act, lambda e: e.activation(out=rl_ap, in_=ob_ap[64:65, :], func=AF.Ln, bias=self.small[64:65, sink_col:sink_col + 1], scale=1.0),
                  self.pg32.b(ob) + [self.const_b], self.pg32.b(rl))
            fw.op(fw.act, lambda e: e.activation(out=rl_ap, in_=rl_ap, func=AF.Exp, scale=-1.0), self.pg32.b(rl), self.pg32.b(rl))
        else:
            fw.op(fw.dve, lambda e: e.reciprocal(out=rl_ap, in_=ob_ap[64:65, :]), self.pg32.b(ob), self.pg32.b(rl))

        def tail():
            rb = self.ps.alloc()
            self.mm(self.psap(rb, 64), self.cf32[64:65, 128:192], rl_ap, True, True, self.pg32.b(rl) + [self.const_b], [self.ps.bufs[rb]])
            self.pg32.free(rl)
            o = self.pg16.alloc()
            fw.op(fw.dve, lambda e: e.tensor_tensor(out=self.pg16.ap(o)[0:64, :], in0=ob_ap[0:64, :], in1=self.psap(rb, 64), op=ALU.mult),
                  self.pg32.b(ob) + [self.ps.bufs[rb]], self.pg16.b(o))
            self.ps.free(rb)
            self.pg32.free(ob)
            self.store(self.pg16.sems[o], self.o_rows(orow, 64, col0, T), self.pg16.ap(o)[0:64, :], self.pg16.b(o), self.Os_b)
            self.pg16.free(o)
        self.deferred.append([self.defer_lag, tail])

    def attn65(self, s, k_ap, kb, d, q_ap, qb_, v3, vb, nkc, scale, orow):
        cfg = self.cfg
        iters = []
        state = {}
        for qb in range(cfg.NSEG // T):
            for kc in range(nkc):
                def qk(sp_, qb=qb, kc=kc):
                    self.mm(self.psap(sp_), k_ap[0:d, kc * 128:(kc + 1) * 128], q_ap[0:d, qb * T:(qb + 1) * T], True, True, kb + qb_, [self.ps.bufs[sp_]])

                def post(sp_, qb=qb, kc=kc):
                    pt = self.exp_to_page(sp_, scale)
                    if kc == 0:
                        state["acc"] = self.ps.alloc()
                    acc = state["acc"]
                    self.mm(self.psap(acc, 65), v3[:, kc // cfg.NC, kc % cfg.NC, :], self.pg16.ap(pt), kc == 0, kc == nkc - 1, vb + self.pg16.b(pt), [self.ps.bufs[acc]])
                    self.pg16.free(pt)
                    if kc == nkc - 1:
                        self.epi65(acc, s, orow, s * cfg.NSEG + qb * T)
                iters.append({"qk": qk, "post": post})
        self.flash_run(iters)

    def attn_ev(self, s):
        fw = self.fw
        cfg = self.cfg
        L = 0
        NC, NSEG, R = cfg.NC, cfg.NSEG, cfg.ranks[s]
        self.pg16.reset()
        self.pg16.lo, self.pg32.lo = NPG16 - 5, 0
        mk = self.pg16.alloc(6)
        self.aload(self.pg16.ap(mk, 6), self.masks_d, mk, [], self.pg16.b(mk, 6))
        hoff = SM_HALO + (0 if s == 0 else 8)
        nkp = -(-((NC + 2) * 128) // T)
        nvp = -(-((NC + 2) * 65) // T)
        for kv in range(2):
            ke = self.pg16.alloc(nkp)
            ke_ap = self.pg16.ap(ke, nkp)[0:64, :]
            ke_full = self.pg16.ap(ke, nkp)
            keb = self.pg16.b(ke, nkp)
            fw.op(fw.dve, lambda e: e.memset(ke_full[64:128, :], 0.0), (), keb)
            ve = self.pg16.alloc(nvp)
            ve3 = self.pg16.ap(ve, nvp)[:, 0:(NC + 2) * 65].rearrange("p (c e) -> p c e", e=65)
            veb = self.pg16.b(ve, nvp)
            kl, klb = self.u_loc(L, s, ("KA",))
            self.aload(ke_ap[:, 128:128 + NSEG], kl[kv * 64:(kv + 1) * 64, :], ke, [klb], keb)
            vl, vlb = self.v_loc3(L, s, kv)
            self.aload(ve3[:, 1:NC + 1, :], vl, ve, [vlb], veb)
            kc_ = self.pg16.alloc(2)
            vc_ = self.pg16.alloc(2)
            vg, vgb, _ = self.v_gat4(L, s, kv)
            kg, kgb = self.u_gat(L, s, ("KA",))
            for side in range(2):
                kcand = self.pg16.ap(kc_ + side)[0:64, 0:R * 128].rearrange("p (k n) -> p k n", k=R)
                cols = slice(NSEG - 128, NSEG) if side == 0 else slice(0, 128)
                self.aload(kcand, kg[:, kv * 64:(kv + 1) * 64, cols].rearrange("k d n -> d k n"), kc_ + side, [kgb], self.pg16.b(kc_ + side))
                vcand = self.pg16.ap(vc_ + side)[:, 0:R * 65].rearrange("p (k e) -> p k e", k=R)
                self.aload(vcand, vg[:, :, NC - 1 if side == 0 else 0, :], vc_ + side, [vgb], self.pg16.b(vc_ + side))
                kdst = ke_ap[:, 0:128] if side == 0 else ke_ap[:, (NC + 1) * 128:(NC + 2) * 128]
                vdst = ve3[:, 0, :] if side == 0 else ve3[:, NC + 1, :]
                for r in range(R):
                    wcol = hoff + side * R + r
                    if r == 0:
                        fw.op(fw.dve, lambda e: e.tensor_scalar(out=kdst, in0=kcand[:, r, :], scalar1=self.sm(wcol, 64), scalar2=None, op0=ALU.mult),
                              self.pg16.b(kc_ + side) + [self.const_b], keb)
                        fw.op(fw.dve, lambda e: e.tensor_scalar(out=vdst, in0=vcand[:, r, :], scalar1=self.sm(wcol), scalar2=None, op0=ALU.mult),
                              self.pg16.b(vc_ + side) + [self.const_b], veb)
                    else:
                        fw.op(fw.dve, lambda e: e.scalar_tensor_tensor(out=kdst, in0=kcand[:, r, :], scalar=self.sm(wcol, 64), in1=kdst, op0=ALU.mult, op1=ALU.add),
                              self.pg16.b(kc_ + side) + [self.const_b] + keb, keb)
                        fw.op(fw.dve, lambda e: e.scalar_tensor_tensor(out=vdst, in0=vcand[:, r, :], scalar=self.sm(wcol), in1=vdst, op0=ALU.mult, op1=ALU.add),
                              self.pg16.b(vc_ + side) + [self.const_b] + veb, veb)
            self.pg16.free(kc_, 2)
            self.pg16.free(vc_, 2)
            qn = self.load_qT(s, (kv * 4) * 64, 64, pad=True)
            for g in range(4):
                hq = kv * 4 + g
                q0, nq = qn
                if g + 1 < 4:
                    qn = self.load_qT(s, (hq + 1) * 64, 64, pad=True)
                q_ap = self.pg16.ap(q0, nq)
                qbufs = self.pg16.b(q0, nq)
                iters = []
                state = {}
                for qb in range(NSEG // T):
                    for e_ in range(6):
                        ext = 4 * qb + e_

                        def qk(sp_, qb=qb, ext=ext):
                            self.mm(self.psap(sp_), ke_full[:, ext * 128:(ext + 1) * 128], q_ap[:, qb * T:(qb + 1) * T], True, True, keb + qbufs, [self.ps.bufs[sp_]])

                        def post(sp_, qb=qb, ext=ext, e_=e_, hq=hq):
                            pt = self.exp_to_page(sp_, 0.125)
                            fw.op(fw.dve, lambda e: e.tensor_tensor(out=self.pg16.ap(pt), in0=self.pg16.ap(pt), in1=self.pg16.ap(mk + e_), op=ALU.mult),
                                  self.pg16.b(pt) + self.pg16.b(mk + e_), self.pg16.b(pt))
                            if e_ == 0:
                                state["acc"] = self.ps.alloc()
                            acc = state["acc"]
                            self.mm(self.psap(acc, 65), ve3[:, ext, :], self.pg16.ap(pt), e_ == 0, e_ == 5, veb + self.pg16.b(pt), [self.ps.bufs[acc]])
                            self.pg16.free(pt)
                            if e_ == 5:
                                self.epi65(acc, s, hq * 64, s * NSEG + qb * T, sink_col=SM_SINK + hq)
                        iters.append({"qk": qk, "post": post})
                self.flash_run(iters)
                self.pg16.free(q0, nq)
            self.pg16.free(ke, nkp)
            self.pg16.free(ve, nvp)
        self.pg16.free(mk, 6)
        nkc = R * NC
        scale = 96.0 ** -0.5
        for h in range(8):
            k0, nk = self.load_kT(L, s, ('KB', h), 96)
            v0, nv, v3 = self.load_v(L, s, 2 + h)
            q0, nq = self.load_qT(s, 512 + h * 96, 96)
            self.attn65(s, self.pg16.ap(k0, nk), self.pg16.b(k0, nk), 96, self.pg16.ap(q0, nq), self.pg16.b(q0, nq), v3, self.pg16.b(v0, nv), nkc, scale, 512 + h * 64)
            self.pg16.free(k0, nk)
            self.pg16.free(v0, nv)
            self.pg16.free(q0, nq)

    def attn_od(self, s):
        fw = self.fw
        cfg = self.cfg
        L = 1
        NC, NSEG, R = cfg.NC, cfg.NSEG, cfg.ranks[s]
        nkc = R * NC
        self.pg16.reset()
        self.pg16.lo, self.pg32.lo = NPG16 - 5, 0
        ones = self.cbf[:, CB_ONES:CB_ONES + 128]
        for hd in range(4):
            k0, nk = self.load_kT(L, s, ('KC', hd), 128)
            v0, nv, v3 = self.load_v(L, s, hd)
            q0, nq = self.load_qT(s, hd * 128, 128)
            k_ap, q_ap = self.pg16.ap(k0, nk), self.pg16.ap(q0, nq)
            kb, qb_, vb = self.pg16.b(k0, nk), self.pg16.b(q0, nq), self.pg16.b(v0, nv)
            iters = []
            state = {}
            ones_f = self.cf32[:, 128:256]
            for qb in range(NSEG // T):
                for kc in range(nkc):
                    def qk(sp2, qb=qb, kc=kc):
                        for c in range(2):
                            self.mm(self.psap(sp2[c]), k_ap[c * 64:(c + 1) * 64, kc * 128:(kc + 1) * 128], q_ap[c * 64:(c + 1) * 64, qb * T:(qb + 1) * T],
                                    True, True, kb + qb_, [self.ps.bufs[sp2[c]]])

                    def post(sp2, qb=qb, kc=kc, hd=hd):
                        pts = [self.exp_to_page(sp2[c], 0.125) for c in range(2)]
                        if kc == 0:
                            state["o", 0] = self.ps.alloc()
                            state["o", 1] = self.ps.alloc()
                            state["l", 0] = self.ps.alloc()
                            state["lacc"] = self.pg32.alloc()
                        for c in range(2):
                            ao = state["o", c]
                            self.mm(self.psap(ao), v3[:, kc // NC, kc % NC, :], self.pg16.ap(pts[c]), kc == 0, kc == nkc - 1, vb + self.pg16.b(pts[c]), [self.ps.bufs[ao]])
                        al = state["l", 0]
                        self.mm(self.psap(al), ones, self.pg16.ap(pts[0]), kc == 0, kc == nkc - 1, [self.const_b] + self.pg16.b(pts[0]), [self.ps.bufs[al]])
                        la = state["lacc"]
                        if kc == 0:
                            fw.op(fw.dve, lambda e: e.tensor_copy(out=self.pg32.ap(la), in_=self.pg16.ap(pts[1])), self.pg16.b(pts[1]), self.pg32.b(la))
                        else:
                            fw.op(fw.dve, lambda e: e.tensor_tensor(out=self.pg32.ap(la), in0=self.pg32.ap(la), in1=self.pg16.ap(pts[1]), op=ALU.add),
                                  self.pg16.b(pts[1]) + self.pg32.b(la), self.pg32.b(la))
                        for c in range(2):
                            self.pg16.free(pts[c])
                        if kc == nkc - 1:
                            l1 = self.ps.alloc()
                            self.mm(self.psap(l1), ones_f, self.pg32.ap(la), True, True, [self.const_b] + self.pg32.b(la), [self.ps.bufs[l1]])
                            self.pg32.free(la)
                            state["l", 1] = l1
                            self.epi_diff(state, s, hd, s * NSEG + qb * T)
                    iters.append({"qk": qk, "post": post})
            self.flash_run(iters, pair=True)
            self.pg16.free(k0, nk)
            self.pg16.free(v0, nv)
            self.pg16.free(q0, nq)
        for kv in range(2):
            nk = nkc * 128 // T
            k0 = self.pg16.alloc(nk)
            self.pg16.free(k0, nk)
            fw.op(fw.dve, lambda e: e.memset(self.pg16.ap(k0, nk)[64:128, :], 0.0), (), self.pg16.b(k0, nk))
            k0b, nkb = self.load_kT(L, s, ("KD",), 64, rows=(kv * 64, (kv + 1) * 64))
            assert k0b == k0 and nkb == nk
            v0, nv, v3 = self.load_v(L, s, 4 + kv)
            qn = self.load_qT(s, 512 + (kv * 4) * 64, 64, pad=True)
            for g in range(4):
                hq = kv * 4 + g
                q0, nq = qn
                if g + 1 < 4:
                    qn = self.load_qT(s, 512 + (hq + 1) * 64, 64, pad=True)
                self.attn65(s, self.pg16.ap(k0, nk), self.pg16.b(k0, nk), 128, self.pg16.ap(q0, nq), self.pg16.b(q0, nq), v3, self.pg16.b(v0, nv), nkc, 0.125, 512 + hq * 64)
                self.pg16.free(q0, nq)
            self.pg16.free(k0, nk)
            self.pg16.free(v0, nv)

    def run_debug(self, stage):
        cfg = self.cfg
        ybuf = Buf("y")
        for t in range(cfg.NTILE):
            self.load_x(t)
            if stage >= 1:
                self.ffn(t, 0, 0)
            if stage >= 2:
                self.inproj_ev(t)
                if (t + 1) % cfg.TPS == 0 and stage >= 3:
                    self.allgather(0, t // cfg.TPS)
        if stage >= 4:
            for s in range(2):
                self.attn_ev(s)
        if stage >= 5:
            for t in range(cfg.NTILE):
                self.outproj(t, "w_ev_out")
        for t in range(cfg.NTILE):
            self.store_y(t, ybuf)
        self.fw.finish([ybuf])

    def epi_diff(self, state, s, hd, col0):
        fw = self.fw
        on = []
        for c in range(2):
            ao, al = state["o", c], state["l", c]
            rl = self.pg32.alloc()
            fw.op(fw.act, lambda e: e.activation(out=self.pg32.ap(rl), in_=self.psap(al), func=AF.Ln), [self.ps.bufs[al]], self.pg32.b(rl))
            self.ps.free(al)
            fw.op(fw.act, lambda e: e.activation(out=self.pg32.ap(rl), in_=self.pg32.ap(rl), func=AF.Exp, scale=-1.0), self.pg32.b(rl), self.pg32.b(rl))
            o_ = self.pg32.alloc()
            fw.op(fw.dve, lambda e: e.tensor_tensor(out=self.pg32.ap(o_), in0=self.psap(ao), in1=self.pg32.ap(rl), op=ALU.mult),
                  [self.ps.bufs[ao]] + self.pg32.b(rl), self.pg32.b(o_))
            self.ps.free(ao)
            self.pg32.free(rl)
            on.append(o_)
        d_ap = self.pg32.ap(on[0])
        fw.op(fw.dve, lambda e: e.scalar_tensor_tensor(out=d_ap, in0=self.pg32.ap(on[1]), scalar=self.neglam, in1=d_ap, op0=ALU.mult, op1=ALU.add),
              self.pg32.b(on[0]) + self.pg32.b(on[1]) + [self.const_b], self.pg32.b(on[0]))
        self.pg32.free(on[1])

        def tail():
            r = self.rstd_of([(d_ap, self.pg32.b(on[0]))], 1.0 / 128, self.cbf[:, CB_ONES:CB_ONES + 128], 128)
            o = self.pg16.alloc()
            fw.op(fw.dve, lambda e: e.scalar_tensor_tensor(out=self.pg16.ap(o), in0=d_ap, scalar=self.sm(SM_GCO), in1=self.pg32.ap(r), op0=ALU.mult, op1=ALU.mult),
                  self.pg32.b(on[0]) + self.pg32.b(r) + [self.const_b], self.pg16.b(o))
            self.pg32.free(r)
            self.pg32.free(on[0])
            self.store(self.pg16.sems[o], self.o_rows(hd * 128, 128, col0, T), self.pg16.ap(o), self.pg16.b(o), self.Os_b)
            self.pg16.free(o)
        self.deferred.append([self.defer_lag, tail])

    def allgather(self, L, s):
        cfg = self.cfg
        R = cfg.ranks[s]
        groups = [[g * R + i for i in range(R)] for g in range(8 // R)]
        for j in range(self.nchunk[L]):
            src, dst = self.loc[L, s, j], self.gat[L, s, j]
            self.fw.async1(self.fw.pool, lambda e: e.collective_compute("AllGather", ALU.bypass, replica_groups=groups, ins=[src], outs=[dst]),
                           self.s_cc[L, s], reads=[self.loc_b[L, s, j]], writes=[self.gat_b[L, s, j]])

    def run(self):
        import os
        stage = int(os.environ.get("KSTAGE", "99"))
        cfg = self.cfg
        if stage == -2:
            self.fw.dma(self.fw.sp, self.cf32[:, :], self.cf32_d, self.s_const2, writes=[self.const_b])
            return self.run_debug(0)
        self.load_consts()
        if stage == -1:
            return self.fw.finish([])
        if stage < 99:
            return self.run_debug(stage)
        self.pg16.lo, self.pg32.lo = 8, 2
        self.load_x(0)
        for t in range(cfg.NTILE):
            self.ffn(t, 0, 0, mid_hook=(lambda t=t: self.load_x(t + 1)) if t + 1 < cfg.NTILE else None)
            self.inproj_ev(t)
            if (t + 1) % cfg.TPS == 0:
                self.allgather(0, t // cfg.TPS)
        for s in range(2):
            self.attn_ev(s)
        self.pg16.lo, self.pg32.lo = 8, 2
        for t in range(cfg.NTILE):
            self.outproj(t, "w_ev_out")
            self.ffn(t, 1, 2)
            self.ffn(t, 2, 3)
            self.inproj_od(t)
            if (t + 1) % cfg.TPS == 0:
                self.allgather(1, t // cfg.TPS)
        for s in range(2):
            self.attn_od(s)
        ybuf = Buf("y")
        self.pg16.lo, self.pg32.lo = 8, 2
        for t in range(cfg.NTILE):
            self.outproj(t, "w_od_out")
            self.ffn(t, 3, 5)
            self.store_y(t, ybuf)
        self.fw.finish([ybuf])


_PROG_CACHE = {}


def build_program(nseg):
    cfg = Cfg(nseg)
    nc0 = bass.Bass("TRN2", target_bir_lowering=False)
    with contextlib.ExitStack() as es0:
        k0 = Kern(nc0, es0, cfg)
        k0.fw.dry = True
        k0.run()
        plan = list(k0.wplan)
    nc = bass.Bass("TRN2", target_bir_lowering=False)
    with contextlib.ExitStack() as es:
        k = Kern(nc, es, cfg, dry_plan=plan)
        k.run()
        assert k.wi == len(plan)
        print('instructions:', k.fw.ninst, 'sems:', k.fw.nsem)
    return nc, cfg


def _f32(a):
    return np.ascontiguousarray(np.asarray(a, dtype=np.float32))


def make_in_maps(cfg, inp):
    NSEG = cfg.NSEG
    g = {k: np.asarray(v, dtype=np.float32) for k, v in inp.items()}
    shared = {}
    w_in = np.zeros((4, NJ, 128, 2048), np.float32)
    w_out = np.zeros((4, 8, 128, 2816), np.float32)
    for f, (nm, l) in enumerate((("ffn1", 0), ("ffn2", 0), ("ffn1", 1), ("ffn2", 1))):
        Wi = g[nm + "_w_in"][l]
        Wo = g[nm + "_w_out"][l]
        for j in range(NJ):
            cols = np.concatenate([np.arange(j * 128, (j + 1) * 128), DFF + np.arange(j * 128, (j + 1) * 128)])
            w_in[f, j] = _kmajor(Wi[:, cols])
        for m in range(8):
            w_out[f, m] = _kmajor(Wo[:, m * 128:(m + 1) * 128])
    shared["w_ffn_in"] = w_in
    shared["w_ffn_out"] = w_out
    Wev = g["ev_w_in"][0]
    shared["w_ev_in_a"] = np.stack([_kmajor(Wev[:, i * 256:(i + 1) * 256]) for i in range(5)])
    shared["w_ev_in_b"] = _kmajor(Wev[:, 1280:1568])[None]
    shared["w_uq"] = _kmajor(g["b_w_uq"][0])[None]
    Wkv = g["b_w_ukv"][0].reshape(2, 128, 8, 128)
    wkp = np.zeros((128, 2, 8, 96), np.float32)
    wkp[:, :, :, 0:64] = Wkv[:, :, :, 0:64].transpose(1, 0, 2, 3)
    wvp = np.ascontiguousarray(Wkv[:, :, :, 64:128].transpose(1, 0, 2, 3)).reshape(128, 1024)
    shared["w_ukv"] = np.concatenate([wkp.reshape(128, 1536), wvp], axis=1)[None]
    shared["w_ev_out"] = np.stack([_kmajor(g["ev_w_out"][0][:, i * 256:(i + 1) * 256]) for i in range(4)])
    Wod = g["od_w_in"][0]
    shared["w_od_in"] = np.stack([_kmajor(Wod[:, i * 256:(i + 1) * 256]) for i in range(9)])
    shared["w_od_out"] = np.stack([_kmajor(g["od_w_out"][0][:, i * 256:(i + 1) * 256]) for i in range(4)])
    shared["cbf"] = _const_bf()
    cf = np.zeros((128, 256), np.float32)
    cf[:, 0:128] = np.eye(128, dtype=np.float32)
    cf[:, 128:256] = 1.0
    shared["cf32"] = cf
    import ml_dtypes
    shared["masks"] = _masks().astype(ml_dtypes.bfloat16)
    small = np.zeros((128, NSM), np.float32)
    for i, v in enumerate((g["ffn1_norm"][0], g["ev_norm"][0], g["ffn2_norm"][0], g["ffn1_norm"][1], g["od_norm"][0], g["ffn2_norm"][1])):
        small[:, SM_GD + i * 8: SM_GD + (i + 1) * 8] = v.reshape(8, 128).T
    for i, nm in enumerate(("a_q_norm", "a_k_norm", "c_q_norm", "c_k_norm", "d_q_norm", "d_k_norm")):
        small[:, SM_GH + i] = _tile2(g[nm][0])
    small[0:96, SM_GB + 0] = g["b_q_norm"][0]
    small[0:96, SM_GB + 1] = g["b_k_norm"][0]
    small[:, SM_GC:SM_GC + 4] = g["b_cq_norm"][0].reshape(4, 128).T
    small[:, SM_GC + 4:SM_GC + 6] = g["b_ckv_norm"][0].reshape(2, 128).T
    small[:, SM_GCO] = g["c_out_norm"][0]
    small[:, SM_SINK:SM_SINK + 8] = g["a_sink"][0][None, :]
    small[:, SM_LAM:SM_LAM + 256] = g["c_lambda"][0].reshape(1, 256)
    small[:, SM_EPS] = EPS
    maps = []
    for c in range(8):
        m = dict(shared)
        qi, hi = c % 4, c % 2
        xin = np.concatenate([g["x_prompt"][c // 4, qi * NSEG:(qi + 1) * NSEG], g["x_sample"][c // 2, hi * NSEG:(hi + 1) * NSEG]], axis=0)
        m["xin"] = _f32(xin)
        pos = np.concatenate([qi * NSEG + np.arange(NSEG), hi * NSEG + np.arange(NSEG)])
        m["rope"] = _rope_tables(pos)
        sm = small.copy()
        for r in range(4):
            sm[:, SM_HALO + r] = 1.0 if r == qi - 1 else 0.0
            sm[:, SM_HALO + 4 + r] = 1.0 if r == qi + 1 else 0.0
        for r in range(2):
            sm[:, SM_HALO + 8 + r] = 1.0 if r == hi - 1 else 0.0
            sm[:, SM_HALO + 10 + r] = 1.0 if r == hi + 1 else 0.0
        m["small"] = sm
        import os
        if os.environ.get("KTINYW") == "1":
            for k in list(m):
                if k.startswith("w_"):
                    a = m[k]
                    m[k] = a.reshape((-1,) + a.shape[-2:])[0:1].reshape((1,) * (a.ndim - 2) + a.shape[-2:])
        maps.append({k: (v if k == "masks" else _f32(v)) for k, v in m.items()})
    return maps


def run_kernel(inp, nseg):
    if nseg not in _PROG_CACHE:
        _PROG_CACHE[nseg] = build_program(nseg)
    nc, cfg = _PROG_CACHE[nseg]
    maps = make_in_maps(cfg, inp)
    res = run_bass_kernel_spmd(nc, maps, core_ids=list(range(8)))
    NSEG = cfg.NSEG
    yp = np.zeros((2, 4 * NSEG, D_MODEL), np.float32)
    ys = np.zeros((4, 2 * NSEG, D_MODEL), np.float32)
    for c in range(8):
        y = np.asarray(res.results[c]["y"], dtype=np.float32)
        yp[c // 4, (c % 4) * NSEG:(c % 4 + 1) * NSEG] = y[0:NSEG]
        ys[c // 2, (c % 2) * NSEG:(c % 2 + 1) * NSEG] = y[NSEG:2 * NSEG]
    return yp, ys


def kernel(**inputs):
    return run_kernel(inputs, 2048)
```
